# Optimizing a Trainium2 kernel written in Bass

```python
import jax, jax.numpy as jnp
from jax import lax
import numpy as np

D_MODEL = 1024
BATCH = 2
SEQ = 16384
DEPTH = 2

GRID_W = 64
CTX_LEN = 256
HEAD_DIM = 64
ATTN_Q_HEADS = 8
ATTN_KV_HEADS = 2
GQA_GROUP = ATTN_Q_HEADS // ATTN_KV_HEADS
WINDOW = 128
BLOCK = 128
ATTN_DIM = ATTN_Q_HEADS * HEAD_DIM
KV_DIM = ATTN_KV_HEADS * HEAD_DIM
GM_GROUPS = 8
GM_DIM = GM_GROUPS * HEAD_DIM
CHUNK = 128
IN_DIM_EVEN = ATTN_DIM + 2 * KV_DIM + 2 * GM_DIM
MIX_DIM_EVEN = ATTN_DIM + GM_DIM
SPLITS_EVEN = [ATTN_DIM, ATTN_DIM + KV_DIM, ATTN_DIM + 2 * KV_DIM, ATTN_DIM + 2 * KV_DIM + GM_DIM]
CONV_DIM = D_MODEL
CONV_WIDTH = 3
D_FF = -(-8 * D_MODEL // (3 * 256)) * 256
ROPE_THETA = 10000.0
RMS_EPS = 1e-6
LN_EPS = 1e-5
NEG_INF = -1e30
N_EVEN = (DEPTH + 1) // 2
N_ODD = DEPTH // 2

kernel_name = "hybrid_swa_gmlp_shortconv_dit"


def rms_norm(x, g):
    xf = x.astype(jnp.float32)
    y = xf * lax.rsqrt(jnp.mean(xf * xf, axis=-1, keepdims=True) + RMS_EPS)
    return (y * g.astype(jnp.float32)).astype(x.dtype)


def modulate(x, g, shift, scale):
    return rms_norm(x, g) * (1.0 + scale) + shift


def axial_rope_tables(rows):
    q = HEAD_DIM // 4
    inv = ROPE_THETA ** (-jnp.arange(q, dtype=jnp.float32) / q)
    row = jnp.repeat(jnp.arange(rows, dtype=jnp.float32), GRID_W)
    col = jnp.tile(jnp.arange(GRID_W, dtype=jnp.float32), rows)
    ang = jnp.stack([row[:, None] * inv, col[:, None] * inv], axis=1)
    return jnp.cos(ang), jnp.sin(ang)


def apply_axial_rope(x, cos, sin):
    q = HEAD_DIM // 4
    xf = x.astype(jnp.float32).reshape(*x.shape[:-1], 2, 2, q)
    a, b = xf[..., 0, :], xf[..., 1, :]
    cc, ss = cos[None, :, None], sin[None, :, None]
    out = jnp.stack([a * cc - b * ss, a * ss + b * cc], axis=-2)
    return out.reshape(x.shape).astype(x.dtype)


def window_attention(q, k, v, kc, vc, sink):
    B, S = q.shape[0], q.shape[1]
    nb = S // BLOCK
    scale = HEAD_DIM ** -0.5
    qb = q.reshape(B, nb, BLOCK, ATTN_KV_HEADS, GQA_GROUP, HEAD_DIM)
    pad = ((0, 0), (BLOCK, BLOCK), (0, 0), (0, 0))
    kp = jnp.pad(k, pad).reshape(B, nb + 2, BLOCK, ATTN_KV_HEADS, HEAD_DIM)
    vp = jnp.pad(v, pad).reshape(B, nb + 2, BLOCK, ATTN_KV_HEADS, HEAD_DIM)
    kw = jnp.concatenate([kp[:, :-2], kp[:, 1:-1], kp[:, 2:]], axis=2)
    vw = jnp.concatenate([vp[:, :-2], vp[:, 1:-1], vp[:, 2:]], axis=2)
    s_win = jnp.einsum('bnqkgd,bnrkd->bnkgqr', qb, kw, preferred_element_type=jnp.float32) * scale
    qpos = jnp.arange(nb)[:, None] * BLOCK + jnp.arange(BLOCK)[None, :]
    kpos = jnp.arange(nb)[:, None] * BLOCK + jnp.arange(3 * BLOCK)[None, :] - BLOCK
    valid = (jnp.abs(qpos[:, :, None] - kpos[:, None, :]) <= WINDOW) \
        & (kpos >= 0)[:, None, :] & (kpos < S)[:, None, :]
    s_win = jnp.where(valid[None, :, None, None], s_win, NEG_INF)
    s_ctx = jnp.einsum('bnqkgd,blkd->bnkgql', qb, kc, preferred_element_type=jnp.float32) * scale
    s_sink = jnp.broadcast_to(
        sink.astype(jnp.float32).reshape(ATTN_KV_HEADS, GQA_GROUP)[None, None, :, :, None, None],
        s_win.shape[:-1] + (1,))
    p = jax.nn.softmax(jnp.concatenate([s_win, s_ctx, s_sink], axis=-1), axis=-1)
    nw = 3 * BLOCK
    L = kc.shape[1]
    p_win = p[..., :nw].astype(v.dtype)
    p_ctx = p[..., nw:nw + L].astype(v.dtype)
    o = jnp.einsum('bnkgqr,bnrkd->bnqkgd', p_win, vw) + jnp.einsum('bnkgql,blkd->bnqkgd', p_ctx, vc)
    return o.reshape(B, S, ATTN_DIM)


def context_attention(qc, kc, vc, sink):
    B, L = qc.shape[0], qc.shape[1]
    qg = qc.reshape(B, L, ATTN_KV_HEADS, GQA_GROUP, HEAD_DIM)
    s = jnp.einsum('blkgd,bmkd->bkglm', qg, kc, preferred_element_type=jnp.float32) * HEAD_DIM ** -0.5
    s_sink = jnp.broadcast_to(
        sink.astype(jnp.float32).reshape(ATTN_KV_HEADS, GQA_GROUP)[None, :, :, None, None],
        s.shape[:-1] + (1,))
    p = jax.nn.softmax(jnp.concatenate([s, s_sink], axis=-1), axis=-1)[..., :L].astype(vc.dtype)
    o = jnp.einsum('bkglm,bmkd->blkgd', p, vc)
    return o.reshape(B, L, ATTN_DIM)


def chunk_gating(u, v, v_norm, ws, bs):
    B, N, _ = v.shape
    nc = N // CHUNK
    vf = v.astype(jnp.float32).reshape(B, N, GM_GROUPS, HEAD_DIM)
    mu = jnp.mean(vf, axis=-1, keepdims=True)
    var = jnp.mean(jnp.square(vf - mu), axis=-1, keepdims=True)
    vn = ((vf - mu) * lax.rsqrt(var + LN_EPS) * v_norm.astype(jnp.float32).reshape(GM_GROUPS, HEAD_DIM)).astype(v.dtype)
    vn = vn.reshape(B, nc, CHUNK, GM_GROUPS, HEAD_DIM)
    s = jnp.einsum('gij,bnjgd->bnigd', ws, vn) + bs.T[None, None, :, :, None]
    return u * s.reshape(B, N, GM_DIM)


def even_mixer(h, hc, w_in, sink, v_norm, ws, bs, w_out, cos, sin, ctx_out):
    B, S, _ = h.shape
    L = hc.shape[1]
    q, k, v, u, gv = jnp.split(h @ w_in, SPLITS_EVEN, axis=-1)
    q = apply_axial_rope(q.reshape(B, S, ATTN_Q_HEADS, HEAD_DIM), cos, sin)
    k = apply_axial_rope(k.reshape(B, S, ATTN_KV_HEADS, HEAD_DIM), cos, sin)
    v = v.reshape(B, S, ATTN_KV_HEADS, HEAD_DIM)
    if ctx_out:
        qc, kc, vc, uc, gvc = jnp.split(hc @ w_in, SPLITS_EVEN, axis=-1)
    else:
        kc, vc = jnp.split(hc @ w_in[:, ATTN_DIM:ATTN_DIM + 2 * KV_DIM], 2, axis=-1)
    kc = kc.reshape(B, L, ATTN_KV_HEADS, HEAD_DIM)
    vc = vc.reshape(B, L, ATTN_KV_HEADS, HEAD_DIM)
    o_attn = window_attention(q, k, v, kc, vc, sink)
    o_gm = chunk_gating(u, gv, v_norm, ws, bs)
    y = jnp.concatenate([o_attn, o_gm], axis=-1) @ w_out
    yc = None
    if ctx_out:
        oc_attn = context_attention(qc.reshape(B, L, ATTN_Q_HEADS, HEAD_DIM), kc, vc, sink)
        oc_gm = chunk_gating(uc, gvc, v_norm, ws, bs)
        yc = jnp.concatenate([oc_attn, oc_gm], axis=-1) @ w_out
    return y, yc


def short_conv_mixer(h, w_in, conv_w, w_out):
    bg, cg, hx = jnp.split(h @ w_in, 3, axis=-1)
    y = cg * hx
    yconv = lax.conv_general_dilated(
        y, conv_w[:, None, :].astype(y.dtype), window_strides=(1,),
        padding=[(CONV_WIDTH // 2, CONV_WIDTH // 2)],
        dimension_numbers=('NWC', 'WIO', 'NWC'), feature_group_count=CONV_DIM)
    return (bg * yconv) @ w_out


def swiglu(h, w1, w3, w2):
    return (jax.nn.silu(h @ w1) * (h @ w3)) @ w2


def setup_inputs(seed: int = 0) -> dict:
    key = jax.random.key(seed)
    ks = jax.random.split(key, 24)
    f32 = jnp.float32
    D = D_MODEL
    nrm = lambda k, shape, s: jax.random.normal(k, shape, f32) * s
    return {
        "x": nrm(ks[0], (BATCH, SEQ, D), 1.0),
        "c": nrm(ks[1], (BATCH, D), 1.0),
        "ctx": nrm(ks[2], (BATCH, CTX_LEN, D), 1.0),
        "c_ctx": nrm(ks[3], (D,), 1.0),
        "w_mod": nrm(ks[4], (DEPTH, D, 6 * D), D ** -0.5),
        "b_mod": nrm(ks[5], (DEPTH, 6 * D), 0.02),
        "g_mix_pre": 1.0 + nrm(ks[6], (DEPTH, D), 0.05),
        "g_mix_post": 1.0 + nrm(ks[7], (DEPTH, D), 0.05),
        "g_ffn_pre": 1.0 + nrm(ks[8], (DEPTH, D), 0.05),
        "g_ffn_post": 1.0 + nrm(ks[9], (DEPTH, D), 0.05),
        "ffn_w1": nrm(ks[10], (DEPTH, D, D_FF), D ** -0.5),
        "ffn_w3": nrm(ks[11], (DEPTH, D, D_FF), D ** -0.5),
        "ffn_w2": nrm(ks[12], (DEPTH, D_FF, D), D_FF ** -0.5),
        "a_w_in": nrm(ks[13], (N_EVEN, D, IN_DIM_EVEN), D ** -0.5),
        "a_sink": nrm(ks[14], (N_EVEN, ATTN_Q_HEADS), 0.5),
        "gm_v_norm": 1.0 + nrm(ks[15], (N_EVEN, GM_DIM), 0.05),
        "gm_ws": nrm(ks[16], (N_EVEN, GM_GROUPS, CHUNK, CHUNK), CHUNK ** -0.5),
        "gm_bs": 1.0 + nrm(ks[17], (N_EVEN, GM_GROUPS, CHUNK), 0.1),
        "a_w_out": nrm(ks[18], (N_EVEN, MIX_DIM_EVEN, D), MIX_DIM_EVEN ** -0.5),
        "sc_w_in": nrm(ks[19], (N_ODD, D, 3 * CONV_DIM), D ** -0.5),
        "sc_conv": nrm(ks[20], (N_ODD, CONV_WIDTH, CONV_DIM), CONV_WIDTH ** -0.5),
        "sc_w_out": nrm(ks[21], (N_ODD, CONV_DIM, D), CONV_DIM ** -0.5),
    }


def reference(x, c, ctx, c_ctx, w_mod, b_mod, g_mix_pre, g_mix_post, g_ffn_pre, g_ffn_post,
              ffn_w1, ffn_w3, ffn_w2, a_w_in, a_sink, gm_v_norm, gm_ws, gm_bs, a_w_out,
              sc_w_in, sc_conv, sc_w_out):
    n_tok = x.shape[1]
    rows = n_tok // GRID_W
    cos, sin = axial_rope_tables(rows)
    silu_c = jax.nn.silu(c)
    silu_cc = jax.nn.silu(c_ctx)
    xc = ctx
    for i in range(DEPTH):
        ctx_out = any(j % 2 == 0 for j in range(i + 1, DEPTH))
        ctx_in = (i % 2 == 0) or ctx_out
        mod = (silu_c @ w_mod[i] + b_mod[i])[:, None, :]
        sh_m, sc_m, gt_m, sh_f, sc_f, gt_f = jnp.split(mod, 6, axis=-1)
        h = modulate(x, g_mix_pre[i], sh_m, sc_m)
        hc = None
        if ctx_in:
            mod_c = silu_cc @ w_mod[i] + b_mod[i]
            csh_m, csc_m, cgt_m, csh_f, csc_f, cgt_f = jnp.split(mod_c, 6, axis=-1)
            hc = modulate(xc, g_mix_pre[i], csh_m, csc_m)
        if i % 2 == 0:
            e = i // 2
            y, yc = even_mixer(h, hc, a_w_in[e], a_sink[e], gm_v_norm[e], gm_ws[e], gm_bs[e],
                               a_w_out[e], cos, sin, ctx_out)
        else:
            o = i // 2
            y = short_conv_mixer(h, sc_w_in[o], sc_conv[o], sc_w_out[o])
            yc = short_conv_mixer(hc, sc_w_in[o], sc_conv[o], sc_w_out[o]) if ctx_out else None
        x = x + gt_m * rms_norm(y, g_mix_post[i])
        hf = modulate(x, g_ffn_pre[i], sh_f, sc_f)
        x = x + gt_f * rms_norm(swiglu(hf, ffn_w1[i], ffn_w3[i], ffn_w2[i]), g_ffn_post[i])
        if ctx_out:
            xc = xc + cgt_m * rms_norm(yc, g_mix_post[i])
            hcf = modulate(xc, g_ffn_pre[i], csh_f, csc_f)
            xc = xc + cgt_f * rms_norm(swiglu(hcf, ffn_w1[i], ffn_w3[i], ffn_w2[i]), g_ffn_post[i])
    return x
```

```python
import numpy as np
from contextlib import ExitStack
import concourse.bass as bass
import concourse.mybir as mybir
from concourse.bass_utils import run_bass_kernel_spmd

F32 = mybir.dt.float32
BF16 = mybir.dt.bfloat16
AF = mybir.ActivationFunctionType
ALU = mybir.AluOpType
AX = mybir.AxisListType

NCORES = 8
D = 1024
SEQ = 16384
NB = 32
NE = NB + 4
DFF = 2816
NJ = DFF // 128
DEBUG = False
PHASES = "MABCD"
A_LIMIT = None
A_STEP = 99
DBG_E = 6


import re
_PSUM_RE = re.compile(r"^(colps|rowps\d|tpq|tm|bk\d|tp|tpC|gps\d|ups\d|yps\d|ypsC\d|cgp\d|hxp\d|bgp)$")


class _Op:
    __slots__ = ("eng", "fn", "deps", "token", "is_dma", "signal", "seq", "same_sync")


class Sched:
    ENGS = ("pe", "act", "dve", "pool", "sp")

    def __init__(self, nc, es):
        self.nc = nc
        self.es = es
        self.ops = {e: [] for e in self.ENGS}
        self.esem = {e: es.enter_context(nc.semaphore("s_" + e)) for e in ("pe", "act", "dve", "pool")}
        self.csem = {e: es.enter_context(nc.semaphore("c_" + e)) for e in ("act", "dve", "pool")}
        self.ccnt = {e: 0 for e in ("act", "dve", "pool")}
        self.buf = {}
        self.dsem = {}
        self.pending = {e: [] for e in self.ENGS}
        self.dma_since_barrier = []
        self.nops = 0

    def op(self, eng, fn, reads=(), writes=(), dma_key=None, same_sync=False):
        o = _Op()
        o.same_sync = same_sync
        o.eng = eng
        o.fn = fn
        o.is_dma = dma_key is not None
        o.signal = False
        o.token = None
        o.seq = self.nops
        self.nops += 1
        deps = []
        reads = list(reads)
        writes = list(writes)
        for k in list(reads):
            if isinstance(k, str) and _PSUM_RE.match(k):
                reads.remove(k)
                if k not in writes:
                    writes.append(k)
        for k in reads:
            b = self.buf.setdefault(k, [None, []])
            if b[0] is not None:
                deps.append(b[0])
        for k in writes:
            b = self.buf.setdefault(k, [None, []])
            if b[0] is not None:
                deps.append(b[0])
            deps.extend(b[1])
        for k in reads:
            self.buf[k][1].append(o)
        for k in writes:
            b = self.buf[k]
            b[0] = o
            b[1] = []
        deps.extend(self.pending[eng])
        self.pending[eng] = []
        o.deps = [d for d in deps if d is not o]
        self.ops[eng].append(o)
        if o.is_dma:
            if dma_key not in self.dsem:
                self.dsem[dma_key] = [self.es.enter_context(self.nc.semaphore("d_" + str(len(self.dsem)))), 0]
            s = self.dsem[dma_key]
            s[1] += 16
            o.token = (s[0], s[1])
            self.dma_since_barrier.append(o)
        return o

    def barrier(self):
        toks = []
        for e in self.ENGS:
            if self.ops[e]:
                toks.append(self.ops[e][-1])
        toks.extend(self.dma_since_barrier)
        self.dma_since_barrier = []
        for e in self.ENGS:
            self.pending[e].extend(toks)
        self.buf = {}

    @staticmethod
    def _needs_sync(d, o):
        return d.eng != o.eng or d.is_dma or o.is_dma or o.same_sync or o.eng != "pe"

    def finalize(self):
        for e in self.ENGS:
            for o in self.ops[e]:
                for d in o.deps:
                    if self._needs_sync(d, o) and not d.is_dma:
                        d.signal = True
        for e in ("pe", "act", "dve", "pool", "sp"):
            c = 0
            for o in self.ops[e]:
                if not o.is_dma and o.signal:
                    assert e != "sp"
                    c += 1
                    o.token = (self.esem[e], c)

    def emit(self, eng, engine):
        waited = {}
        for o in self.ops[eng]:
            need = {}
            for d in o.deps:
                if self._needs_sync(d, o):
                    sem, v = d.token
                    k = id(sem)
                    if v > need.get(k, (None, 0))[1]:
                        need[k] = (sem, v)
            for k, (sem, v) in need.items():
                if waited.get(k, 0) < v:
                    engine.wait_ge(sem, v)
                    waited[k] = v
            if isinstance(o.fn, (list, tuple)):
                ins = None
                for i, f in enumerate(o.fn):
                    ins = f(engine)
                    if i < len(o.fn) - 1:
                        self.ccnt[eng] += 1
                        ins.then_inc(self.csem[eng], 1)
                        engine.wait_ge(self.csem[eng], self.ccnt[eng])
            else:
                ins = o.fn(engine)
            if o.is_dma:
                ins.then_inc(o.token[0], 16)
            elif o.signal:
                ins.then_inc(o.token[0], 1)

    def final_wait(self, eng, engine, ops):
        for o in ops:
            engine.wait_ge(o.token[0], o.token[1])


def build_nc():
    nc = bass.Bass("TRN2", target_bir_lowering=False)
    dk = "ExternalOutput" if DEBUG else "Internal"

    def din(name, shape):
        return nc.dram_tensor(name, list(shape), F32, kind="ExternalInput").ap()

    xin = din("xin", [NE * 128, D])
    ctxin = din("ctxin", [256, D])
    if "M" in PHASES:
        cT = din("cT", [128, 16])
        w_mod = din("w_mod", [2, D, 6 * D])
        b_modT = din("b_modT", [128, 96])
        b_mod = din("b_mod", [2, 6 * D])
        gT = din("gT", [128, 32])
        g_post = din("g_post", [4, D])
    w_in = din("w_in", [D, 1792])
    w_out = din("w_out", [D, D])
    wsT = din("wsT", [128, 8 * 128])
    bsT = din("bsT", [128, 8])
    vnorm = din("vnorm", [1, 512])
    sink = din("sink", [1, 8])
    ropeC = din("ropeC", [128, NE * 64])
    ropeS = din("ropeS", [128, NE * 64])
    kbias_in = din("kbias", [128, NE + 2])
    trimask_in = din("trimask", [128, 1024])
    ident_in = din("ident", [128, 128])
    if "C" in PHASES:
        cvalid_in = din("cvalid", [128, 2])
    if "B" in PHASES or "D" in PHASES:
        ffn_w1 = din("ffn_w1", [2, D, DFF])
        ffn_w3 = din("ffn_w3", [2, D, DFF])
        ffn_w2 = din("ffn_w2", [2, DFF, D])
    if "C" in PHASES:
        sc_w_in = din("sc_w_in", [D, 3 * D])
        sc_w_out = din("sc_w_out", [D, D])
        cwT = din("cwT", [128, 24])
    out = nc.dram_tensor("out", [NB * 128, D], F32, kind="ExternalOutput").ap()
    x1 = nc.dram_tensor("x1", [(NB + 2) * 128, D], F32, kind=dk).ap()
    x2 = nc.dram_tensor("x2", [(NB + 2) * 128, D], F32, kind=dk).ap()
    x3 = nc.dram_tensor("x3", [NB * 128, D], F32, kind=dk).ap()
    gtgscr = nc.dram_tensor("gtgscr", [4 * 128, D], F32, kind=dk).ap()
    dbg_modc = nc.dram_tensor("dbg_modc", [128, 128], F32, kind=dk).ap()
    dbgt = {}
    if DEBUG:
        for nm, shp, dt_ in [("d_hT", [128, 1024], BF16), ("d_qks", [128, 640], F32), ("d_qr", [128, 640], BF16),
                             ("d_u", [128, 512], F32), ("d_vn", [128, 512], BF16), ("d_mix", [128, 1024], BF16),
                             ("d_qT", [128, 512], BF16), ("d_pT", [128, 512], BF16), ]:
            dbgt[nm] = nc.dram_tensor(nm, shp, dt_, kind="ExternalOutput").ap()

    with ExitStack() as es:
        S = Sched(nc, es)

        def sb(stack, name, shape, dt):
            return stack.enter_context(nc.sbuf_tensor("sb_" + name, list(shape), dt))

        def ps(stack, name, shape, dt=F32):
            return stack.enter_context(nc.psum_tensor("ps_" + name, list(shape), dt))

        def dma(eng, out_ap, in_ap, key, reads=(), writes=()):
            return S.op(eng, lambda e: e.dma_start(out=out_ap, in_=in_ap), reads=reads, writes=writes, dma_key=key)

        def dbgdump(name, ap2d, key):
            if DEBUG:
                dma("sp", dbgt[name], ap2d, "dbg_" + name, reads=[key])

        ident = sb(es, "ident", [128, 128], BF16)
        modc = sb(es, "modc", [128, 2, 2, 2, 8, 2], F32)
        consts = sb(es, "consts", [128, 8], F32)
        mh8 = sb(es, "mh8", [128, 8], F32)
        epsl8 = sb(es, "epsl8", [128, 8], F32)

        dma("pool", ident[:], ident_in, "ident", writes=["ident"])
        S.op("pool", lambda e: e.memset(consts[:, 0:1], 1e-6), writes=["consts"])
        S.op("pool", lambda e: e.memset(consts[:, 1:2], -0.5), writes=["consts"])
        S.op("pool", lambda e: e.memset(mh8[:], -0.5), writes=["mh8"])
        S.op("pool", lambda e: e.memset(epsl8[:], 1e-5), writes=["epsl8"])

        def rstd_ops(ss, ncols, rs, key_ss, key_rs):
            if ncols == 2:
                f = [lambda e: e.tensor_tensor(out=rs, in0=ss[:, 0:1], in1=ss[:, 1:2], op=ALU.add),
                     lambda e: e.tensor_tensor(out=rs, in0=rs, in1=consts[:, 0:1], op=ALU.add),
                     lambda e: e.tensor_tensor(out=rs, in0=rs, in1=consts[:, 1:2], op=ALU.pow)]
            else:
                f = [lambda e: e.tensor_tensor(out=rs, in0=ss[:, 0:1], in1=consts[:, 0:1], op=ALU.add),
                     lambda e: e.tensor_tensor(out=rs, in0=rs, in1=consts[:, 1:2], op=ALU.pow)]
            S.op("pool", f, reads=[key_ss, "consts"], writes=[key_rs])

        with ExitStack() as pm:
          if "M" in PHASES:
            cTs = sb(pm, "cTs", [128, 16], F32)
            sil = sb(pm, "sil", [128, 16], F32)
            srep = sb(pm, "srep", [128, 8, 128], F32)
            svec = sb(pm, "svec", [128, 8, 2], F32)
            bmT = sb(pm, "bmT", [128, 2, 48], F32)
            gTs = sb(pm, "gTs", [128, 2, 2, 8], F32)
            modT = sb(pm, "modT", [128, 2, 4, 8, 2], F32)
            wm = [sb(pm, "wm%d" % i, [128, 8, 1024], F32) for i in range(3)]
            brow = [sb(pm, "brow%d" % i, [128, 1024], F32) for i in range(2)]
            grow = [sb(pm, "grow%d" % i, [128, 1024], F32) for i in range(2)]
            gtg = [sb(pm, "gtgm%d" % i, [128, 1024], F32) for i in range(2)]
            colps = ps(pm, "colps", [128, 8, 2])
            rowps = [ps(pm, "rowps%d" % i, [128, 512]) for i in range(2)]

            dma("sp", cTs[:], cT, "cTs", writes=["cTs"])
            dma("sp", bmT[:].rearrange("p a b -> p (a b)"), b_modT, "bmT", writes=["bmT"])
            dma("sp", gTs[:].rearrange("p a b c -> p (a b c)"), gT, "gTs", writes=["gTs"])
            S.op("act", lambda e: e.activation(out=sil[:], in_=cTs[:], func=AF.Silu), reads=["cTs"], writes=["sil"])
            S.op("dve", lambda e: e.tensor_copy(out=srep[:], in_=sil[:, 0:8].unsqueeze(2).to_broadcast([128, 8, 128])),
                 reads=["sil"], writes=["srep"])

            def f_svec(e):
                e.tensor_copy(out=svec[:, :, 0], in_=sil[:, 0:8])
                return e.tensor_copy(out=svec[:, :, 1], in_=sil[:, 8:16])
            S.op("dve", f_svec, reads=["sil"], writes=["svec"])

            pi = 0
            ri = 0
            pieces = [(i, pc) for i in range(2) for pc in range(6)]

            def load_piece(q):
                if q < len(pieces):
                    i_, pc_ = pieces[q]
                    dma("sp", wm[q % 3][:], w_mod[i_, :, pc_ * 1024:(pc_ + 1) * 1024].rearrange("(k p) n -> p k n", p=128),
                        "wm%d" % (q % 3), writes=["wm%d" % (q % 3)])
            load_piece(0)
            load_piece(1)
            for i in range(2):
                for pc in range(6):
                    wmb = wm[pi % 3]
                    wk = "wm%d" % (pi % 3)
                    load_piece(pi + 2)
                    pi += 1
                    if pc in (0, 1, 3, 4):
                        kind = {0: 0, 1: 1, 3: 2, 4: 3}[pc]

                        def f_col(e, wmb=wmb):
                            ins = None
                            for oc in range(8):
                                for k in range(8):
                                    ins = e.matmul(colps[:, oc, :], lhsT=wmb[:, k, oc * 128:(oc + 1) * 128],
                                                   rhs=svec[:, k, :], start=(k == 0), stop=(k == 7))
                            return ins
                        S.op("pe", f_col, reads=[wk, "svec"], writes=["colps"])
                        S.op("dve", lambda e, i=i, kind=kind, pc=pc: e.tensor_tensor(
                            out=modT[:, i, kind], in0=colps[:],
                            in1=bmT[:, i, pc * 8:(pc + 1) * 8].unsqueeze(2).to_broadcast([128, 8, 2]), op=ALU.add),
                            reads=["colps", "bmT"], writes=["modT"])
                    else:
                        which = 0 if pc == 2 else 1
                        r = ri % 2
                        ri += 1
                        dma("sp", brow[r][:], b_mod[i:i + 1, pc * 1024:(pc + 1) * 1024].partition_broadcast(128),
                            "brow%d" % r, writes=["brow%d" % r])
                        dma("sp", grow[r][:], g_post[2 * i + which:2 * i + which + 1, :].partition_broadcast(128),
                            "grow%d" % r, writes=["grow%d" % r])
                        for hf in range(2):
                            def f_row(e, wmb=wmb, hf=hf):
                                ins = None
                                for k in range(8):
                                    ins = e.matmul(rowps[hf][:], lhsT=srep[:, k, :],
                                                   rhs=wmb[:, k, hf * 512:(hf + 1) * 512], start=(k == 0), stop=(k == 7))
                                return ins
                            S.op("pe", f_row, reads=[wk, "srep"], writes=["rowps%d" % hf])

                            sl_ = slice(hf * 512, (hf + 1) * 512)
                            f_rowev = [lambda e, r=r, hf=hf, sl=sl_: e.tensor_tensor(out=gtg[r][:, sl], in0=rowps[hf][:], in1=brow[r][:, sl], op=ALU.add),
                                       lambda e, r=r, hf=hf, sl=sl_: e.tensor_tensor(out=gtg[r][:, sl], in0=gtg[r][:, sl], in1=grow[r][:, sl], op=ALU.mult)]
                            S.op("dve", f_rowev, reads=["rowps%d" % hf, "brow%d" % r, "grow%d" % r], writes=["gtg%d" % r])
                        gi = 2 * i + which
                        dma("sp", gtgscr[gi * 128:(gi + 1) * 128, :], gtg[r][:], "gtg%d" % r,
                            reads=["gtg%d" % r], writes=[("gtgscr", gi)])

            def f_modc(e):
                ins = None
                for i in range(2):
                    for kd in range(2):
                        e.scalar_tensor_tensor(out=modc[:, i, kd, 0], in0=modT[:, i, 2 * kd + 1], scalar=1.0,
                                               in1=gTs[:, i, kd].unsqueeze(2).to_broadcast([128, 8, 2]),
                                               op0=ALU.add, op1=ALU.mult)
                        ins = e.tensor_copy(out=modc[:, i, kd, 1], in_=modT[:, i, 2 * kd])
                return ins
            S.op("dve", f_modc, reads=["modT", "gTs"], writes=["modc"])
            if DEBUG:
                dma("sp", dbg_modc, modc[:].rearrange("p a b c d e -> p (a b c d e)"), "dbgmodc", reads=["modc"])
            S.barrier()

        def prenorm(xb, xkey, ms, mskey, rs, rskey, xn, xnkey, tp, tpkey, hT_dst, hkey, li, kd, vec, scr, scrkey, part="ab"):
            if "a" in part:
                S.op("act", lambda e: e.activation(out=scr[:], in_=xb[:], func=AF.Square, scale=1.0 / 32.0, accum_out=ms),
                     reads=[xkey], writes=[scrkey, mskey])
                rstd_ops(ms, 1, rs, mskey, rskey)
                S.op("dve", lambda e: e.tensor_scalar(out=xn[:], in0=xb[:], scalar1=rs, scalar2=None, op0=ALU.mult),
                     reads=[xkey, rskey], writes=[xnkey])
            if "b" not in part:
                return

            def f_tr(e):
                ins = None
                for k in range(8):
                    ins = e.transpose(tp[:, k, :], xn[:, k * 128:(k + 1) * 128], ident[:])
                return ins
            S.op("pe", f_tr, reads=[xnkey, "ident"], writes=[tpkey])
            if A_STEP < 6:
                return

            def f_mod(e):
                ins = None
                for k in range(8):
                    ins = e.activation(out=hT_dst(k), in_=tp[:, k, :], func=AF.Identity,
                                       scale=modc[:, li, kd, 0, k, vec:vec + 1], bias=modc[:, li, kd, 1, k, vec:vec + 1])
                return ins
            S.op("act", f_mod, reads=[tpkey, "modc"], writes=[hkey])

        def postnorm(yps, ykeys, ss2, sskey, rs, rskey, gtgt, xb, xkey, tmp, tmpkey, scr, scrkey, dst_ap, dstkey):
            def f_sq(e):
                e.activation(out=scr[:, 0:512], in_=yps[0][:], func=AF.Square, scale=1.0 / 32.0, accum_out=ss2[:, 0:1])
                return e.activation(out=scr[:, 512:1024], in_=yps[1][:], func=AF.Square, scale=1.0 / 32.0,
                                    accum_out=ss2[:, 1:2])
            S.op("act", f_sq, reads=list(ykeys), writes=[scrkey, sskey])
            rstd_ops(ss2, 2, rs, sskey, rskey)

            def f_pn1(e):
                ins = None
                for hf in range(2):
                    sl = slice(hf * 512, (hf + 1) * 512)
                    ins = e.scalar_tensor_tensor(out=tmp[:, sl], in0=yps[hf][:], scalar=rs, in1=gtgt[:, sl],
                                                 op0=ALU.mult, op1=ALU.mult)
                return ins
            f_pn = [f_pn1, lambda e: e.tensor_tensor(out=tmp[:], in0=tmp[:], in1=xb[:], op=ALU.add)]
            S.op("dve", f_pn, reads=list(ykeys) + [rskey, "gtgt", xkey], writes=[tmpkey])
            dma("sp", dst_ap, tmp[:], tmpkey, reads=[tmpkey], writes=[dstkey])

        with ExitStack() as pa:
          if "A" in PHASES:
            w_in_b = sb(pa, "w_in_b", [128, 8, 1792], BF16)
            w_out_b = sb(pa, "w_out_b", [128, 8, 1024], BF16)
            wsT_b = sb(pa, "wsT_b", [128, 8, 128], BF16)
            bsT_s = sb(pa, "bsT_s", [128, 8], F32)
            vnorm_b = sb(pa, "vnorm_b", [128, 512], F32)
            sinkexp = sb(pa, "sinkexp", [128, 8], F32)
            rC = sb(pa, "rC", [128, NE, 64], F32)
            rS = sb(pa, "rS", [128, NE, 64], F32)
            kbias = sb(pa, "kbias_s", [128, NE + 2], F32)
            trim = sb(pa, "trim", [128, 2, 512], BF16)
            gtgt = sb(pa, "gtgt", [128, 1024], F32)
            kTc = sb(pa, "kTc", [128, NE + 2, 128], BF16)
            vc = sb(pa, "vc", [128, NE + 2, 2, 65], BF16)
            NXR = 5
            xr = [sb(pa, "xr%d" % i, [128, 1024], F32) for i in range(NXR)]
            msr = sb(pa, "msr", [128, 8], F32)
            rsr = sb(pa, "rsr", [128, 8], F32)
            scr = sb(pa, "scrA", [128, 1024], F32)
            xn = [sb(pa, "xn%d" % i, [128, 1024], BF16) for i in range(2)]
            hT = [sb(pa, "hT%d" % i, [128, 8, 128], BF16) for i in range(2)]
            ccx = sb(pa, "ccx", [128, 8, 64], F32)
            ssx = sb(pa, "ssx", [128, 8, 64], F32)
            qks = sb(pa, "qks", [128, 640], F32)
            bsx = sb(pa, "bsx", [128, 8, 64], F32)
            rt1 = sb(pa, "rt1", [128, 640], F32)
            rt2 = sb(pa, "rt2", [128, 640], F32)
            qr = sb(pa, "qr", [128, 640], BF16)
            qT = [sb(pa, "qT%d" % i, [128, 4, 128], BF16) for i in range(2)]
            u_sb = [sb(pa, "u_sb%d" % i, [128, 512], F32) for i in range(2)]
            cen = sb(pa, "cen", [128, 512], F32)
            sq = sb(pa, "sq", [128, 512], F32)
            st8 = sb(pa, "st8", [128, 4, 8], F32)
            vn = [sb(pa, "vn%d" % i, [128, 512], BF16) for i in range(2)]
            pT = [sb(pa, "pT%d" % i, [128, 512], BF16) for i in range(3)]
            den = sb(pa, "den", [128, 2, 8], F32)
            mix = sb(pa, "mix", [128, 1024], BF16)
            gtmp = sb(pa, "gtmp", [128, 512], F32)
            mixT = sb(pa, "mixT", [128, 8, 128], BF16)
            tmpo = [sb(pa, "tmpo%d" % i, [128, 1024], F32) for i in range(2)]
            ss2 = sb(pa, "ss2", [128, 2], F32)
            rs2 = sb(pa, "rs2", [128, 1], F32)
            tpq = ps(pa, "tpq", [128, 8, 128], BF16)
            tm = tpq
            bk = [ps(pa, "bkA%d" % i, [128, 512]) for i in range(7)]
            pj0, pj1, pj2, pj3, spsb, ops0b, spsb2 = bk
            spsl = [spsb, spsb2]
            spskey = ["bk4", "bk6"]
            yps = [pj0, pj1]
            gps = pj2
            ops = [ops0b[:, 0:260].rearrange("p (h d) -> p h d", d=65), pj3[:, 0:260].rearrange("p (h d) -> p h d", d=65)]
            opskey = ["bk5", "bk3"]

            dma("pool", w_in_b[:], w_in.rearrange("(k p) n -> p k n", p=128), "w_in_b", writes=["w_in_b"])
            dma("pool", wsT_b[:].rearrange("p g i -> p (g i)"), wsT, "wsT_b", writes=["wsT_b"])
            dma("pool", trim[:].rearrange("p a n -> p (a n)"), trimask_in, "trim", writes=["trim"])
            dma("pool", w_out_b[:], w_out.rearrange("(k p) n -> p k n", p=128), "w_out_b", writes=["w_out_b"])
            dma("sp", bsT_s[:], bsT, "bsT_s", writes=["bsT_s"])
            dma("sp", vnorm_b[:], vnorm.partition_broadcast(128), "vnorm_b", writes=["vnorm_b"])
            dma("sp", sinkexp[:], sink.partition_broadcast(128), "sinkexp", writes=["sinkexp"])
            dma("sp", rC[:].rearrange("p e f -> p (e f)"), ropeC, "rC", writes=["rC"])
            dma("sp", rS[:].rearrange("p e f -> p (e f)"), ropeS, "rS", writes=["rS"])
            dma("sp", kbias[:], kbias_in, "kbias", writes=["kbias"])
            dma("sp", gtgt[:], gtgscr[0:128, :], "gtgt", reads=[("gtgscr", 0)], writes=["gtgt"])
            S.op("act", lambda e: e.activation(out=sinkexp[:], in_=sinkexp[:], func=AF.Exp), reads=["sinkexp"], writes=["sinkexp"])
            S.op("dve", lambda e: e.memset(vc[:].rearrange("p e k d -> p (e k d)"), 1.0), writes=["vc_init"])
            S.op("pool", lambda e: e.tensor_copy(out=bsx[:], in_=bsT_s[:].unsqueeze(2).to_broadcast([128, 8, 64])),
                 reads=["bsT_s"], writes=["bsx"])

            def proj_stage(e_idx, n):
                is_ctx = e_idx >= NE
                full = (not is_ctx) and (1 <= e_idx <= NE - 2)
                xb = xr[n % NXR]
                xk = "xr%d" % (n % NXR)
                c8 = n % 8
                h = hT[n % 2]
                hk = "hT%d" % (n % 2)
                prenorm(xb, xk, msr[:, c8:c8 + 1], ("ms", c8), rsr[:, c8:c8 + 1], ("rs", c8), xn[n % 2], "xn%d" % (n % 2),
                        tpq, "tpq", lambda k: h[:, k, :], hk, 0, 0, 1 if is_ctx else 0, scr, "scrA")
                if e_idx == DBG_E:
                    dbgdump("d_hT", h[:].rearrange("p k t -> p (k t)"), hk)
                if A_STEP < 7:
                    return
                groups = [(3, 1024 + 512, 256)]
                if full:
                    groups = [(0, 0, 512), (1, 512, 512), (2, 1024, 512), (3, 1536, 256)]

                def f_proj(e):
                    ins = None
                    for (b, c0, w) in groups:
                        for k in range(8):
                            ins = e.matmul(bk[b][:, 0:w], lhsT=h[:, k, :], rhs=w_in_b[:, k, c0:c0 + w],
                                           start=(k == 0), stop=(k == 7))
                    return ins
                S.op("pe", f_proj, reads=[hk, "w_in_b"], writes=["bk%d" % g[0] for g in groups])
                if A_STEP < 8:
                    return
                S.op("act", lambda e: e.activation(out=vc[:, e_idx, :, 0:64],
                                                   in_=pj3[:, 128:256].rearrange("p (k d) -> p k d", d=64), func=AF.Copy),
                     reads=["bk3", "vc_init"], writes=[("vc", e_idx)])
                if A_STEP < 9:
                    return
                if is_ctx:
                    S.op("dve", lambda e: e.tensor_copy(out=qr[:, 512:640], in_=pj3[:, 0:128]), reads=["bk3"], writes=["qr_k"])
                else:
                    def f_exp(e):
                        e.tensor_copy(out=ccx[:], in_=rC[:, e_idx, :].unsqueeze(1).to_broadcast([128, 8, 64]))
                        return e.tensor_copy(out=ssx[:], in_=rS[:, e_idx, :].unsqueeze(1).to_broadcast([128, 8, 64]))
                    S.op("pool", f_exp, reads=["rC", "rS"], writes=["ccx"])

                    def f_cp(e):
                        ins = e.activation(out=qks[:, 512:640], in_=pj3[:, 0:128], func=AF.Copy)
                        if full:
                            ins = e.activation(out=qks[:, 0:512], in_=pj0[:], func=AF.Copy)
                        return ins
                    S.op("act", f_cp, reads=["bk3"] + (["bk0"] if full else []), writes=["qks"])

                    segs = [(512, 640, 2)] + ([(0, 512, 8)] if full else [])

                    def f_rope1(e):
                        ins = None
                        for (c0, c1, nh) in segs:
                            e.tensor_tensor(out=rt1[:, c0:c1], in0=qks[:, c0:c1],
                                            in1=ccx[:, 0:nh, :].rearrange("p h f -> p (h f)"), op=ALU.mult)
                            s5 = qks[:, c0:c1].rearrange("p (h a b f) -> p h a b f", a=2, b=2, f=16)
                            t5 = rt2[:, c0:c1].rearrange("p (h a b f) -> p h a b f", a=2, b=2, f=16)
                            x5 = ssx[:, 0:nh, :].rearrange("p h (a b f) -> p h a b f", a=2, b=2, f=16)
                            for ab in range(2):
                                ins = e.tensor_tensor(out=t5[:, :, :, ab, :], in0=s5[:, :, :, 1 - ab, :], in1=x5[:, :, :, ab, :],
                                                      op=ALU.mult)
                        return ins

                    def f_rope2(e):
                        ins = None
                        for (c0, c1, nh) in segs:
                            ins = e.tensor_tensor(out=qr[:, c0:c1], in0=rt1[:, c0:c1], in1=rt2[:, c0:c1], op=ALU.add)
                        return ins
                    f_rope = [f_rope1, f_rope2]
                    S.op("dve", f_rope, reads=["qks", "ccx"], writes=["qr_k", "qr_q", "rt_k"])
                    if e_idx == DBG_E:
                        dbgdump("d_qks", qks[:], "qks")
                        dbgdump("d_qr", qr[:], "qr_k")
                if A_STEP < 10:
                    return
                nq = 4 if full else 0

                def f_trq(e):
                    ins = None
                    for s_ in range(nq):
                        ins = e.transpose(tpq[:, s_, :], qr[:, s_ * 128:(s_ + 1) * 128], ident[:])
                    return e.transpose(tpq[:, 4, :], qr[:, 512:640], ident[:])
                S.op("pe", f_trq, reads=["qr_k", "ident"] + (["qr_q"] if full else []), writes=["tpq"])
                if A_STEP < 11:
                    return
                S.op("act", lambda e: e.activation(out=kTc[:, e_idx, :], in_=tpq[:, 4, :], func=AF.Copy),
                     reads=["tpq"], writes=[("kT", e_idx)])
                if full:
                    qTt = qT[e_idx % 2]
                    S.op("act", lambda e: e.activation(out=qTt[:], in_=tpq[:, 0:4, :], func=AF.Copy),
                         reads=["tpq"], writes=["qT%d" % (e_idx % 2)])
                    us = u_sb[e_idx % 2]
                    S.op("act", lambda e: e.activation(out=us[:], in_=pj1[:], func=AF.Copy), reads=["bk1"],
                         writes=["u%d" % (e_idx % 2)])
                    vnt = vn[e_idx % 2]

                    S.op("act", lambda e: e.activation(out=sq[:], in_=pj2[:], func=AF.Square), reads=["bk2"], writes=["sq"])
                    S.op("act", lambda e: e.activation(out=cen[:], in_=pj2[:], func=AF.Copy), reads=["bk2"], writes=["cen"])

                    def f_ln1a(e):
                        e.tensor_reduce(out=st8[:, 0, :], in_=cen[:].rearrange("p (g d) -> p g d", d=64), axis=AX.X, op=ALU.add)
                        return e.tensor_reduce(out=st8[:, 2, :], in_=sq[:].rearrange("p (g d) -> p g d", d=64), axis=AX.X, op=ALU.add)

                    def f_ln1b(e):
                        e.tensor_scalar(out=st8[:, 0, :], in0=st8[:, 0, :], scalar1=1.0 / 64.0, scalar2=None, op0=ALU.mult)
                        return e.tensor_scalar(out=st8[:, 2, :], in0=st8[:, 2, :], scalar1=1.0 / 64.0, scalar2=None, op0=ALU.mult)
                    f_ln1 = [f_ln1a, f_ln1b,
                             lambda e: e.tensor_tensor(out=st8[:, 1, :], in0=st8[:, 0, :], in1=st8[:, 0, :], op=ALU.mult),
                             lambda e: e.tensor_tensor(out=st8[:, 2, :], in0=st8[:, 2, :], in1=st8[:, 1, :], op=ALU.subtract)]
                    S.op("dve", f_ln1, reads=["cen", "sq"], writes=["st8v", "st8n"])

                    f_ln3 = [lambda e: e.tensor_tensor(out=st8[:, 3, :], in0=st8[:, 2, :], in1=epsl8[:], op=ALU.add),
                             lambda e: e.tensor_tensor(out=st8[:, 3, :], in0=st8[:, 3, :], in1=mh8[:], op=ALU.pow)]
                    S.op("pool", f_ln3, reads=["st8v", "epsl8", "mh8"], writes=["st8r"])

                    S.op("dve", lambda e: e.scalar_tensor_tensor(out=st8[:, 1, :], in0=st8[:, 0, :], scalar=-1.0, in1=st8[:, 3, :],
                                                                   op0=ALU.mult, op1=ALU.mult),
                         reads=["st8r", "st8v"], writes=["st8n"])

                    def f_ln4a(e):
                        ins = None
                        for g in range(8):
                            ins = e.tensor_scalar(out=cen[:, g * 64:(g + 1) * 64], in0=cen[:, g * 64:(g + 1) * 64],
                                                  scalar1=st8[:, 3, g:g + 1], scalar2=st8[:, 1, g:g + 1], op0=ALU.mult, op1=ALU.add)
                        return ins
                    f_ln4 = [f_ln4a, lambda e: e.tensor_tensor(out=vnt[:], in0=cen[:], in1=vnorm_b[:], op=ALU.mult)]
                    S.op("dve", f_ln4, reads=["cen", "st8r", "st8n", "vnorm_b"], writes=["vn%d" % (e_idx % 2), "cen"], same_sync=True)
                    if e_idx == DBG_E:
                        dbgdump("d_u", us[:], "u%d" % (e_idx % 2))
                        dbgdump("d_vn", vnt[:], "vn%d" % (e_idx % 2))
                        dbgdump("d_qT", qTt[:].rearrange("p s t -> p (s t)"), "qT%d" % (e_idx % 2))

            def attn_stage(e_idx):
                qTt = qT[e_idx % 2]
                qk = "qT%d" % (e_idx % 2)
                us = u_sb[e_idx % 2]
                vnt = vn[e_idx % 2]
                items = [(kv, ci, kb) for kv in range(2) for ci, kb in enumerate([e_idx - 1, e_idx, e_idx + 1, NE, NE + 1])]

                def emit_qk(i):
                    kv, ci, kb = items[i]
                    sp_ = spsl[i % 2]

                    def f_qk(e):
                        ins = e.matmul(sp_[:], lhsT=kTc[kv * 64:(kv + 1) * 64, kb, :],
                                       rhs=qTt[kv * 64:(kv + 1) * 64, :, :].rearrange("p s t -> p (s t)"),
                                       start=True, stop=(ci not in (0, 2)))
                        if ci in (0, 2):
                            ins = e.matmul(sp_[:], lhsT=ident[:], rhs=trim[:, ci // 2, :], start=False, stop=True)
                        return ins
                    S.op("pe", f_qk, reads=[("kT", kb), qk, "ident", "trim"], writes=[spskey[i % 2]])
                    pt = pT[i % 3]
                    S.op("act", lambda e: e.activation(out=pt[:], in_=sp_[:], func=AF.Exp, scale=0.125, bias=kbias[:, kb:kb + 1]),
                         reads=[spskey[i % 2], "kbias"], writes=["pT%d" % (i % 3)])
                    if e_idx == DBG_E and i == 0:
                        dbgdump("d_pT", pt[:], "pT%d" % (i % 3))

                def emit_pv(i):
                    kv, ci, kb = items[i]
                    pt = pT[i % 3]

                    def f_pv(e):
                        ins = None
                        for hh in range(4):
                            ins = e.matmul(ops[kv][:, hh, :], lhsT=pt[:, hh * 128:(hh + 1) * 128], rhs=vc[:, kb, kv, :],
                                           start=(ci == 0 and hh == 0), stop=(ci == 4), skip_group_check=True)
                        return ins
                    S.op("pe", f_pv, reads=["pT%d" % (i % 3), ("vc", kb), "vc_init"], writes=[opskey[kv]])

                emit_qk(0)
                for i in range(len(items)):
                    if i + 1 < len(items):
                        emit_qk(i + 1)
                    emit_pv(i)
                for kv in range(2):
                    f_den = [lambda e, kv=kv: e.tensor_tensor(out=den[:, 0, kv * 4:(kv + 1) * 4], in0=ops[kv][:, :, 64],
                                                               in1=sinkexp[:, kv * 4:(kv + 1) * 4], op=ALU.add),
                             lambda e, kv=kv: e.reciprocal(out=den[:, 1, kv * 4:(kv + 1) * 4], in_=den[:, 0, kv * 4:(kv + 1) * 4])]
                    S.op("dve", f_den, reads=[opskey[kv], "sinkexp"], writes=[("den", kv)])

                    def f_norm(e, kv=kv):
                        ins = None
                        for hh in range(4):
                            c0 = kv * 256 + hh * 64
                            ins = e.tensor_scalar(out=mix[:, c0:c0 + 64], in0=ops[kv][:, hh, 0:64],
                                                  scalar1=den[:, 1, kv * 4 + hh:kv * 4 + hh + 1], scalar2=None, op0=ALU.mult)
                        return ins
                    S.op("dve", f_norm, reads=[opskey[kv], ("den", kv)], writes=["mix_a%d" % kv], same_sync=True)

                def f_gate(e):
                    ins = None
                    for g in range(8):
                        ins = e.matmul(gps[:, g * 64:(g + 1) * 64], lhsT=wsT_b[:, g, :], rhs=vnt[:, g * 64:(g + 1) * 64],
                                       start=True, stop=True)
                    return ins
                S.op("pe", f_gate, reads=["wsT_b", "vn%d" % (e_idx % 2)], writes=["bk2"])

                f_gev = [lambda e: e.tensor_tensor(out=gtmp[:], in0=gps[:], in1=bsx[:].rearrange("p g d -> p (g d)"), op=ALU.add),
                         lambda e: e.tensor_tensor(out=mix[:, 512:1024], in0=gtmp[:], in1=us[:], op=ALU.mult)]
                S.op("dve", f_gev, reads=["bk2", "bsx", "u%d" % (e_idx % 2)], writes=["mix_g", "gtmp"])

                def f_trm(e):
                    ins = None
                    for c in range(8):
                        ins = e.transpose(tm[:, c, :], mix[:, c * 128:(c + 1) * 128], ident[:])
                    return ins
                if e_idx == DBG_E:
                    dbgdump("d_mix", mix[:], "mix_g")
                S.op("pe", f_trm, reads=["mix_a0", "mix_a1", "mix_g", "ident"], writes=["tpq"])
                S.op("act", lambda e: e.activation(out=mixT[:], in_=tm[:], func=AF.Copy), reads=["tpq"], writes=["mixT"])
                for hf in range(2):
                    def f_y(e, hf=hf):
                        ins = None
                        for c in range(8):
                            ins = e.matmul(yps[hf][:], lhsT=mixT[:, c, :], rhs=w_out_b[:, c, hf * 512:(hf + 1) * 512],
                                           start=(c == 0), stop=(c == 7))
                        return ins
                    S.op("pe", f_y, reads=["mixT", "w_out_b"], writes=["bk%d" % hf])
                n = stage_of[e_idx]
                to = tmpo[e_idx % 2]
                postnorm(yps, ["bk0", "bk1"], ss2, "ss2", rs2[:, 0:1], "rs2", gtgt, xr[n % NXR], "xr%d" % (n % NXR), to,
                         "tmpo%d" % (e_idx % 2), scr, "scrA", x1[(e_idx - 1) * 128:e_idx * 128, :], ("x1", e_idx))

            stage_of = {}
            n = 0
            order = [NE, NE + 1] + list(range(NE))
            done_attn = 0
            if A_LIMIT is not None:
                order = order[:A_LIMIT[0]]
            def load_x(i):
                if i < len(order):
                    ei = order[i]
                    src = ctxin[(ei - NE) * 128:(ei - NE + 1) * 128, :] if ei >= NE else xin[ei * 128:(ei + 1) * 128, :]
                    dma("sp", xr[i % NXR][:], src, "xr%d" % (i % NXR), writes=["xr%d" % (i % NXR)])
            load_x(0)
            load_x(1)
            for e_idx in order:
                load_x(n + 2)
                stage_of[e_idx] = n
                proj_stage(e_idx, n)
                n += 1
                if e_idx < NE and e_idx >= 2 and (A_LIMIT is None or A_LIMIT[1]):
                    attn_stage(e_idx - 1)
            S.barrier()

        def ffn_phase(li, src, dst, nblk, gi, tagp):
            TT = 256
            ntile = nblk // 2
            with ExitStack() as pf:
                w1b = sb(pf, tagp + "w1b", [128, 8, DFF], BF16)
                w3b = sb(pf, tagp + "w3b", [128, 8, DFF], BF16)
                w2b = sb(pf, tagp + "w2b", [128, NJ, 1024], BF16)
                gtgt = sb(pf, tagp + "gtgt", [128, 1024], F32)
                NXR = 6
                xr = [sb(pf, tagp + "xr%d" % i, [128, 1024], F32) for i in range(NXR)]
                msr = sb(pf, tagp + "msr", [128, 8], F32)
                rsr = sb(pf, tagp + "rsr", [128, 8], F32)
                scr = sb(pf, tagp + "scr", [128, 1024], F32)
                xn = [sb(pf, tagp + "xn%d" % i, [128, 1024], BF16) for i in range(4)]
                hT = [sb(pf, tagp + "hT%d" % i, [128, 8, TT], BF16) for i in range(2)]
                sg = [sb(pf, tagp + "sg%d" % i, [128, TT], F32) for i in range(2)]
                act = sb(pf, tagp + "act", [128, NJ, TT], BF16)
                tmpo = [sb(pf, tagp + "tmpo%d" % i, [128, 1024], F32) for i in range(2)]
                ss2 = sb(pf, tagp + "ss2", [128, 2], F32)
                rs2 = sb(pf, tagp + "rs2", [128, 1], F32)
                tp = ps(pf, tagp + "tp", [128, 8, 128], BF16)
                gpsb = [ps(pf, tagp + "gps%d" % i, [128, 512]) for i in range(2)]
                upsb = [ps(pf, tagp + "ups%d" % i, [128, 512]) for i in range(2)]
                ypsb = [ps(pf, tagp + "yps%d" % i, [128, 512]) for i in range(3)]

                for k in range(8):
                    dma("pool", w1b[:, k, :], ffn_w1[li, k * 128:(k + 1) * 128, :], "w1b%d" % k, writes=[("w1b", k)])
                    dma("pool", w3b[:, k, :], ffn_w3[li, k * 128:(k + 1) * 128, :], "w3b%d" % k, writes=[("w3b", k)])
                for j0 in range(0, NJ, 2):
                    dma("pool", w2b[:, j0:j0 + 2, :], ffn_w2[li, j0 * 128:(j0 + 2) * 128, :].rearrange("(c p) n -> p c n", p=128),
                        "w2b%d" % j0, writes=[("w2b", j0)])
                dma("sp", gtgt[:], gtgscr[gi * 128:(gi + 1) * 128, :], "gtgt", reads=[("gtgscr", gi)], writes=["gtgt"])

                def pre(t, part="lab"):
                    for bl in range(2):
                        b = 2 * t + bl
                        xb = xr[b % NXR]
                        xk = "xr%d" % (b % NXR)
                        if "l" in part:
                            dma("sp", xb[:], src[b * 128:(b + 1) * 128, :], xk, reads=[("src", b)], writes=[xk])
                        h = hT[t % 2]
                        c8 = b % 8
                        prenorm(xb, xk, msr[:, c8:c8 + 1], ("ms", c8), rsr[:, c8:c8 + 1], ("rs", c8), xn[b % 4], "xn%d" % (b % 4),
                                tp, "tp", lambda k, h=h, bl=bl: h[:, k, bl * 128:(bl + 1) * 128], ("hT", t % 2, bl), li, 1, 0,
                                scr, "scr", part=part)

                yrot = 0
                pre(0)
                if ntile > 1:
                    pre(1, "l")
                for t in range(ntile):
                    if t + 2 < ntile:
                        pre(t + 2, "l")
                    if t + 1 < ntile:
                        pre(t + 1, "a")
                    h = hT[t % 2]
                    hkeys = [("hT", t % 2, 0), ("hT", t % 2, 1)]
                    for j in range(NJ):
                        if j == NJ // 2 and t + 1 < ntile:
                            pre(t + 1, "b")
                        g_ = gpsb[j % 2]
                        u_ = upsb[j % 2]

                        def f_gu(e, j=j, g_=g_, u_=u_, h=h):
                            ins = None
                            for k in range(8):
                                ins = e.matmul(g_[:, 0:TT], lhsT=w1b[:, k, j * 128:(j + 1) * 128], rhs=h[:, k, :],
                                               start=(k == 0), stop=(k == 7))
                            for k in range(8):
                                ins = e.matmul(u_[:, 0:TT], lhsT=w3b[:, k, j * 128:(j + 1) * 128], rhs=h[:, k, :],
                                               start=(k == 0), stop=(k == 7))
                            return ins
                        S.op("pe", f_gu, reads=hkeys + [("w1b", k) for k in range(8)] + [("w3b", k) for k in range(8)],
                             writes=["gps%d" % (j % 2), "ups%d" % (j % 2)])
                        sgt = sg[j % 2]
                        S.op("act", lambda e, g_=g_, sgt=sgt: e.activation(out=sgt[:], in_=g_[:, 0:TT], func=AF.Silu),
                             reads=["gps%d" % (j % 2)], writes=["sg%d" % (j % 2)])
                        S.op("dve", lambda e, j=j, u_=u_, sgt=sgt: e.tensor_tensor(out=act[:, j, :], in0=u_[:, 0:TT], in1=sgt[:],
                                                                                 op=ALU.mult),
                             reads=["ups%d" % (j % 2), "sg%d" % (j % 2)], writes=[("act", j)])
                    for bl in range(2):
                        b = 2 * t + bl
                        ybanks = []
                        ykeys = []
                        for hf in range(2):
                            yb = ypsb[yrot % 3]
                            yk = "yps%d" % (yrot % 3)
                            yrot += 1
                            ybanks.append(yb)
                            ykeys.append(yk)

                            def f_y(e, yb=yb, hf=hf, bl=bl):
                                ins = None
                                for j in range(NJ):
                                    ins = e.matmul(yb[:], lhsT=act[:, j, bl * 128:(bl + 1) * 128],
                                                   rhs=w2b[:, j, hf * 512:(hf + 1) * 512], start=(j == 0), stop=(j == NJ - 1))
                                return ins
                            S.op("pe", f_y, reads=[("act", j) for j in range(NJ)] + [("w2b", j0) for j0 in range(0, NJ, 2)], writes=[yk])
                        to = tmpo[b % 2]
                        postnorm(ybanks, ykeys, ss2, "ss2", rs2[:, 0:1], "rs2", gtgt, xr[b % NXR], "xr%d" % (b % NXR), to,
                                 "tmpo%d" % (b % 2), scr, "scr", dst[b * 128:(b + 1) * 128, :], ("dst", b))
                S.barrier()

        if "B" in PHASES:
            ffn_phase(0, x1, x2, NB + 2, 1, "B")

        with ExitStack() as pc_:
          if "C" in PHASES:
            scin = sb(pc_, "scin", [128, 8, 3072], BF16)
            scout = sb(pc_, "scout", [128, 8, 1024], BF16)
            cw = sb(pc_, "cw", [128, 8, 3], F32)
            cval = sb(pc_, "cval", [128, 2], F32)
            gtgt = sb(pc_, "gtgtC", [128, 1024], F32)
            NBX = NB + 2
            hTa = sb(pc_, "hTa", [128, 8, NBX * 128], BF16)
            NXR = 6
            xr = [sb(pc_, "xrC%d" % i, [128, 1024], F32) for i in range(NXR)]
            xres = [sb(pc_, "xresC%d" % i, [128, 1024], F32) for i in range(2)]
            msr = sb(pc_, "msrC", [128, 8], F32)
            rsr = sb(pc_, "rsrC", [128, 8], F32)
            scr = sb(pc_, "scrC", [128, 1024], F32)
            xn = [sb(pc_, "xnC%d" % i, [128, 1024], BF16) for i in range(4)]
            cgs = [sb(pc_, "cgs%d" % i, [128, 258], F32) for i in range(2)]
            yb_ = [sb(pc_, "ybC%d" % i, [128, 258], F32) for i in range(2)]
            t1_ = [sb(pc_, "t1C%d" % i, [128, 256], F32) for i in range(2)]
            bgs = [sb(pc_, "bgs%d" % i, [128, 256], F32) for i in range(2)]
            z = [sb(pc_, "zC%d" % i, [128, 8, 256], BF16) for i in range(2)]
            tmpo = [sb(pc_, "tmpoC%d" % i, [128, 1024], F32) for i in range(2)]
            ss2 = sb(pc_, "ss2C", [128, 2], F32)
            rs2 = sb(pc_, "rs2C", [128, 1], F32)
            tp = ps(pc_, "tpC", [128, 8, 128], BF16)
            cgp = [ps(pc_, "cgp%d" % i, [128, 512]) for i in range(2)]
            hxp = [ps(pc_, "hxp%d" % i, [128, 512]) for i in range(2)]
            bgp = ps(pc_, "bgp", [128, 512])
            ypsb = [ps(pc_, "ypsC%d" % i, [128, 512]) for i in range(2)]

            for k in range(8):
                dma("pool", scin[:, k, :], sc_w_in[k * 128:(k + 1) * 128, :], "scin%d" % k, writes=[("scin", k)])
            dma("pool", scout[:], sc_w_out.rearrange("(k p) n -> p k n", p=128), "scout", writes=["scout"])
            dma("sp", cw[:].rearrange("p c k -> p (c k)"), cwT, "cw", writes=["cw"])
            dma("sp", cval[:], cvalid_in, "cval", writes=["cval"])
            dma("sp", gtgt[:], gtgscr[2 * 128:3 * 128, :], "gtgt", reads=[("gtgscr", 2)], writes=["gtgt"])

            def preC(b, part="ab"):
                xb = xr[b % NXR]
                xk = "xrC%d" % (b % NXR)
                if "l" in part:
                    dma("sp", xb[:], x2[b * 128:(b + 1) * 128, :], xk, writes=[xk])
                c8 = b % 8
                prenorm(xb, xk, msr[:, c8:c8 + 1], ("ms", c8), rsr[:, c8:c8 + 1], ("rs", c8), xn[b % 4], "xnC%d" % (b % 4),
                        tp, "tpC", lambda k, b=b: hTa[:, k, b * 128:(b + 1) * 128], ("hTa", b), 1, 0, 0, scr, "scrC", part=part)

            nexta = 0
            nextb = 0
            ci_ = 0
            for b0 in range(6):
                preC(b0, "l")
            nextl = 6
            for b0 in range(4):
                preC(b0, "a")
            nexta = 4
            for t in range(NB // 2):
                needl = min(NBX, 2 * t + 8)
                while nextl < needl:
                    preC(nextl, "l")
                    nextl += 1
                for bl in range(2):
                    b = 2 * t + bl
                    dma("sp", xres[b % 2][:], x2[(b + 1) * 128:(b + 2) * 128, :], "xresC%d" % (b % 2), writes=["xresC%d" % (b % 2)])
                need = min(NBX, 2 * t + 4)
                while nextb < need:
                    preC(nextb, "b")
                    nextb += 1
                needa = min(NBX, 2 * t + 6)
                while nexta < needa:
                    preC(nexta, "a")
                    nexta += 1
                base = 128 + t * 256
                hk = [("hTa", 2 * t), ("hTa", 2 * t + 1), ("hTa", 2 * t + 2), ("hTa", 2 * t + 3)]
                zt = z[t % 2]
                for c in range(8):
                    pp = ci_ % 2
                    ci_ += 1

                    def f_c(e, c=c, pp=pp, base=base):
                        ins = None
                        for k in range(8):
                            ins = e.matmul(cgp[pp][:, 0:258], lhsT=scin[:, k, 1024 + c * 128:1024 + (c + 1) * 128],
                                           rhs=hTa[:, k, base - 1:base + 257], start=(k == 0), stop=(k == 7))
                        for k in range(8):
                            ins = e.matmul(hxp[pp][:, 0:258], lhsT=scin[:, k, 2048 + c * 128:2048 + (c + 1) * 128],
                                           rhs=hTa[:, k, base - 1:base + 257], start=(k == 0), stop=(k == 7))
                        return ins
                    S.op("pe", f_c, reads=hk + [("scin", k) for k in range(8)], writes=["cgp%d" % pp, "hxp%d" % pp])

                    def f_b(e, c=c, base=base):
                        ins = None
                        for k in range(8):
                            ins = e.matmul(bgp[:, 0:256], lhsT=scin[:, k, c * 128:(c + 1) * 128],
                                           rhs=hTa[:, k, base:base + 256], start=(k == 0), stop=(k == 7))
                        return ins
                    S.op("pe", f_b, reads=hk + [("scin", k) for k in range(8)], writes=["bgp"])
                    S.op("act", lambda e, pp=pp: e.activation(out=bgs[pp][:], in_=bgp[:, 0:256], func=AF.Copy),
                         reads=["bgp"], writes=["bgs%d" % pp])
                    S.op("act", lambda e, pp=pp: e.activation(out=cgs[pp][:], in_=cgp[pp][:, 0:258], func=AF.Copy),
                         reads=["cgp%d" % pp], writes=["cgs%d" % pp])

                    f_y1 = [lambda e, pp=pp: e.tensor_tensor(out=yb_[pp][:], in0=hxp[pp][:, 0:258], in1=cgs[pp][:], op=ALU.mult)]
                    if t == 0:
                        f_y1.append(lambda e, pp=pp: e.tensor_scalar(out=yb_[pp][:, 0:1], in0=yb_[pp][:, 0:1], scalar1=cval[:, 0:1],
                                                                    scalar2=None, op0=ALU.mult))
                    if t == NB // 2 - 1:
                        f_y1.append(lambda e, pp=pp: e.tensor_scalar(out=yb_[pp][:, 257:258], in0=yb_[pp][:, 257:258],
                                                                    scalar1=cval[:, 1:2], scalar2=None, op0=ALU.mult))
                    S.op("dve", f_y1, reads=["hxp%d" % pp, "cgs%d" % pp, "cval"], writes=["ybC%d" % pp])

                    f_cv = [lambda e, pp=pp, c=c: e.tensor_scalar(out=t1_[pp][:], in0=yb_[pp][:, 1:257], scalar1=cw[:, c, 1:2],
                                                                 scalar2=None, op0=ALU.mult),
                            lambda e, pp=pp, c=c: e.scalar_tensor_tensor(out=t1_[pp][:], in0=yb_[pp][:, 0:256], scalar=cw[:, c, 0:1],
                                                                        in1=t1_[pp][:], op0=ALU.mult, op1=ALU.add),
                            lambda e, pp=pp, c=c: e.scalar_tensor_tensor(out=t1_[pp][:], in0=yb_[pp][:, 2:258], scalar=cw[:, c, 2:3],
                                                                        in1=t1_[pp][:], op0=ALU.mult, op1=ALU.add)]
                    S.op("dve", f_cv, reads=["ybC%d" % pp, "cw"], writes=["t1C%d" % pp])
                    S.op("dve", lambda e, pp=pp, c=c, zt=zt: e.tensor_tensor(out=zt[:, c, :], in0=bgs[pp][:], in1=t1_[pp][:],
                                                                            op=ALU.mult),
                         reads=["bgs%d" % pp, "t1C%d" % pp], writes=[("z", t % 2, c)])
                for bl in range(2):
                    b = 2 * t + bl
                    for hf in range(2):
                        def f_yo(e, hf=hf, bl=bl, zt=zt):
                            ins = None
                            for c in range(8):
                                ins = e.matmul(ypsb[hf][:], lhsT=zt[:, c, bl * 128:(bl + 1) * 128],
                                               rhs=scout[:, c, hf * 512:(hf + 1) * 512], start=(c == 0), stop=(c == 7))
                            return ins
                        S.op("pe", f_yo, reads=[("z", t % 2, c) for c in range(8)] + ["scout"], writes=["ypsC%d" % hf])
                    xb = xres[b % 2]
                    xk = "xresC%d" % (b % 2)
                    to = tmpo[b % 2]
                    postnorm(ypsb, ["ypsC0", "ypsC1"], ss2, "ss2C", rs2[:, 0:1], "rs2C", gtgt, xb, xk, to, "tmpoC%d" % (b % 2),
                             scr, "scrC", x3[b * 128:(b + 1) * 128, :], ("x3", b))
            S.barrier()

        if "D" in PHASES:
            ffn_phase(1, x3, out, NB, 3, "D")

        S.finalize()
        last_dmas = list(S.dsem.values())
        block = es.enter_context(nc.Block())

        @block.tensor
        def _(e):
            S.emit("pe", e)

        @block.scalar
        def _(e):
            S.emit("act", e)

        @block.vector
        def _(e):
            S.emit("dve", e)

        @block.gpsimd
        def _(e):
            S.emit("pool", e)

        @block.sync
        def _(e):
            S.emit("sp", e)
            for sem, cnt in last_dmas:
                e.wait_ge(sem, cnt)
    return nc


def _host_prep(inputs):
    f32 = np.float32
    x = np.asarray(inputs["x"], f32)
    c = np.asarray(inputs["c"], f32)
    ctx = np.asarray(inputs["ctx"], f32)
    c_ctx = np.asarray(inputs["c_ctx"], f32)
    w_mod = np.ascontiguousarray(np.asarray(inputs["w_mod"], f32))
    b_mod = np.ascontiguousarray(np.asarray(inputs["b_mod"], f32))
    g_mix_pre = np.asarray(inputs["g_mix_pre"], f32)
    g_mix_post = np.asarray(inputs["g_mix_post"], f32)
    g_ffn_pre = np.asarray(inputs["g_ffn_pre"], f32)
    g_ffn_post = np.asarray(inputs["g_ffn_post"], f32)
    a_w_in = np.asarray(inputs["a_w_in"], f32)[0]
    qcols = np.concatenate([np.arange(h * 64, (h + 1) * 64) for h in (0, 4, 1, 5, 2, 6, 3, 7)])
    cols = np.concatenate([qcols, np.arange(768, 1280), np.arange(1280, 1792), np.arange(512, 640), np.arange(640, 768)])
    w_in = np.ascontiguousarray(a_w_in[:, cols])
    shared = {
        "w_mod": w_mod,
        "b_mod": b_mod,
        "b_modT": np.ascontiguousarray(b_mod.reshape(2, 48, 128).transpose(2, 0, 1).reshape(128, 96)),
        "gT": np.ascontiguousarray(np.stack([g_mix_pre, g_ffn_pre], axis=1).reshape(2, 2, 8, 128).transpose(3, 0, 1, 2).reshape(128, 32)),
        "g_post": np.ascontiguousarray(np.stack([g_mix_post, g_ffn_post], axis=1).reshape(4, D)),
        "w_in": w_in,
        "w_out": np.ascontiguousarray(np.asarray(inputs["a_w_out"], f32)[0]),
        "wsT": np.ascontiguousarray(np.asarray(inputs["gm_ws"], f32)[0].transpose(2, 0, 1).reshape(128, 1024)),
        "bsT": np.ascontiguousarray(np.asarray(inputs["gm_bs"], f32)[0].T),
        "vnorm": np.ascontiguousarray(np.asarray(inputs["gm_v_norm"], f32).reshape(1, 512)),
        "sink": np.ascontiguousarray(np.asarray(inputs["a_sink"], f32).reshape(1, 8)),
        "ident": np.eye(128, dtype=f32),
        "ffn_w1": np.ascontiguousarray(np.asarray(inputs["ffn_w1"], f32)),
        "ffn_w3": np.ascontiguousarray(np.asarray(inputs["ffn_w3"], f32)),
        "ffn_w2": np.ascontiguousarray(np.asarray(inputs["ffn_w2"], f32)),
        "sc_w_in": np.ascontiguousarray(np.asarray(inputs["sc_w_in"], f32)[0]),
        "sc_w_out": np.ascontiguousarray(np.asarray(inputs["sc_w_out"], f32)[0]),
        "cwT": np.ascontiguousarray(np.asarray(inputs["sc_conv"], f32)[0].reshape(3, 8, 128).transpose(2, 1, 0).reshape(128, 24)),
    }
    kj = np.arange(128)[:, None]
    qi = np.arange(128)[None, :]
    m_prev = np.where(kj >= qi, 0.0, -30000.0).astype(f32)
    m_next = np.where(kj <= qi, 0.0, -30000.0).astype(f32)
    shared["trimask"] = np.ascontiguousarray(np.concatenate([np.tile(m_prev, (1, 4)), np.tile(m_next, (1, 4))], axis=1))
    inv = (np.float32(10000.0) ** (-np.arange(16, dtype=f32) / np.float32(16))).astype(f32)
    in_maps = []
    for core in range(NCORES):
        b, qd = divmod(core, 4)
        t0 = qd * 4096 - 256
        xin = np.zeros((NE * 128, D), f32)
        lo, hi = max(t0, 0), min(t0 + NE * 128, SEQ)
        xin[lo - t0:hi - t0] = x[b, lo:hi]
        tpos = np.arange(t0, t0 + NE * 128)
        tpos = np.clip(tpos, 0, SEQ - 1)
        row = (tpos // 64).astype(f32)[:, None]
        col = (tpos % 64).astype(f32)[:, None]
        ar = (row * inv).astype(f32)
        ac = (col * inv).astype(f32)
        cr, sr, cc_, sc_ = np.cos(ar).astype(f32), np.sin(ar).astype(f32), np.cos(ac).astype(f32), np.sin(ac).astype(f32)
        ropeC = np.concatenate([cr, cr, cc_, cc_], axis=1)
        ropeS = np.concatenate([-sr, sr, -sc_, sc_], axis=1)
        kb = np.zeros((128, NE + 2), f32)
        for e in range(NE):
            gb = qd * 32 + e - 2
            if gb < 0 or gb >= SEQ // 128:
                kb[:, e] = -30000.0
        cv = np.zeros((128, 2), f32)
        cv[:, 0] = 1.0 if qd > 0 else 0.0
        cv[:, 1] = 1.0 if qd < 3 else 0.0
        cT = np.concatenate([c[b].reshape(8, 128).T, c_ctx.reshape(8, 128).T], axis=1)
        m = dict(shared)
        m.update({
            "xin": xin,
            "ctxin": np.ascontiguousarray(ctx[b]),
            "cT": np.ascontiguousarray(cT),
            "ropeC": np.ascontiguousarray(ropeC.reshape(NE, 128, 64).transpose(1, 0, 2).reshape(128, NE * 64)),
            "ropeS": np.ascontiguousarray(ropeS.reshape(NE, 128, 64).transpose(1, 0, 2).reshape(128, NE * 64)),
            "kbias": kb,
            "cvalid": cv,
        })
        in_maps.append(m)
    return in_maps


_NC_CACHE = {}


def kernel(**inputs):
    in_maps = _host_prep(inputs)
    if "nc" not in _NC_CACHE:
        _NC_CACHE["nc"] = build_nc()
    nc = _NC_CACHE["nc"]
    if PHASES != "MABCD":
        drop = set()
        if "M" not in PHASES:
            drop |= {"cT", "w_mod", "b_modT", "b_mod", "gT", "g_post"}
        if "B" not in PHASES and "D" not in PHASES:
            drop |= {"ffn_w1", "ffn_w3", "ffn_w2"}
        if "C" not in PHASES:
            drop |= {"sc_w_in", "sc_w_out", "cwT", "cvalid"}
        in_maps = [{k: v for k, v in m.items() if k not in drop} for m in in_maps]
    res = run_bass_kernel_spmd(nc, in_maps, core_ids=list(range(NCORES)))
    outs = [np.asarray(r["out"]) for r in res.results]
    full = np.stack([np.concatenate(outs[b * 4:(b + 1) * 4], axis=0) for b in range(2)], axis=0)
    if DEBUG:
        kernel.debug = res.results
    return full.astype(np.float32)
```

```python
import numpy as np
from contextlib import ExitStack
import concourse.bass as bass
import concourse.mybir as mybir
from concourse.bass_utils import run_bass_kernel_spmd

F32 = mybir.dt.float32
BF16 = mybir.dt.bfloat16
AF = mybir.ActivationFunctionType
ALU = mybir.AluOpType
AX = mybir.AxisListType

NCORES = 8
D = 1024
SEQ = 16384
NB = 32
NE = NB + 4
DFF = 2816
NJ = DFF // 128
DEBUG = False
PHASES = "MABCD"
A_LIMIT = None
A_STEP = 99
DBG_E = 6


import re
_PSUM_RE = re.compile(r"^(colps|rowps\d|tpq|tm|bk\d|tp|tpC|gps\d|ups\d|yps\d|ypsC\d|cgp\d|hxp\d|bgp)$")


class _Op:
    __slots__ = ("eng", "fn", "deps", "token", "is_dma", "signal", "seq", "same_sync")


class Sched:
    ENGS = ("pe", "act", "dve", "pool", "sp")

    def __init__(self, nc, es):
        self.nc = nc
        self.es = es
        self.ops = {e: [] for e in self.ENGS}
        self.esem = {e: es.enter_context(nc.semaphore("s_" + e)) for e in ("pe", "act", "dve", "pool")}
        self.csem = {e: es.enter_context(nc.semaphore("c_" + e)) for e in ("act", "dve", "pool")}
        self.ccnt = {e: 0 for e in ("act", "dve", "pool")}
        self.buf = {}
        self.dsem = {}
        self.pending = {e: [] for e in self.ENGS}
        self.dma_since_barrier = []
        self.nops = 0

    def op(self, eng, fn, reads=(), writes=(), dma_key=None, same_sync=False):
        o = _Op()
        o.same_sync = same_sync
        o.eng = eng
        o.fn = fn
        o.is_dma = dma_key is not None
        o.signal = False
        o.token = None
        o.seq = self.nops
        self.nops += 1
        deps = []
        reads = list(reads)
        writes = list(writes)
        for k in list(reads):
            if isinstance(k, str) and _PSUM_RE.match(k):
                reads.remove(k)
                if k not in writes:
                    writes.append(k)
        for k in reads:
            b = self.buf.setdefault(k, [None, []])
            if b[0] is not None:
                deps.append(b[0])
        for k in writes:
            b = self.buf.setdefault(k, [None, []])
            if b[0] is not None:
                deps.append(b[0])
            deps.extend(b[1])
        for k in reads:
            self.buf[k][1].append(o)
        for k in writes:
            b = self.buf[k]
            b[0] = o
            b[1] = []
        deps.extend(self.pending[eng])
        self.pending[eng] = []
        o.deps = [d for d in deps if d is not o]
        self.ops[eng].append(o)
        if o.is_dma:
            if dma_key not in self.dsem:
                self.dsem[dma_key] = [self.es.enter_context(self.nc.semaphore("d_" + str(len(self.dsem)))), 0]
            s = self.dsem[dma_key]
            s[1] += 16
            o.token = (s[0], s[1])
            self.dma_since_barrier.append(o)
        return o

    def barrier(self):
        toks = []
        for e in self.ENGS:
            if self.ops[e]:
                toks.append(self.ops[e][-1])
        toks.extend(self.dma_since_barrier)
        self.dma_since_barrier = []
        for e in self.ENGS:
            self.pending[e].extend(toks)
        self.buf = {}

    @staticmethod
    def _needs_sync(d, o):
        return d.eng != o.eng or d.is_dma or o.is_dma or o.same_sync or o.eng != "pe"

    def finalize(self):
        for e in self.ENGS:
            for o in self.ops[e]:
                for d in o.deps:
                    if self._needs_sync(d, o) and not d.is_dma:
                        d.signal = True
        for e in ("pe", "act", "dve", "pool", "sp"):
            c = 0
            for o in self.ops[e]:
                if not o.is_dma and o.signal:
                    assert e != "sp"
                    c += 1
                    o.token = (self.esem[e], c)

    def emit(self, eng, engine):
        waited = {}
        for o in self.ops[eng]:
            need = {}
            for d in o.deps:
                if self._needs_sync(d, o):
                    sem, v = d.token
                    k = id(sem)
                    if v > need.get(k, (None, 0))[1]:
                        need[k] = (sem, v)
            for k, (sem, v) in need.items():
                if waited.get(k, 0) < v:
                    engine.wait_ge(sem, v)
                    waited[k] = v
            if isinstance(o.fn, (list, tuple)):
                ins = None
                for i, f in enumerate(o.fn):
                    ins = f(engine)
                    if i < len(o.fn) - 1:
                        self.ccnt[eng] += 1
                        ins.then_inc(self.csem[eng], 1)
                        engine.wait_ge(self.csem[eng], self.ccnt[eng])
            else:
                ins = o.fn(engine)
            if o.is_dma:
                ins.then_inc(o.token[0], 16)
            elif o.signal:
                ins.then_inc(o.token[0], 1)

    def final_wait(self, eng, engine, ops):
        for o in ops:
            engine.wait_ge(o.token[0], o.token[1])


def build_nc():
    nc = bass.Bass("TRN2", target_bir_lowering=False)
    dk = "ExternalOutput" if DEBUG else "Internal"

    def din(name, shape):
        return nc.dram_tensor(name, list(shape), F32, kind="ExternalInput").ap()

    xin = din("xin", [NE * 128, D])
    ctxin = din("ctxin", [256, D])
    if "M" in PHASES:
        cT = din("cT", [128, 16])
        w_mod = din("w_mod", [2, D, 6 * D])
        b_modT = din("b_modT", [128, 96])
        b_mod = din("b_mod", [2, 6 * D])
        gT = din("gT", [128, 32])
        g_post = din("g_post", [4, D])
    w_in = din("w_in", [D, 1792])
    w_out = din("w_out", [D, D])
    wsT = din("wsT", [128, 8 * 128])
    bsT = din("bsT", [128, 8])
    vnorm = din("vnorm", [1, 512])
    sink = din("sink", [1, 8])
    ropeC = din("ropeC", [128, NE * 64])
    ropeS = din("ropeS", [128, NE * 64])
    kbias_in = din("kbias", [128, NE + 2])
    trimask_in = din("trimask", [128, 1024])
    ident_in = din("ident", [128, 128])
    if "C" in PHASES:
        cvalid_in = din("cvalid", [128, 2])
    if "B" in PHASES or "D" in PHASES:
        ffn_w1 = din("ffn_w1", [2, D, DFF])
        ffn_w3 = din("ffn_w3", [2, D, DFF])
        ffn_w2 = din("ffn_w2", [2, DFF, D])
    if "C" in PHASES:
        sc_w_in = din("sc_w_in", [D, 3 * D])
        sc_w_out = din("sc_w_out", [D, D])
        cwT = din("cwT", [128, 24])
    out = nc.dram_tensor("out", [NB * 128, D], F32, kind="ExternalOutput").ap()
    x1 = nc.dram_tensor("x1", [(NB + 2) * 128, D], F32, kind=dk).ap()
    x2 = nc.dram_tensor("x2", [(NB + 2) * 128, D], F32, kind=dk).ap()
    x3 = nc.dram_tensor("x3", [NB * 128, D], F32, kind=dk).ap()
    gtgscr = nc.dram_tensor("gtgscr", [4 * 128, D], F32, kind=dk).ap()
    dbg_modc = nc.dram_tensor("dbg_modc", [128, 128], F32, kind=dk).ap()
    dbgt = {}
    if DEBUG:
        for nm, shp, dt_ in [("d_hT", [128, 1024], BF16), ("d_qks", [128, 640], F32), ("d_qr", [128, 640], BF16),
                             ("d_u", [128, 512], F32), ("d_vn", [128, 512], BF16), ("d_mix", [128, 1024], BF16),
                             ("d_qT", [128, 512], BF16), ("d_pT", [128, 512], BF16), ]:
            dbgt[nm] = nc.dram_tensor(nm, shp, dt_, kind="ExternalOutput").ap()

    with ExitStack() as es:
        S = Sched(nc, es)

        def sb(stack, name, shape, dt):
            return stack.enter_context(nc.sbuf_tensor("sb_" + name, list(shape), dt))

        def ps(stack, name, shape, dt=F32):
            return stack.enter_context(nc.psum_tensor("ps_" + name, list(shape), dt))

        def dma(eng, out_ap, in_ap, key, reads=(), writes=()):
            return S.op(eng, lambda e: e.dma_start(out=out_ap, in_=in_ap), reads=reads, writes=writes, dma_key=key)

        def dbgdump(name, ap2d, key):
            if DEBUG:
                dma("sp", dbgt[name], ap2d, "dbg_" + name, reads=[key])

        ident = sb(es, "ident", [128, 128], BF16)
        modc = sb(es, "modc", [128, 2, 2, 2, 8, 2], F32)
        consts = sb(es, "consts", [128, 8], F32)
        mh8 = sb(es, "mh8", [128, 8], F32)
        epsl8 = sb(es, "epsl8", [128, 8], F32)

        dma("pool", ident[:], ident_in, "ident", writes=["ident"])
        S.op("pool", lambda e: e.memset(consts[:, 0:1], 1e-6), writes=["consts"])
        S.op("pool", lambda e: e.memset(consts[:, 1:2], -0.5), writes=["consts"])
        S.op("pool", lambda e: e.memset(mh8[:], -0.5), writes=["mh8"])
        S.op("pool", lambda e: e.memset(epsl8[:], 1e-5), writes=["epsl8"])

        def rstd_ops(ss, ncols, rs, key_ss, key_rs):
            if ncols == 2:
                f = [lambda e: e.tensor_tensor(out=rs, in0=ss[:, 0:1], in1=ss[:, 1:2], op=ALU.add),
                     lambda e: e.tensor_tensor(out=rs, in0=rs, in1=consts[:, 0:1], op=ALU.add),
                     lambda e: e.tensor_tensor(out=rs, in0=rs, in1=consts[:, 1:2], op=ALU.pow)]
            else:
                f = [lambda e: e.tensor_tensor(out=rs, in0=ss[:, 0:1], in1=consts[:, 0:1], op=ALU.add),
                     lambda e: e.tensor_tensor(out=rs, in0=rs, in1=consts[:, 1:2], op=ALU.pow)]
            S.op("pool", f, reads=[key_ss, "consts"], writes=[key_rs])

        with ExitStack() as pm:
          if "M" in PHASES:
            cTs = sb(pm, "cTs", [128, 16], F32)
            sil = sb(pm, "sil", [128, 16], F32)
            srep = sb(pm, "srep", [128, 8, 128], F32)
            svec = sb(pm, "svec", [128, 8, 2], F32)
            bmT = sb(pm, "bmT", [128, 2, 48], F32)
            gTs = sb(pm, "gTs", [128, 2, 2, 8], F32)
            modT = sb(pm, "modT", [128, 2, 4, 8, 2], F32)
            wm = [sb(pm, "wm%d" % i, [128, 8, 1024], F32) for i in range(3)]
            brow = [sb(pm, "brow%d" % i, [128, 1024], F32) for i in range(2)]
            grow = [sb(pm, "grow%d" % i, [128, 1024], F32) for i in range(2)]
            gtg = [sb(pm, "gtgm%d" % i, [128, 1024], F32) for i in range(2)]
            colps = ps(pm, "colps", [128, 8, 2])
            rowps = [ps(pm, "rowps%d" % i, [128, 512]) for i in range(2)]

            dma("sp", cTs[:], cT, "cTs", writes=["cTs"])
            dma("sp", bmT[:].rearrange("p a b -> p (a b)"), b_modT, "bmT", writes=["bmT"])
            dma("sp", gTs[:].rearrange("p a b c -> p (a b c)"), gT, "gTs", writes=["gTs"])
            S.op("act", lambda e: e.activation(out=sil[:], in_=cTs[:], func=AF.Silu), reads=["cTs"], writes=["sil"])
            S.op("dve", lambda e: e.tensor_copy(out=srep[:], in_=sil[:, 0:8].unsqueeze(2).to_broadcast([128, 8, 128])),
                 reads=["sil"], writes=["srep"])

            def f_svec(e):
                e.tensor_copy(out=svec[:, :, 0], in_=sil[:, 0:8])
                return e.tensor_copy(out=svec[:, :, 1], in_=sil[:, 8:16])
            S.op("dve", f_svec, reads=["sil"], writes=["svec"])

            pi = 0
            ri = 0
            pieces = [(i, pc) for i in range(2) for pc in range(6)]

            def load_piece(q):
                if q < len(pieces):
                    i_, pc_ = pieces[q]
                    dma("sp", wm[q % 3][:], w_mod[i_, :, pc_ * 1024:(pc_ + 1) * 1024].rearrange("(k p) n -> p k n", p=128),
                        "wm%d" % (q % 3), writes=["wm%d" % (q % 3)])
            load_piece(0)
            load_piece(1)
            for i in range(2):
                for pc in range(6):
                    wmb = wm[pi % 3]
                    wk = "wm%d" % (pi % 3)
                    load_piece(pi + 2)
                    pi += 1
                    if pc in (0, 1, 3, 4):
                        kind = {0: 0, 1: 1, 3: 2, 4: 3}[pc]

                        def f_col(e, wmb=wmb):
                            ins = None
                            for oc in range(8):
                                for k in range(8):
                                    ins = e.matmul(colps[:, oc, :], lhsT=wmb[:, k, oc * 128:(oc + 1) * 128],
                                                   rhs=svec[:, k, :], start=(k == 0), stop=(k == 7))
                            return ins
                        S.op("pe", f_col, reads=[wk, "svec"], writes=["colps"])
                        S.op("dve", lambda e, i=i, kind=kind, pc=pc: e.tensor_tensor(
                            out=modT[:, i, kind], in0=colps[:],
                            in1=bmT[:, i, pc * 8:(pc + 1) * 8].unsqueeze(2).to_broadcast([128, 8, 2]), op=ALU.add),
                            reads=["colps", "bmT"], writes=["modT"])
                    else:
                        which = 0 if pc == 2 else 1
                        r = ri % 2
                        ri += 1
                        dma("sp", brow[r][:], b_mod[i:i + 1, pc * 1024:(pc + 1) * 1024].partition_broadcast(128),
                            "brow%d" % r, writes=["brow%d" % r])
                        dma("sp", grow[r][:], g_post[2 * i + which:2 * i + which + 1, :].partition_broadcast(128),
                            "grow%d" % r, writes=["grow%d" % r])
                        for hf in range(2):
                            def f_row(e, wmb=wmb, hf=hf):
                                ins = None
                                for k in range(8):
                                    ins = e.matmul(rowps[hf][:], lhsT=srep[:, k, :],
                                                   rhs=wmb[:, k, hf * 512:(hf + 1) * 512], start=(k == 0), stop=(k == 7))
                                return ins
                            S.op("pe", f_row, reads=[wk, "srep"], writes=["rowps%d" % hf])

                            sl_ = slice(hf * 512, (hf + 1) * 512)
                            f_rowev = [lambda e, r=r, hf=hf, sl=sl_: e.tensor_tensor(out=gtg[r][:, sl], in0=rowps[hf][:], in1=brow[r][:, sl], op=ALU.add),
                                       lambda e, r=r, hf=hf, sl=sl_: e.tensor_tensor(out=gtg[r][:, sl], in0=gtg[r][:, sl], in1=grow[r][:, sl], op=ALU.mult)]
                            S.op("dve", f_rowev, reads=["rowps%d" % hf, "brow%d" % r, "grow%d" % r], writes=["gtg%d" % r])
                        gi = 2 * i + which
                        dma("sp", gtgscr[gi * 128:(gi + 1) * 128, :], gtg[r][:], "gtg%d" % r,
                            reads=["gtg%d" % r], writes=[("gtgscr", gi)])

            def f_modc(e):
                ins = None
                for i in range(2):
                    for kd in range(2):
                        e.scalar_tensor_tensor(out=modc[:, i, kd, 0], in0=modT[:, i, 2 * kd + 1], scalar=1.0,
                                               in1=gTs[:, i, kd].unsqueeze(2).to_broadcast([128, 8, 2]),
                                               op0=ALU.add, op1=ALU.mult)
                        ins = e.tensor_copy(out=modc[:, i, kd, 1], in_=modT[:, i, 2 * kd])
                return ins
            S.op("dve", f_modc, reads=["modT", "gTs"], writes=["modc"])
            if DEBUG:
                dma("sp", dbg_modc, modc[:].rearrange("p a b c d e -> p (a b c d e)"), "dbgmodc", reads=["modc"])
            S.barrier()

        def prenorm(xb, xkey, ms, mskey, rs, rskey, xn, xnkey, tp, tpkey, hT_dst, hkey, li, kd, vec, scr, scrkey, part="ab"):
            if "a" in part:
                S.op("act", lambda e: e.activation(out=scr[:], in_=xb[:], func=AF.Square, scale=1.0 / 32.0, accum_out=ms),
                     reads=[xkey], writes=[scrkey, mskey])
                rstd_ops(ms, 1, rs, mskey, rskey)
                S.op("dve", lambda e: e.tensor_scalar(out=xn[:], in0=xb[:], scalar1=rs, scalar2=None, op0=ALU.mult),
                     reads=[xkey, rskey], writes=[xnkey])
            if "b" not in part:
                return

            def f_tr(e):
                ins = None
                for k in range(8):
                    ins = e.transpose(tp[:, k, :], xn[:, k * 128:(k + 1) * 128], ident[:])
                return ins
            S.op("pe", f_tr, reads=[xnkey, "ident"], writes=[tpkey])
            if A_STEP < 6:
                return

            def f_mod(e):
                ins = None
                for k in range(8):
                    ins = e.activation(out=hT_dst(k), in_=tp[:, k, :], func=AF.Identity,
                                       scale=modc[:, li, kd, 0, k, vec:vec + 1], bias=modc[:, li, kd, 1, k, vec:vec + 1])
                return ins
            S.op("act", f_mod, reads=[tpkey, "modc"], writes=[hkey])

        def postnorm(yps, ykeys, ss2, sskey, rs, rskey, gtgt, xb, xkey, tmp, tmpkey, scr, scrkey, dst_ap, dstkey):
            def f_sq(e):
                e.activation(out=scr[:, 0:512], in_=yps[0][:], func=AF.Square, scale=1.0 / 32.0, accum_out=ss2[:, 0:1])
                return e.activation(out=scr[:, 512:1024], in_=yps[1][:], func=AF.Square, scale=1.0 / 32.0,
                                    accum_out=ss2[:, 1:2])
            S.op("act", f_sq, reads=list(ykeys), writes=[scrkey, sskey])
            rstd_ops(ss2, 2, rs, sskey, rskey)

            def f_pn1(e):
                ins = None
                for hf in range(2):
                    sl = slice(hf * 512, (hf + 1) * 512)
                    ins = e.scalar_tensor_tensor(out=tmp[:, sl], in0=yps[hf][:], scalar=rs, in1=gtgt[:, sl],
                                                 op0=ALU.mult, op1=ALU.mult)
                return ins
            f_pn = [f_pn1, lambda e: e.tensor_tensor(out=tmp[:], in0=tmp[:], in1=xb[:], op=ALU.add)]
            S.op("dve", f_pn, reads=list(ykeys) + [rskey, "gtgt", xkey], writes=[tmpkey])
            dma("sp", dst_ap, tmp[:], tmpkey, reads=[tmpkey], writes=[dstkey])

        with ExitStack() as pa:
          if "A" in PHASES:
            w_in_b = sb(pa, "w_in_b", [128, 8, 1792], BF16)
            w_out_b = sb(pa, "w_out_b", [128, 8, 1024], BF16)
            wsT_b = sb(pa, "wsT_b", [128, 8, 128], BF16)
            bsT_s = sb(pa, "bsT_s", [128, 8], F32)
            vnorm_b = sb(pa, "vnorm_b", [128, 512], F32)
            sinkexp = sb(pa, "sinkexp", [128, 8], F32)
            rC = sb(pa, "rC", [128, NE, 64], F32)
            rS = sb(pa, "rS", [128, NE, 64], F32)
            kbias = sb(pa, "kbias_s", [128, NE + 2], F32)
            trim = sb(pa, "trim", [128, 2, 512], BF16)
            gtgt = sb(pa, "gtgt", [128, 1024], F32)
            kTc = sb(pa, "kTc", [128, NE + 2, 128], BF16)
            vc = sb(pa, "vc", [128, NE + 2, 2, 65], BF16)
            NXR = 5
            xr = [sb(pa, "xr%d" % i, [128, 1024], F32) for i in range(NXR)]
            msr = sb(pa, "msr", [128, 8], F32)
            rsr = sb(pa, "rsr", [128, 8], F32)
            scr = sb(pa, "scrA", [128, 1024], F32)
            xn = [sb(pa, "xn%d" % i, [128, 1024], BF16) for i in range(2)]
            hT = [sb(pa, "hT%d" % i, [128, 8, 128], BF16) for i in range(2)]
            ccx = sb(pa, "ccx", [128, 8, 64], F32)
            ssx = sb(pa, "ssx", [128, 8, 64], F32)
            qks = sb(pa, "qks", [128, 640], F32)
            bsx = sb(pa, "bsx", [128, 8, 64], F32)
            rt1 = sb(pa, "rt1", [128, 640], F32)
            rt2 = sb(pa, "rt2", [128, 640], F32)
            qr = sb(pa, "qr", [128, 640], BF16)
            qT = [sb(pa, "qT%d" % i, [128, 4, 128], BF16) for i in range(2)]
            u_sb = [sb(pa, "u_sb%d" % i, [128, 512], F32) for i in range(2)]
            cen = sb(pa, "cen", [128, 512], F32)
            sq = sb(pa, "sq", [128, 512], F32)
            st8 = sb(pa, "st8", [128, 4, 8], F32)
            vn = [sb(pa, "vn%d" % i, [128, 512], BF16) for i in range(2)]
            pT = [sb(pa, "pT%d" % i, [128, 512], BF16) for i in range(3)]
            den = sb(pa, "den", [128, 2, 8], F32)
            mix = sb(pa, "mix", [128, 1024], BF16)
            gtmp = sb(pa, "gtmp", [128, 512], F32)
            mixT = sb(pa, "mixT", [128, 8, 128], BF16)
            tmpo = [sb(pa, "tmpo%d" % i, [128, 1024], F32) for i in range(2)]
            ss2 = sb(pa, "ss2", [128, 2], F32)
            rs2 = sb(pa, "rs2", [128, 1], F32)
            tpq = ps(pa, "tpq", [128, 8, 128], BF16)
            tm = tpq
            bk = [ps(pa, "bkA%d" % i, [128, 512]) for i in range(7)]
            pj0, pj1, pj2, pj3, spsb, ops0b, spsb2 = bk
            spsl = [spsb, spsb2]
            spskey = ["bk4", "bk6"]
            yps = [pj0, pj1]
            gps = pj2
            ops = [ops0b[:, 0:260].rearrange("p (h d) -> p h d", d=65), pj3[:, 0:260].rearrange("p (h d) -> p h d", d=65)]
            opskey = ["bk5", "bk3"]

            dma("pool", w_in_b[:], w_in.rearrange("(k p) n -> p k n", p=128), "w_in_b", writes=["w_in_b"])
            dma("pool", wsT_b[:].rearrange("p g i -> p (g i)"), wsT, "wsT_b", writes=["wsT_b"])
            dma("pool", trim[:].rearrange("p a n -> p (a n)"), trimask_in, "trim", writes=["trim"])
            dma("pool", w_out_b[:], w_out.rearrange("(k p) n -> p k n", p=128), "w_out_b", writes=["w_out_b"])
            dma("sp", bsT_s[:], bsT, "bsT_s", writes=["bsT_s"])
            dma("sp", vnorm_b[:], vnorm.partition_broadcast(128), "vnorm_b", writes=["vnorm_b"])
            dma("sp", sinkexp[:], sink.partition_broadcast(128), "sinkexp", writes=["sinkexp"])
            dma("sp", rC[:].rearrange("p e f -> p (e f)"), ropeC, "rC", writes=["rC"])
            dma("sp", rS[:].rearrange("p e f -> p (e f)"), ropeS, "rS", writes=["rS"])
            dma("sp", kbias[:], kbias_in, "kbias", writes=["kbias"])
            dma("sp", gtgt[:], gtgscr[0:128, :], "gtgt", reads=[("gtgscr", 0)], writes=["gtgt"])
            S.op("act", lambda e: e.activation(out=sinkexp[:], in_=sinkexp[:], func=AF.Exp), reads=["sinkexp"], writes=["sinkexp"])
            S.op("dve", lambda e: e.memset(vc[:].rearrange("p e k d -> p (e k d)"), 1.0), writes=["vc_init"])
            S.op("pool", lambda e: e.tensor_copy(out=bsx[:], in_=bsT_s[:].unsqueeze(2).to_broadcast([128, 8, 64])),
                 reads=["bsT_s"], writes=["bsx"])

            def proj_stage(e_idx, n):
                is_ctx = e_idx >= NE
                full = (not is_ctx) and (1 <= e_idx <= NE - 2)
                xb = xr[n % NXR]
                xk = "xr%d" % (n % NXR)
                c8 = n % 8
                h = hT[n % 2]
                hk = "hT%d" % (n % 2)
                prenorm(xb, xk, msr[:, c8:c8 + 1], ("ms", c8), rsr[:, c8:c8 + 1], ("rs", c8), xn[n % 2], "xn%d" % (n % 2),
                        tpq, "tpq", lambda k: h[:, k, :], hk, 0, 0, 1 if is_ctx else 0, scr, "scrA")
                if e_idx == DBG_E:
                    dbgdump("d_hT", h[:].rearrange("p k t -> p (k t)"), hk)
                if A_STEP < 7:
                    return
                groups = [(3, 1024 + 512, 256)]
                if full:
                    groups = [(0, 0, 512), (1, 512, 512), (2, 1024, 512), (3, 1536, 256)]

                def f_proj(e):
                    ins = None
                    for (b, c0, w) in groups:
                        for k in range(8):
                            ins = e.matmul(bk[b][:, 0:w], lhsT=h[:, k, :], rhs=w_in_b[:, k, c0:c0 + w],
                                           start=(k == 0), stop=(k == 7))
                    return ins
                S.op("pe", f_proj, reads=[hk, "w_in_b"], writes=["bk%d" % g[0] for g in groups])
                if A_STEP < 8:
                    return
                S.op("act", lambda e: e.activation(out=vc[:, e_idx, :, 0:64],
                                                   in_=pj3[:, 128:256].rearrange("p (k d) -> p k d", d=64), func=AF.Copy),
                     reads=["bk3", "vc_init"], writes=[("vc", e_idx)])
                if A_STEP < 9:
                    return
                if is_ctx:
                    S.op("dve", lambda e: e.tensor_copy(out=qr[:, 512:640], in_=pj3[:, 0:128]), reads=["bk3"], writes=["qr_k"])
                else:
                    def f_exp(e):
                        e.tensor_copy(out=ccx[:], in_=rC[:, e_idx, :].unsqueeze(1).to_broadcast([128, 8, 64]))
                        return e.tensor_copy(out=ssx[:], in_=rS[:, e_idx, :].unsqueeze(1).to_broadcast([128, 8, 64]))
                    S.op("pool", f_exp, reads=["rC", "rS"], writes=["ccx"])

                    def f_cp(e):
                        ins = e.activation(out=qks[:, 512:640], in_=pj3[:, 0:128], func=AF.Copy)
                        if full:
                            ins = e.activation(out=qks[:, 0:512], in_=pj0[:], func=AF.Copy)
                        return ins
                    S.op("act", f_cp, reads=["bk3"] + (["bk0"] if full else []), writes=["qks"])

                    segs = [(512, 640, 2)] + ([(0, 512, 8)] if full else [])

                    def f_rope1(e):
                        ins = None
                        for (c0, c1, nh) in segs:
                            e.tensor_tensor(out=rt1[:, c0:c1], in0=qks[:, c0:c1],
                                            in1=ccx[:, 0:nh, :].rearrange("p h f -> p (h f)"), op=ALU.mult)
                            s5 = qks[:, c0:c1].rearrange("p (h a b f) -> p h a b f", a=2, b=2, f=16)
                            t5 = rt2[:, c0:c1].rearrange("p (h a b f) -> p h a b f", a=2, b=2, f=16)
                            x5 = ssx[:, 0:nh, :].rearrange("p h (a b f) -> p h a b f", a=2, b=2, f=16)
                            for ab in range(2):
                                ins = e.tensor_tensor(out=t5[:, :, :, ab, :], in0=s5[:, :, :, 1 - ab, :], in1=x5[:, :, :, ab, :],
                                                      op=ALU.mult)
                        return ins

                    def f_rope2(e):
                        ins = None
                        for (c0, c1, nh) in segs:
                            ins = e.tensor_tensor(out=qr[:, c0:c1], in0=rt1[:, c0:c1], in1=rt2[:, c0:c1], op=ALU.add)
                        return ins
                    f_rope = [f_rope1, f_rope2]
                    S.op("dve", f_rope, reads=["qks", "ccx"], writes=["qr_k", "qr_q", "rt_k"])
                    if e_idx == DBG_E:
                        dbgdump("d_qks", qks[:], "qks")
                        dbgdump("d_qr", qr[:], "qr_k")
                if A_STEP < 10:
                    return
                nq = 4 if full else 0

                def f_trq(e):
                    ins = None
                    for s_ in range(nq):
                        ins = e.transpose(tpq[:, s_, :], qr[:, s_ * 128:(s_ + 1) * 128], ident[:])
                    return e.transpose(tpq[:, 4, :], qr[:, 512:640], ident[:])
                S.op("pe", f_trq, reads=["qr_k", "ident"] + (["qr_q"] if full else []), writes=["tpq"])
                if A_STEP < 11:
                    return
                S.op("act", lambda e: e.activation(out=kTc[:, e_idx, :], in_=tpq[:, 4, :], func=AF.Copy),
                     reads=["tpq"], writes=[("kT", e_idx)])
                if full:
                    qTt = qT[e_idx % 2]
                    S.op("act", lambda e: e.activation(out=qTt[:], in_=tpq[:, 0:4, :], func=AF.Copy),
                         reads=["tpq"], writes=["qT%d" % (e_idx % 2)])
                    us = u_sb[e_idx % 2]
                    S.op("act", lambda e: e.activation(out=us[:], in_=pj1[:], func=AF.Copy), reads=["bk1"],
                         writes=["u%d" % (e_idx % 2)])
                    vnt = vn[e_idx % 2]

                    S.op("act", lambda e: e.activation(out=sq[:], in_=pj2[:], func=AF.Square), reads=["bk2"], writes=["sq"])
                    S.op("act", lambda e: e.activation(out=cen[:], in_=pj2[:], func=AF.Copy), reads=["bk2"], writes=["cen"])

                    def f_ln1a(e):
                        e.tensor_reduce(out=st8[:, 0, :], in_=cen[:].rearrange("p (g d) -> p g d", d=64), axis=AX.X, op=ALU.add)
                        return e.tensor_reduce(out=st8[:, 2, :], in_=sq[:].rearrange("p (g d) -> p g d", d=64), axis=AX.X, op=ALU.add)

                    def f_ln1b(e):
                        e.tensor_scalar(out=st8[:, 0, :], in0=st8[:, 0, :], scalar1=1.0 / 64.0, scalar2=None, op0=ALU.mult)
                        return e.tensor_scalar(out=st8[:, 2, :], in0=st8[:, 2, :], scalar1=1.0 / 64.0, scalar2=None, op0=ALU.mult)
                    f_ln1 = [f_ln1a, f_ln1b,
                             lambda e: e.tensor_tensor(out=st8[:, 1, :], in0=st8[:, 0, :], in1=st8[:, 0, :], op=ALU.mult),
                             lambda e: e.tensor_tensor(out=st8[:, 2, :], in0=st8[:, 2, :], in1=st8[:, 1, :], op=ALU.subtract)]
                    S.op("dve", f_ln1, reads=["cen", "sq"], writes=["st8v", "st8n"])

                    f_ln3 = [lambda e: e.tensor_tensor(out=st8[:, 3, :], in0=st8[:, 2, :], in1=epsl8[:], op=ALU.add),
                             lambda e: e.tensor_tensor(out=st8[:, 3, :], in0=st8[:, 3, :], in1=mh8[:], op=ALU.pow)]
                    S.op("pool", f_ln3, reads=["st8v", "epsl8", "mh8"], writes=["st8r"])

                    S.op("dve", lambda e: e.scalar_tensor_tensor(out=st8[:, 1, :], in0=st8[:, 0, :], scalar=-1.0, in1=st8[:, 3, :],
                                                                   op0=ALU.mult, op1=ALU.mult),
                         reads=["st8r", "st8v"], writes=["st8n"])

                    def f_ln4a(e):
                        ins = None
                        for g in range(8):
                            ins = e.tensor_scalar(out=cen[:, g * 64:(g + 1) * 64], in0=cen[:, g * 64:(g + 1) * 64],
                                                  scalar1=st8[:, 3, g:g + 1], scalar2=st8[:, 1, g:g + 1], op0=ALU.mult, op1=ALU.add)
                        return ins
                    f_ln4 = [f_ln4a, lambda e: e.tensor_tensor(out=vnt[:], in0=cen[:], in1=vnorm_b[:], op=ALU.mult)]
                    S.op("dve", f_ln4, reads=["cen", "st8r", "st8n", "vnorm_b"], writes=["vn%d" % (e_idx % 2), "cen"], same_sync=True)
                    if e_idx == DBG_E:
                        dbgdump("d_u", us[:], "u%d" % (e_idx % 2))
                        dbgdump("d_vn", vnt[:], "vn%d" % (e_idx % 2))
                        dbgdump("d_qT", qTt[:].rearrange("p s t -> p (s t)"), "qT%d" % (e_idx % 2))

            def attn_stage(e_idx):
                qTt = qT[e_idx % 2]
                qk = "qT%d" % (e_idx % 2)
                us = u_sb[e_idx % 2]
                vnt = vn[e_idx % 2]
                items = [(kv, ci, kb) for kv in range(2) for ci, kb in enumerate([e_idx - 1, e_idx, e_idx + 1, NE, NE + 1])]

                def emit_qk(i):
                    kv, ci, kb = items[i]
                    sp_ = spsl[i % 2]

                    def f_qk(e):
                        ins = e.matmul(sp_[:], lhsT=kTc[kv * 64:(kv + 1) * 64, kb, :],
                                       rhs=qTt[kv * 64:(kv + 1) * 64, :, :].rearrange("p s t -> p (s t)"),
                                       start=True, stop=(ci not in (0, 2)))
                        if ci in (0, 2):
                            ins = e.matmul(sp_[:], lhsT=ident[:], rhs=trim[:, ci // 2, :], start=False, stop=True)
                        return ins
                    S.op("pe", f_qk, reads=[("kT", kb), qk, "ident", "trim"], writes=[spskey[i % 2]])
                    pt = pT[i % 3]
                    S.op("act", lambda e: e.activation(out=pt[:], in_=sp_[:], func=AF.Exp, scale=0.125, bias=kbias[:, kb:kb + 1]),
                         reads=[spskey[i % 2], "kbias"], writes=["pT%d" % (i % 3)])
                    if e_idx == DBG_E and i == 0:
                        dbgdump("d_pT", pt[:], "pT%d" % (i % 3))

                def emit_pv(i):
                    kv, ci, kb = items[i]
                    pt = pT[i % 3]

                    def f_pv(e):
                        ins = None
                        for hh in range(4):
                            ins = e.matmul(ops[kv][:, hh, :], lhsT=pt[:, hh * 128:(hh + 1) * 128], rhs=vc[:, kb, kv, :],
                                           start=(ci == 0 and hh == 0), stop=(ci == 4), skip_group_check=True)
                        return ins
                    S.op("pe", f_pv, reads=["pT%d" % (i % 3), ("vc", kb), "vc_init"], writes=[opskey[kv]])

                emit_qk(0)
                for i in range(len(items)):
                    if i + 1 < len(items):
                        emit_qk(i + 1)
                    emit_pv(i)
                for kv in range(2):
                    f_den = [lambda e, kv=kv: e.tensor_tensor(out=den[:, 0, kv * 4:(kv + 1) * 4], in0=ops[kv][:, :, 64],
                                                               in1=sinkexp[:, kv * 4:(kv + 1) * 4], op=ALU.add),
                             lambda e, kv=kv: e.reciprocal(out=den[:, 1, kv * 4:(kv + 1) * 4], in_=den[:, 0, kv * 4:(kv + 1) * 4])]
                    S.op("dve", f_den, reads=[opskey[kv], "sinkexp"], writes=[("den", kv)])

                    def f_norm(e, kv=kv):
                        ins = None
                        for hh in range(4):
                            c0 = kv * 256 + hh * 64
                            ins = e.tensor_scalar(out=mix[:, c0:c0 + 64], in0=ops[kv][:, hh, 0:64],
                                                  scalar1=den[:, 1, kv * 4 + hh:kv * 4 + hh + 1], scalar2=None, op0=ALU.mult)
                        return ins
                    S.op("dve", f_norm, reads=[opskey[kv], ("den", kv)], writes=["mix_a%d" % kv], same_sync=True)

                def f_gate(e):
                    ins = None
                    for g in range(8):
                        ins = e.matmul(gps[:, g * 64:(g + 1) * 64], lhsT=wsT_b[:, g, :], rhs=vnt[:, g * 64:(g + 1) * 64],
                                       start=True, stop=True)
                    return ins
                S.op("pe", f_gate, reads=["wsT_b", "vn%d" % (e_idx % 2)], writes=["bk2"])

                f_gev = [lambda e: e.tensor_tensor(out=gtmp[:], in0=gps[:], in1=bsx[:].rearrange("p g d -> p (g d)"), op=ALU.add),
                         lambda e: e.tensor_tensor(out=mix[:, 512:1024], in0=gtmp[:], in1=us[:], op=ALU.mult)]
                S.op("dve", f_gev, reads=["bk2", "bsx", "u%d" % (e_idx % 2)], writes=["mix_g", "gtmp"])

                def f_trm(e):
                    ins = None
                    for c in range(8):
                        ins = e.transpose(tm[:, c, :], mix[:, c * 128:(c + 1) * 128], ident[:])
                    return ins
                if e_idx == DBG_E:
                    dbgdump("d_mix", mix[:], "mix_g")
                S.op("pe", f_trm, reads=["mix_a0", "mix_a1", "mix_g", "ident"], writes=["tpq"])
                S.op("act", lambda e: e.activation(out=mixT[:], in_=tm[:], func=AF.Copy), reads=["tpq"], writes=["mixT"])
                for hf in range(2):
                    def f_y(e, hf=hf):
                        ins = None
                        for c in range(8):
                            ins = e.matmul(yps[hf][:], lhsT=mixT[:, c, :], rhs=w_out_b[:, c, hf * 512:(hf + 1) * 512],
                                           start=(c == 0), stop=(c == 7))
                        return ins
                    S.op("pe", f_y, reads=["mixT", "w_out_b"], writes=["bk%d" % hf])
                n = stage_of[e_idx]
                to = tmpo[e_idx % 2]
                postnorm(yps, ["bk0", "bk1"], ss2, "ss2", rs2[:, 0:1], "rs2", gtgt, xr[n % NXR], "xr%d" % (n % NXR), to,
                         "tmpo%d" % (e_idx % 2), scr, "scrA", x1[(e_idx - 1) * 128:e_idx * 128, :], ("x1", e_idx))

            stage_of = {}
            n = 0
            order = [NE, NE + 1] + list(range(NE))
            done_attn = 0
            if A_LIMIT is not None:
                order = order[:A_LIMIT[0]]
            def load_x(i):
                if i < len(order):
                    ei = order[i]
                    src = ctxin[(ei - NE) * 128:(ei - NE + 1) * 128, :] if ei >= NE else xin[ei * 128:(ei + 1) * 128, :]
                    dma("sp", xr[i % NXR][:], src, "xr%d" % (i % NXR), writes=["xr%d" % (i % NXR)])
            load_x(0)
            load_x(1)
            for e_idx in order:
                load_x(n + 2)
                stage_of[e_idx] = n
                proj_stage(e_idx, n)
                n += 1
                if e_idx < NE and e_idx >= 2 and (A_LIMIT is None or A_LIMIT[1]):
                    attn_stage(e_idx - 1)
            S.barrier()

        def ffn_phase(li, src, dst, nblk, gi, tagp):
            TT = 256
            ntile = nblk // 2
            with ExitStack() as pf:
                w1b = sb(pf, tagp + "w1b", [128, 8, DFF], BF16)
                w3b = sb(pf, tagp + "w3b", [128, 8, DFF], BF16)
                w2b = sb(pf, tagp + "w2b", [128, NJ, 1024], BF16)
                gtgt = sb(pf, tagp + "gtgt", [128, 1024], F32)
                NXR = 6
                xr = [sb(pf, tagp + "xr%d" % i, [128, 1024], F32) for i in range(NXR)]
                msr = sb(pf, tagp + "msr", [128, 8], F32)
                rsr = sb(pf, tagp + "rsr", [128, 8], F32)
                scr = sb(pf, tagp + "scr", [128, 1024], F32)
                xn = [sb(pf, tagp + "xn%d" % i, [128, 1024], BF16) for i in range(4)]
                hT = [sb(pf, tagp + "hT%d" % i, [128, 8, TT], BF16) for i in range(2)]
                sg = [sb(pf, tagp + "sg%d" % i, [128, TT], F32) for i in range(2)]
                act = sb(pf, tagp + "act", [128, NJ, TT], BF16)
                tmpo = [sb(pf, tagp + "tmpo%d" % i, [128, 1024], F32) for i in range(2)]
                ss2 = sb(pf, tagp + "ss2", [128, 2], F32)
                rs2 = sb(pf, tagp + "rs2", [128, 1], F32)
                tp = ps(pf, tagp + "tp", [128, 8, 128], BF16)
                gpsb = [ps(pf, tagp + "gps%d" % i, [128, 512]) for i in range(2)]
                upsb = [ps(pf, tagp + "ups%d" % i, [128, 512]) for i in range(2)]
                ypsb = [ps(pf, tagp + "yps%d" % i, [128, 512]) for i in range(3)]

                for k in range(8):
                    dma("pool", w1b[:, k, :], ffn_w1[li, k * 128:(k + 1) * 128, :], "w1b%d" % k, writes=[("w1b", k)])
                    dma("pool", w3b[:, k, :], ffn_w3[li, k * 128:(k + 1) * 128, :], "w3b%d" % k, writes=[("w3b", k)])
                for j0 in range(0, NJ, 2):
                    dma("pool", w2b[:, j0:j0 + 2, :], ffn_w2[li, j0 * 128:(j0 + 2) * 128, :].rearrange("(c p) n -> p c n", p=128),
                        "w2b%d" % j0, writes=[("w2b", j0)])
                dma("sp", gtgt[:], gtgscr[gi * 128:(gi + 1) * 128, :], "gtgt", reads=[("gtgscr", gi)], writes=["gtgt"])

                def pre(t, part="lab", bls=(0, 1)):
                    for bl in bls:
                        b = 2 * t + bl
                        xb = xr[b % NXR]
                        xk = "xr%d" % (b % NXR)
                        if "l" in part:
                            dma("sp", xb[:], src[b * 128:(b + 1) * 128, :], xk, reads=[("src", b)], writes=[xk])
                        h = hT[t % 2]
                        c8 = b % 8
                        prenorm(xb, xk, msr[:, c8:c8 + 1], ("ms", c8), rsr[:, c8:c8 + 1], ("rs", c8), xn[b % 4], "xn%d" % (b % 4),
                                tp, "tp", lambda k, h=h, bl=bl: h[:, k, bl * 128:(bl + 1) * 128], ("hT", t % 2, bl), li, 1, 0,
                                scr, "scr", part=part)

                yrot = 0
                pre(0)
                if ntile > 1:
                    pre(1, "l")
                for t in range(ntile):
                    if t + 2 < ntile:
                        pre(t + 2, "l")
                    if t + 1 < ntile:
                        pre(t + 1, "a")
                    h = hT[t % 2]
                    hkeys = [("hT", t % 2, 0), ("hT", t % 2, 1)]
                    for j in range(NJ):
                        if j == 7 and t + 1 < ntile:
                            pre(t + 1, "b", bls=(0,))
                        if j == 15 and t + 1 < ntile:
                            pre(t + 1, "b", bls=(1,))
                        g_ = gpsb[j % 2]
                        u_ = upsb[j % 2]

                        def f_gu(e, j=j, g_=g_, u_=u_, h=h):
                            ins = None
                            for k in range(8):
                                ins = e.matmul(g_[:, 0:TT], lhsT=w1b[:, k, j * 128:(j + 1) * 128], rhs=h[:, k, :],
                                               start=(k == 0), stop=(k == 7))
                            for k in range(8):
                                ins = e.matmul(u_[:, 0:TT], lhsT=w3b[:, k, j * 128:(j + 1) * 128], rhs=h[:, k, :],
                                               start=(k == 0), stop=(k == 7))
                            return ins
                        S.op("pe", f_gu, reads=hkeys + [("w1b", k) for k in range(8)] + [("w3b", k) for k in range(8)],
                             writes=["gps%d" % (j % 2), "ups%d" % (j % 2)])
                        sgt = sg[j % 2]
                        S.op("act", lambda e, g_=g_, sgt=sgt: e.activation(out=sgt[:], in_=g_[:, 0:TT], func=AF.Silu),
                             reads=["gps%d" % (j % 2)], writes=["sg%d" % (j % 2)])
                        S.op("dve", lambda e, j=j, u_=u_, sgt=sgt: e.tensor_tensor(out=act[:, j, :], in0=u_[:, 0:TT], in1=sgt[:],
                                                                                 op=ALU.mult),
                             reads=["ups%d" % (j % 2), "sg%d" % (j % 2)], writes=[("act", j)])
                    for bl in range(2):
                        b = 2 * t + bl
                        ybanks = []
                        ykeys = []
                        for hf in range(2):
                            yb = ypsb[yrot % 3]
                            yk = "yps%d" % (yrot % 3)
                            yrot += 1
                            ybanks.append(yb)
                            ykeys.append(yk)

                            def f_y(e, yb=yb, hf=hf, bl=bl):
                                ins = None
                                for j in range(NJ):
                                    ins = e.matmul(yb[:], lhsT=act[:, j, bl * 128:(bl + 1) * 128],
                                                   rhs=w2b[:, j, hf * 512:(hf + 1) * 512], start=(j == 0), stop=(j == NJ - 1))
                                return ins
                            S.op("pe", f_y, reads=[("act", j) for j in range(NJ)] + [("w2b", j0) for j0 in range(0, NJ, 2)], writes=[yk])
                        to = tmpo[b % 2]
                        postnorm(ybanks, ykeys, ss2, "ss2", rs2[:, 0:1], "rs2", gtgt, xr[b % NXR], "xr%d" % (b % NXR), to,
                                 "tmpo%d" % (b % 2), scr, "scr", dst[b * 128:(b + 1) * 128, :], ("dst", b))
                S.barrier()

        if "B" in PHASES:
            ffn_phase(0, x1, x2, NB + 2, 1, "B")

        with ExitStack() as pc_:
          if "C" in PHASES:
            scin = sb(pc_, "scin", [128, 8, 3072], BF16)
            scout = sb(pc_, "scout", [128, 8, 1024], BF16)
            cw = sb(pc_, "cw", [128, 8, 3], F32)
            cval = sb(pc_, "cval", [128, 2], F32)
            gtgt = sb(pc_, "gtgtC", [128, 1024], F32)
            NBX = NB + 2
            hTa = sb(pc_, "hTa", [128, 8, NBX * 128], BF16)
            NXR = 6
            xr = [sb(pc_, "xrC%d" % i, [128, 1024], F32) for i in range(NXR)]
            xres = [sb(pc_, "xresC%d" % i, [128, 1024], F32) for i in range(2)]
            msr = sb(pc_, "msrC", [128, 8], F32)
            rsr = sb(pc_, "rsrC", [128, 8], F32)
            scr = sb(pc_, "scrC", [128, 1024], F32)
            xn = [sb(pc_, "xnC%d" % i, [128, 1024], BF16) for i in range(4)]
            cgs = [sb(pc_, "cgs%d" % i, [128, 258], F32) for i in range(2)]
            yb_ = [sb(pc_, "ybC%d" % i, [128, 258], F32) for i in range(2)]
            t1_ = [sb(pc_, "t1C%d" % i, [128, 256], F32) for i in range(2)]
            bgs = [sb(pc_, "bgs%d" % i, [128, 256], F32) for i in range(2)]
            z = [sb(pc_, "zC%d" % i, [128, 8, 256], BF16) for i in range(2)]
            tmpo = [sb(pc_, "tmpoC%d" % i, [128, 1024], F32) for i in range(2)]
            ss2 = sb(pc_, "ss2C", [128, 2], F32)
            rs2 = sb(pc_, "rs2C", [128, 1], F32)
            tp = ps(pc_, "tpC", [128, 8, 128], BF16)
            cgp = [ps(pc_, "cgp%d" % i, [128, 512]) for i in range(2)]
            hxp = [ps(pc_, "hxp%d" % i, [128, 512]) for i in range(2)]
            bgp = ps(pc_, "bgp", [128, 512])
            ypsb = [ps(pc_, "ypsC%d" % i, [128, 512]) for i in range(2)]

            for k in range(8):
                dma("pool", scin[:, k, :], sc_w_in[k * 128:(k + 1) * 128, :], "scin%d" % k, writes=[("scin", k)])
            dma("pool", scout[:], sc_w_out.rearrange("(k p) n -> p k n", p=128), "scout", writes=["scout"])
            dma("sp", cw[:].rearrange("p c k -> p (c k)"), cwT, "cw", writes=["cw"])
            dma("sp", cval[:], cvalid_in, "cval", writes=["cval"])
            dma("sp", gtgt[:], gtgscr[2 * 128:3 * 128, :], "gtgt", reads=[("gtgscr", 2)], writes=["gtgt"])

            def preC(b, part="ab"):
                xb = xr[b % NXR]
                xk = "xrC%d" % (b % NXR)
                if "l" in part:
                    dma("sp", xb[:], x2[b * 128:(b + 1) * 128, :], xk, writes=[xk])
                c8 = b % 8
                prenorm(xb, xk, msr[:, c8:c8 + 1], ("ms", c8), rsr[:, c8:c8 + 1], ("rs", c8), xn[b % 4], "xnC%d" % (b % 4),
                        tp, "tpC", lambda k, b=b: hTa[:, k, b * 128:(b + 1) * 128], ("hTa", b), 1, 0, 0, scr, "scrC", part=part)

            nexta = 0
            nextb = 0
            ci_ = 0
            for b0 in range(6):
                preC(b0, "l")
            nextl = 6
            for b0 in range(4):
                preC(b0, "a")
            nexta = 4
            for t in range(NB // 2):
                needl = min(NBX, 2 * t + 8)
                while nextl < needl:
                    preC(nextl, "l")
                    nextl += 1
                for bl in range(2):
                    b = 2 * t + bl
                    dma("sp", xres[b % 2][:], x2[(b + 1) * 128:(b + 2) * 128, :], "xresC%d" % (b % 2), writes=["xresC%d" % (b % 2)])
                need = min(NBX, 2 * t + 4)
                while nextb < need:
                    preC(nextb, "b")
                    nextb += 1
                needa = min(NBX, 2 * t + 6)
                while nexta < needa:
                    preC(nexta, "a")
                    nexta += 1
                base = 128 + t * 256
                hk = [("hTa", 2 * t), ("hTa", 2 * t + 1), ("hTa", 2 * t + 2), ("hTa", 2 * t + 3)]
                zt = z[t % 2]
                for c in range(8):
                    pp = ci_ % 2
                    ci_ += 1

                    def f_c(e, c=c, pp=pp, base=base):
                        ins = None
                        for k in range(8):
                            ins = e.matmul(cgp[pp][:, 0:258], lhsT=scin[:, k, 1024 + c * 128:1024 + (c + 1) * 128],
                                           rhs=hTa[:, k, base - 1:base + 257], start=(k == 0), stop=(k == 7))
                        for k in range(8):
                            ins = e.matmul(hxp[pp][:, 0:258], lhsT=scin[:, k, 2048 + c * 128:2048 + (c + 1) * 128],
                                           rhs=hTa[:, k, base - 1:base + 257], start=(k == 0), stop=(k == 7))
                        return ins
                    S.op("pe", f_c, reads=hk + [("scin", k) for k in range(8)], writes=["cgp%d" % pp, "hxp%d" % pp])

                    def f_b(e, c=c, base=base):
                        ins = None
                        for k in range(8):
                            ins = e.matmul(bgp[:, 0:256], lhsT=scin[:, k, c * 128:(c + 1) * 128],
                                           rhs=hTa[:, k, base:base + 256], start=(k == 0), stop=(k == 7))
                        return ins
                    S.op("pe", f_b, reads=hk + [("scin", k) for k in range(8)], writes=["bgp"])
                    S.op("act", lambda e, pp=pp: e.activation(out=bgs[pp][:], in_=bgp[:, 0:256], func=AF.Copy),
                         reads=["bgp"], writes=["bgs%d" % pp])
                    S.op("act", lambda e, pp=pp: e.activation(out=cgs[pp][:], in_=cgp[pp][:, 0:258], func=AF.Copy),
                         reads=["cgp%d" % pp], writes=["cgs%d" % pp])

                    f_y1 = [lambda e, pp=pp: e.tensor_tensor(out=yb_[pp][:], in0=hxp[pp][:, 0:258], in1=cgs[pp][:], op=ALU.mult)]
                    if t == 0:
                        f_y1.append(lambda e, pp=pp: e.tensor_scalar(out=yb_[pp][:, 0:1], in0=yb_[pp][:, 0:1], scalar1=cval[:, 0:1],
                                                                    scalar2=None, op0=ALU.mult))
                    if t == NB // 2 - 1:
                        f_y1.append(lambda e, pp=pp: e.tensor_scalar(out=yb_[pp][:, 257:258], in0=yb_[pp][:, 257:258],
                                                                    scalar1=cval[:, 1:2], scalar2=None, op0=ALU.mult))
                    S.op("dve", f_y1, reads=["hxp%d" % pp, "cgs%d" % pp, "cval"], writes=["ybC%d" % pp])

                    f_cv = [lambda e, pp=pp, c=c: e.tensor_scalar(out=t1_[pp][:], in0=yb_[pp][:, 1:257], scalar1=cw[:, c, 1:2],
                                                                 scalar2=None, op0=ALU.mult),
                            lambda e, pp=pp, c=c: e.scalar_tensor_tensor(out=t1_[pp][:], in0=yb_[pp][:, 0:256], scalar=cw[:, c, 0:1],
                                                                        in1=t1_[pp][:], op0=ALU.mult, op1=ALU.add),
                            lambda e, pp=pp, c=c: e.scalar_tensor_tensor(out=t1_[pp][:], in0=yb_[pp][:, 2:258], scalar=cw[:, c, 2:3],
                                                                        in1=t1_[pp][:], op0=ALU.mult, op1=ALU.add)]
                    S.op("dve", f_cv, reads=["ybC%d" % pp, "cw"], writes=["t1C%d" % pp])
                    S.op("dve", lambda e, pp=pp, c=c, zt=zt: e.tensor_tensor(out=zt[:, c, :], in0=bgs[pp][:], in1=t1_[pp][:],
                                                                            op=ALU.mult),
                         reads=["bgs%d" % pp, "t1C%d" % pp], writes=[("z", t % 2, c)])
                for bl in range(2):
                    b = 2 * t + bl
                    for hf in range(2):
                        def f_yo(e, hf=hf, bl=bl, zt=zt):
                            ins = None
                            for c in range(8):
                                ins = e.matmul(ypsb[hf][:], lhsT=zt[:, c, bl * 128:(bl + 1) * 128],
                                               rhs=scout[:, c, hf * 512:(hf + 1) * 512], start=(c == 0), stop=(c == 7))
                            return ins
                        S.op("pe", f_yo, reads=[("z", t % 2, c) for c in range(8)] + ["scout"], writes=["ypsC%d" % hf])
                    xb = xres[b % 2]
                    xk = "xresC%d" % (b % 2)
                    to = tmpo[b % 2]
                    postnorm(ypsb, ["ypsC0", "ypsC1"], ss2, "ss2C", rs2[:, 0:1], "rs2C", gtgt, xb, xk, to, "tmpoC%d" % (b % 2),
                             scr, "scrC", x3[b * 128:(b + 1) * 128, :], ("x3", b))
            S.barrier()

        if "D" in PHASES:
            ffn_phase(1, x3, out, NB, 3, "D")

        S.finalize()
        last_dmas = list(S.dsem.values())
        block = es.enter_context(nc.Block())

        @block.tensor
        def _(e):
            S.emit("pe", e)

        @block.scalar
        def _(e):
            S.emit("act", e)

        @block.vector
        def _(e):
            S.emit("dve", e)

        @block.gpsimd
        def _(e):
            S.emit("pool", e)

        @block.sync
        def _(e):
            S.emit("sp", e)
            for sem, cnt in last_dmas:
                e.wait_ge(sem, cnt)
    return nc


def _host_prep(inputs):
    f32 = np.float32
    x = np.asarray(inputs["x"], f32)
    c = np.asarray(inputs["c"], f32)
    ctx = np.asarray(inputs["ctx"], f32)
    c_ctx = np.asarray(inputs["c_ctx"], f32)
    w_mod = np.ascontiguousarray(np.asarray(inputs["w_mod"], f32))
    b_mod = np.ascontiguousarray(np.asarray(inputs["b_mod"], f32))
    g_mix_pre = np.asarray(inputs["g_mix_pre"], f32)
    g_mix_post = np.asarray(inputs["g_mix_post"], f32)
    g_ffn_pre = np.asarray(inputs["g_ffn_pre"], f32)
    g_ffn_post = np.asarray(inputs["g_ffn_post"], f32)
    a_w_in = np.asarray(inputs["a_w_in"], f32)[0]
    qcols = np.concatenate([np.arange(h * 64, (h + 1) * 64) for h in (0, 4, 1, 5, 2, 6, 3, 7)])
    cols = np.concatenate([qcols, np.arange(768, 1280), np.arange(1280, 1792), np.arange(512, 640), np.arange(640, 768)])
    w_in = np.ascontiguousarray(a_w_in[:, cols])
    shared = {
        "w_mod": w_mod,
        "b_mod": b_mod,
        "b_modT": np.ascontiguousarray(b_mod.reshape(2, 48, 128).transpose(2, 0, 1).reshape(128, 96)),
        "gT": np.ascontiguousarray(np.stack([g_mix_pre, g_ffn_pre], axis=1).reshape(2, 2, 8, 128).transpose(3, 0, 1, 2).reshape(128, 32)),
        "g_post": np.ascontiguousarray(np.stack([g_mix_post, g_ffn_post], axis=1).reshape(4, D)),
        "w_in": w_in,
        "w_out": np.ascontiguousarray(np.asarray(inputs["a_w_out"], f32)[0]),
        "wsT": np.ascontiguousarray(np.asarray(inputs["gm_ws"], f32)[0].transpose(2, 0, 1).reshape(128, 1024)),
        "bsT": np.ascontiguousarray(np.asarray(inputs["gm_bs"], f32)[0].T),
        "vnorm": np.ascontiguousarray(np.asarray(inputs["gm_v_norm"], f32).reshape(1, 512)),
        "sink": np.ascontiguousarray(np.asarray(inputs["a_sink"], f32).reshape(1, 8)),
        "ident": np.eye(128, dtype=f32),
        "ffn_w1": np.ascontiguousarray(np.asarray(inputs["ffn_w1"], f32)),
        "ffn_w3": np.ascontiguousarray(np.asarray(inputs["ffn_w3"], f32)),
        "ffn_w2": np.ascontiguousarray(np.asarray(inputs["ffn_w2"], f32)),
        "sc_w_in": np.ascontiguousarray(np.asarray(inputs["sc_w_in"], f32)[0]),
        "sc_w_out": np.ascontiguousarray(np.asarray(inputs["sc_w_out"], f32)[0]),
        "cwT": np.ascontiguousarray(np.asarray(inputs["sc_conv"], f32)[0].reshape(3, 8, 128).transpose(2, 1, 0).reshape(128, 24)),
    }
    kj = np.arange(128)[:, None]
    qi = np.arange(128)[None, :]
    m_prev = np.where(kj >= qi, 0.0, -30000.0).astype(f32)
    m_next = np.where(kj <= qi, 0.0, -30000.0).astype(f32)
    shared["trimask"] = np.ascontiguousarray(np.concatenate([np.tile(m_prev, (1, 4)), np.tile(m_next, (1, 4))], axis=1))
    inv = (np.float32(10000.0) ** (-np.arange(16, dtype=f32) / np.float32(16))).astype(f32)
    in_maps = []
    for core in range(NCORES):
        b, qd = divmod(core, 4)
        t0 = qd * 4096 - 256
        xin = np.zeros((NE * 128, D), f32)
        lo, hi = max(t0, 0), min(t0 + NE * 128, SEQ)
        xin[lo - t0:hi - t0] = x[b, lo:hi]
        tpos = np.arange(t0, t0 + NE * 128)
        tpos = np.clip(tpos, 0, SEQ - 1)
        row = (tpos // 64).astype(f32)[:, None]
        col = (tpos % 64).astype(f32)[:, None]
        ar = (row * inv).astype(f32)
        ac = (col * inv).astype(f32)
        cr, sr, cc_, sc_ = np.cos(ar).astype(f32), np.sin(ar).astype(f32), np.cos(ac).astype(f32), np.sin(ac).astype(f32)
        ropeC = np.concatenate([cr, cr, cc_, cc_], axis=1)
        ropeS = np.concatenate([-sr, sr, -sc_, sc_], axis=1)
        kb = np.zeros((128, NE + 2), f32)
        for e in range(NE):
            gb = qd * 32 + e - 2
            if gb < 0 or gb >= SEQ // 128:
                kb[:, e] = -30000.0
        cv = np.zeros((128, 2), f32)
        cv[:, 0] = 1.0 if qd > 0 else 0.0
        cv[:, 1] = 1.0 if qd < 3 else 0.0
        cT = np.concatenate([c[b].reshape(8, 128).T, c_ctx.reshape(8, 128).T], axis=1)
        m = dict(shared)
        m.update({
            "xin": xin,
            "ctxin": np.ascontiguousarray(ctx[b]),
            "cT": np.ascontiguousarray(cT),
            "ropeC": np.ascontiguousarray(ropeC.reshape(NE, 128, 64).transpose(1, 0, 2).reshape(128, NE * 64)),
            "ropeS": np.ascontiguousarray(ropeS.reshape(NE, 128, 64).transpose(1, 0, 2).reshape(128, NE * 64)),
            "kbias": kb,
            "cvalid": cv,
        })
        in_maps.append(m)
    return in_maps


_NC_CACHE = {}


def kernel(**inputs):
    in_maps = _host_prep(inputs)
    if "nc" not in _NC_CACHE:
        _NC_CACHE["nc"] = build_nc()
    nc = _NC_CACHE["nc"]
    if PHASES != "MABCD":
        drop = set()
        if "M" not in PHASES:
            drop |= {"cT", "w_mod", "b_modT", "b_mod", "gT", "g_post"}
        if "B" not in PHASES and "D" not in PHASES:
            drop |= {"ffn_w1", "ffn_w3", "ffn_w2"}
        if "C" not in PHASES:
            drop |= {"sc_w_in", "sc_w_out", "cwT", "cvalid"}
        in_maps = [{k: v for k, v in m.items() if k not in drop} for m in in_maps]
    res = run_bass_kernel_spmd(nc, in_maps, core_ids=list(range(NCORES)))
    outs = [np.asarray(r["out"]) for r in res.results]
    full = np.stack([np.concatenate(outs[b * 4:(b + 1) * 4], axis=0) for b in range(2)], axis=0)
    if DEBUG:
        kernel.debug = res.results
    return full.astype(np.float32)
```

```python
import numpy as np
from contextlib import ExitStack
import concourse.bass as bass
import concourse.mybir as mybir
from concourse.bass_utils import run_bass_kernel_spmd

F32 = mybir.dt.float32
BF16 = mybir.dt.bfloat16
AF = mybir.ActivationFunctionType
ALU = mybir.AluOpType
AX = mybir.AxisListType

NCORES = 8
D = 1024
SEQ = 16384
NB = 32
NE = NB + 4
DFF = 2816
NJ = DFF // 128
DEBUG = False
PHASES = "MABCD"
A_LIMIT = None
A_STEP = 99
DBG_E = 6


import re
_PSUM_RE = re.compile(r"^(colps|rowps\d|tpq|tm|bk\d|tp|tpC|gps\d|ups\d|yps\d|ypsC\d|cgp\d|hxp\d|bgp)$")


class _Op:
    __slots__ = ("eng", "fn", "deps", "token", "is_dma", "signal", "seq", "same_sync")


class Sched:
    ENGS = ("pe", "act", "dve", "pool", "sp")

    def __init__(self, nc, es):
        self.nc = nc
        self.es = es
        self.ops = {e: [] for e in self.ENGS}
        self.esem = {e: es.enter_context(nc.semaphore("s_" + e)) for e in ("pe", "act", "dve", "pool")}
        self.csem = {e: es.enter_context(nc.semaphore("c_" + e)) for e in ("act", "dve", "pool")}
        self.ccnt = {e: 0 for e in ("act", "dve", "pool")}
        self.buf = {}
        self.dsem = {}
        self.pending = {e: [] for e in self.ENGS}
        self.dma_since_barrier = []
        self.nops = 0

    def op(self, eng, fn, reads=(), writes=(), dma_key=None, same_sync=False):
        o = _Op()
        o.same_sync = same_sync
        o.eng = eng
        o.fn = fn
        o.is_dma = dma_key is not None
        o.signal = False
        o.token = None
        o.seq = self.nops
        self.nops += 1
        deps = []
        reads = list(reads)
        writes = list(writes)
        for k in list(reads):
            if isinstance(k, str) and _PSUM_RE.match(k):
                reads.remove(k)
                if k not in writes:
                    writes.append(k)
        for k in reads:
            b = self.buf.setdefault(k, [None, []])
            if b[0] is not None:
                deps.append(b[0])
        for k in writes:
            b = self.buf.setdefault(k, [None, []])
            if b[0] is not None:
                deps.append(b[0])
            deps.extend(b[1])
        for k in reads:
            self.buf[k][1].append(o)
        for k in writes:
            b = self.buf[k]
            b[0] = o
            b[1] = []
        deps.extend(self.pending[eng])
        self.pending[eng] = []
        o.deps = [d for d in deps if d is not o]
        self.ops[eng].append(o)
        if o.is_dma:
            if dma_key not in self.dsem:
                self.dsem[dma_key] = [self.es.enter_context(self.nc.semaphore("d_" + str(len(self.dsem)))), 0]
            s = self.dsem[dma_key]
            s[1] += 16
            o.token = (s[0], s[1])
            self.dma_since_barrier.append(o)
        return o

    def barrier(self):
        toks = []
        for e in self.ENGS:
            if self.ops[e]:
                toks.append(self.ops[e][-1])
        toks.extend(self.dma_since_barrier)
        self.dma_since_barrier = []
        for e in self.ENGS:
            self.pending[e].extend(toks)
        self.buf = {}

    @staticmethod
    def _needs_sync(d, o):
        return d.eng != o.eng or d.is_dma or o.is_dma or o.same_sync or o.eng != "pe"

    def finalize(self):
        for e in self.ENGS:
            for o in self.ops[e]:
                for d in o.deps:
                    if self._needs_sync(d, o) and not d.is_dma:
                        d.signal = True
        for e in ("pe", "act", "dve", "pool", "sp"):
            c = 0
            for o in self.ops[e]:
                if not o.is_dma and o.signal:
                    assert e != "sp"
                    c += 1
                    o.token = (self.esem[e], c)

    def emit(self, eng, engine):
        waited = {}
        for o in self.ops[eng]:
            need = {}
            for d in o.deps:
                if self._needs_sync(d, o):
                    sem, v = d.token
                    k = id(sem)
                    if v > need.get(k, (None, 0))[1]:
                        need[k] = (sem, v)
            for k, (sem, v) in need.items():
                if waited.get(k, 0) < v:
                    engine.wait_ge(sem, v)
                    waited[k] = v
            if isinstance(o.fn, (list, tuple)):
                ins = None
                for i, f in enumerate(o.fn):
                    ins = f(engine)
                    if i < len(o.fn) - 1:
                        self.ccnt[eng] += 1
                        ins.then_inc(self.csem[eng], 1)
                        engine.wait_ge(self.csem[eng], self.ccnt[eng])
            else:
                ins = o.fn(engine)
            if o.is_dma:
                ins.then_inc(o.token[0], 16)
            elif o.signal:
                ins.then_inc(o.token[0], 1)

    def final_wait(self, eng, engine, ops):
        for o in ops:
            engine.wait_ge(o.token[0], o.token[1])


def build_nc():
    nc = bass.Bass("TRN2", target_bir_lowering=False)
    dk = "ExternalOutput" if DEBUG else "Internal"

    def din(name, shape):
        return nc.dram_tensor(name, list(shape), F32, kind="ExternalInput").ap()

    xin = din("xin", [NE * 128, D])
    ctxin = din("ctxin", [256, D])
    if "M" in PHASES:
        cT = din("cT", [128, 16])
        w_mod = din("w_mod", [2, D, 6 * D])
        b_modT = din("b_modT", [128, 96])
        b_mod = din("b_mod", [2, 6 * D])
        gT = din("gT", [128, 32])
        g_post = din("g_post", [4, D])
    w_in = din("w_in", [D, 1792])
    w_out = din("w_out", [D, D])
    wsT = din("wsT", [128, 8 * 128])
    bsT = din("bsT", [128, 8])
    vnorm = din("vnorm", [1, 512])
    sink = din("sink", [1, 8])
    ropeC = din("ropeC", [128, NE * 64])
    ropeS = din("ropeS", [128, NE * 64])
    kbias_in = din("kbias", [128, NE + 2])
    trimask_in = din("trimask", [128, 1024])
    ident_in = din("ident", [128, 128])
    if "C" in PHASES:
        cvalid_in = din("cvalid", [128, 2])
    if "B" in PHASES or "D" in PHASES:
        ffn_w1 = din("ffn_w1", [2, D, DFF])
        ffn_w3 = din("ffn_w3", [2, D, DFF])
        ffn_w2 = din("ffn_w2", [2, DFF, D])
    if "C" in PHASES:
        sc_w_in = din("sc_w_in", [D, 3 * D])
        sc_w_out = din("sc_w_out", [D, D])
        cwT = din("cwT", [128, 24])
    out = nc.dram_tensor("out", [NB * 128, D], F32, kind="ExternalOutput").ap()
    x1 = nc.dram_tensor("x1", [(NB + 2) * 128, D], F32, kind=dk).ap()
    x2 = nc.dram_tensor("x2", [(NB + 2) * 128, D], F32, kind=dk).ap()
    x3 = nc.dram_tensor("x3", [NB * 128, D], F32, kind=dk).ap()
    gtgscr = nc.dram_tensor("gtgscr", [4 * 128, D], F32, kind=dk).ap()
    dbg_modc = nc.dram_tensor("dbg_modc", [128, 128], F32, kind=dk).ap()
    dbgt = {}
    if DEBUG:
        for nm, shp, dt_ in [("d_hT", [128, 1024], BF16), ("d_qks", [128, 640], F32), ("d_qr", [128, 640], BF16),
                             ("d_u", [128, 512], F32), ("d_vn", [128, 512], BF16), ("d_mix", [128, 1024], BF16),
                             ("d_qT", [128, 512], BF16), ("d_pT", [128, 512], BF16), ]:
            dbgt[nm] = nc.dram_tensor(nm, shp, dt_, kind="ExternalOutput").ap()

    with ExitStack() as es:
        S = Sched(nc, es)

        def sb(stack, name, shape, dt):
            return stack.enter_context(nc.sbuf_tensor("sb_" + name, list(shape), dt))

        def ps(stack, name, shape, dt=F32):
            return stack.enter_context(nc.psum_tensor("ps_" + name, list(shape), dt))

        def dma(eng, out_ap, in_ap, key, reads=(), writes=()):
            return S.op(eng, lambda e: e.dma_start(out=out_ap, in_=in_ap), reads=reads, writes=writes, dma_key=key)

        def dbgdump(name, ap2d, key):
            if DEBUG:
                dma("sp", dbgt[name], ap2d, "dbg_" + name, reads=[key])

        ident = sb(es, "ident", [128, 128], BF16)
        modc = sb(es, "modc", [128, 2, 2, 2, 8, 2], F32)
        consts = sb(es, "consts", [128, 8], F32)
        mh8 = sb(es, "mh8", [128, 8], F32)
        epsl8 = sb(es, "epsl8", [128, 8], F32)

        dma("pool", ident[:], ident_in, "ident", writes=["ident"])
        S.op("pool", lambda e: e.memset(consts[:, 0:1], 1e-6), writes=["consts"])
        S.op("pool", lambda e: e.memset(consts[:, 1:2], -0.5), writes=["consts"])
        S.op("pool", lambda e: e.memset(mh8[:], -0.5), writes=["mh8"])
        S.op("pool", lambda e: e.memset(epsl8[:], 1e-5), writes=["epsl8"])

        def rstd_ops(ss, ncols, rs, key_ss, key_rs):
            if ncols == 2:
                f = [lambda e: e.tensor_tensor(out=rs, in0=ss[:, 0:1], in1=ss[:, 1:2], op=ALU.add),
                     lambda e: e.tensor_tensor(out=rs, in0=rs, in1=consts[:, 0:1], op=ALU.add),
                     lambda e: e.tensor_tensor(out=rs, in0=rs, in1=consts[:, 1:2], op=ALU.pow)]
            else:
                f = [lambda e: e.tensor_tensor(out=rs, in0=ss[:, 0:1], in1=consts[:, 0:1], op=ALU.add),
                     lambda e: e.tensor_tensor(out=rs, in0=rs, in1=consts[:, 1:2], op=ALU.pow)]
            S.op("pool", f, reads=[key_ss, "consts"], writes=[key_rs])

        with ExitStack() as pm:
          if "M" in PHASES:
            cTs = sb(pm, "cTs", [128, 16], F32)
            sil = sb(pm, "sil", [128, 16], F32)
            srep = sb(pm, "srep", [128, 8, 128], F32)
            svec = sb(pm, "svec", [128, 8, 2], F32)
            bmT = sb(pm, "bmT", [128, 2, 48], F32)
            gTs = sb(pm, "gTs", [128, 2, 2, 8], F32)
            modT = sb(pm, "modT", [128, 2, 4, 8, 2], F32)
            wm = [sb(pm, "wm%d" % i, [128, 8, 1024], F32) for i in range(3)]
            brow = [sb(pm, "brow%d" % i, [128, 1024], F32) for i in range(2)]
            grow = [sb(pm, "grow%d" % i, [128, 1024], F32) for i in range(2)]
            gtg = [sb(pm, "gtgm%d" % i, [128, 1024], F32) for i in range(2)]
            colps = ps(pm, "colps", [128, 8, 2])
            rowps = [ps(pm, "rowps%d" % i, [128, 512]) for i in range(2)]

            dma("sp", cTs[:], cT, "cTs", writes=["cTs"])
            dma("sp", bmT[:].rearrange("p a b -> p (a b)"), b_modT, "bmT", writes=["bmT"])
            dma("sp", gTs[:].rearrange("p a b c -> p (a b c)"), gT, "gTs", writes=["gTs"])
            S.op("act", lambda e: e.activation(out=sil[:], in_=cTs[:], func=AF.Silu), reads=["cTs"], writes=["sil"])
            S.op("dve", lambda e: e.tensor_copy(out=srep[:], in_=sil[:, 0:8].unsqueeze(2).to_broadcast([128, 8, 128])),
                 reads=["sil"], writes=["srep"])

            def f_svec(e):
                e.tensor_copy(out=svec[:, :, 0], in_=sil[:, 0:8])
                return e.tensor_copy(out=svec[:, :, 1], in_=sil[:, 8:16])
            S.op("dve", f_svec, reads=["sil"], writes=["svec"])

            pi = 0
            ri = 0
            pieces = [(i, pc) for i in range(2) for pc in range(6)]

            def load_piece(q):
                if q < len(pieces):
                    i_, pc_ = pieces[q]
                    dma("sp", wm[q % 3][:], w_mod[i_, :, pc_ * 1024:(pc_ + 1) * 1024].rearrange("(k p) n -> p k n", p=128),
                        "wm%d" % (q % 3), writes=["wm%d" % (q % 3)])
            load_piece(0)
            load_piece(1)
            for i in range(2):
                for pc in range(6):
                    wmb = wm[pi % 3]
                    wk = "wm%d" % (pi % 3)
                    load_piece(pi + 2)
                    pi += 1
                    if pc in (0, 1, 3, 4):
                        kind = {0: 0, 1: 1, 3: 2, 4: 3}[pc]

                        def f_col(e, wmb=wmb):
                            ins = None
                            for oc in range(8):
                                for k in range(8):
                                    ins = e.matmul(colps[:, oc, :], lhsT=wmb[:, k, oc * 128:(oc + 1) * 128],
                                                   rhs=svec[:, k, :], start=(k == 0), stop=(k == 7))
                            return ins
                        S.op("pe", f_col, reads=[wk, "svec"], writes=["colps"])
                        S.op("dve", lambda e, i=i, kind=kind, pc=pc: e.tensor_tensor(
                            out=modT[:, i, kind], in0=colps[:],
                            in1=bmT[:, i, pc * 8:(pc + 1) * 8].unsqueeze(2).to_broadcast([128, 8, 2]), op=ALU.add),
                            reads=["colps", "bmT"], writes=["modT"])
                    else:
                        which = 0 if pc == 2 else 1
                        r = ri % 2
                        ri += 1
                        dma("sp", brow[r][:], b_mod[i:i + 1, pc * 1024:(pc + 1) * 1024].partition_broadcast(128),
                            "brow%d" % r, writes=["brow%d" % r])
                        dma("sp", grow[r][:], g_post[2 * i + which:2 * i + which + 1, :].partition_broadcast(128),
                            "grow%d" % r, writes=["grow%d" % r])
                        for hf in range(2):
                            def f_row(e, wmb=wmb, hf=hf):
                                ins = None
                                for k in range(8):
                                    ins = e.matmul(rowps[hf][:], lhsT=srep[:, k, :],
                                                   rhs=wmb[:, k, hf * 512:(hf + 1) * 512], start=(k == 0), stop=(k == 7))
                                return ins
                            S.op("pe", f_row, reads=[wk, "srep"], writes=["rowps%d" % hf])

                            sl_ = slice(hf * 512, (hf + 1) * 512)
                            f_rowev = [lambda e, r=r, hf=hf, sl=sl_: e.tensor_tensor(out=gtg[r][:, sl], in0=rowps[hf][:], in1=brow[r][:, sl], op=ALU.add),
                                       lambda e, r=r, hf=hf, sl=sl_: e.tensor_tensor(out=gtg[r][:, sl], in0=gtg[r][:, sl], in1=grow[r][:, sl], op=ALU.mult)]
                            S.op("dve", f_rowev, reads=["rowps%d" % hf, "brow%d" % r, "grow%d" % r], writes=["gtg%d" % r])
                        gi = 2 * i + which
                        dma("sp", gtgscr[gi * 128:(gi + 1) * 128, :], gtg[r][:], "gtg%d" % r,
                            reads=["gtg%d" % r], writes=[("gtgscr", gi)])

            def f_modc(e):
                ins = None
                for i in range(2):
                    for kd in range(2):
                        e.scalar_tensor_tensor(out=modc[:, i, kd, 0], in0=modT[:, i, 2 * kd + 1], scalar=1.0,
                                               in1=gTs[:, i, kd].unsqueeze(2).to_broadcast([128, 8, 2]),
                                               op0=ALU.add, op1=ALU.mult)
                        ins = e.tensor_copy(out=modc[:, i, kd, 1], in_=modT[:, i, 2 * kd])
                return ins
            S.op("dve", f_modc, reads=["modT", "gTs"], writes=["modc"])
            if DEBUG:
                dma("sp", dbg_modc, modc[:].rearrange("p a b c d e -> p (a b c d e)"), "dbgmodc", reads=["modc"])
            S.barrier()

        def prenorm(xb, xkey, ms, mskey, rs, rskey, xn, xnkey, tp, tpkey, hT_dst, hkey, li, kd, vec, scr, scrkey, part="ab"):
            if "a" in part:
                S.op("act", lambda e: e.activation(out=scr[:], in_=xb[:], func=AF.Square, scale=1.0 / 32.0, accum_out=ms),
                     reads=[xkey], writes=[scrkey, mskey])
                rstd_ops(ms, 1, rs, mskey, rskey)
                S.op("dve", lambda e: e.tensor_scalar(out=xn[:], in0=xb[:], scalar1=rs, scalar2=None, op0=ALU.mult),
                     reads=[xkey, rskey], writes=[xnkey])
            if "b" not in part:
                return

            def f_tr(e):
                ins = None
                for k in range(8):
                    ins = e.transpose(tp[:, k, :], xn[:, k * 128:(k + 1) * 128], ident[:])
                return ins
            S.op("pe", f_tr, reads=[xnkey, "ident"], writes=[tpkey])
            if A_STEP < 6:
                return

            def f_mod(e):
                ins = None
                for k in range(8):
                    ins = e.activation(out=hT_dst(k), in_=tp[:, k, :], func=AF.Identity,
                                       scale=modc[:, li, kd, 0, k, vec:vec + 1], bias=modc[:, li, kd, 1, k, vec:vec + 1])
                return ins
            S.op("act", f_mod, reads=[tpkey, "modc"], writes=[hkey])

        def postnorm(yps, ykeys, ss2, sskey, rs, rskey, gtgt, xb, xkey, tmp, tmpkey, scr, scrkey, dst_ap, dstkey):
            def f_sq(e):
                e.activation(out=scr[:, 0:512], in_=yps[0][:], func=AF.Square, scale=1.0 / 32.0, accum_out=ss2[:, 0:1])
                return e.activation(out=scr[:, 512:1024], in_=yps[1][:], func=AF.Square, scale=1.0 / 32.0,
                                    accum_out=ss2[:, 1:2])
            S.op("act", f_sq, reads=list(ykeys), writes=[scrkey, sskey])
            rstd_ops(ss2, 2, rs, sskey, rskey)

            def f_pn1(e):
                ins = None
                for hf in range(2):
                    sl = slice(hf * 512, (hf + 1) * 512)
                    ins = e.scalar_tensor_tensor(out=tmp[:, sl], in0=yps[hf][:], scalar=rs, in1=gtgt[:, sl],
                                                 op0=ALU.mult, op1=ALU.mult)
                return ins
            f_pn = [f_pn1, lambda e: e.tensor_tensor(out=tmp[:], in0=tmp[:], in1=xb[:], op=ALU.add)]
            S.op("dve", f_pn, reads=list(ykeys) + [rskey, "gtgt", xkey], writes=[tmpkey])
            dma("sp", dst_ap, tmp[:], tmpkey, reads=[tmpkey], writes=[dstkey])

        with ExitStack() as pa:
          if "A" in PHASES:
            w_in_b = sb(pa, "w_in_b", [128, 8, 1792], BF16)
            w_out_b = sb(pa, "w_out_b", [128, 8, 1024], BF16)
            wsT_b = sb(pa, "wsT_b", [128, 8, 128], BF16)
            bsT_s = sb(pa, "bsT_s", [128, 8], F32)
            vnorm_b = sb(pa, "vnorm_b", [128, 512], F32)
            sinkexp = sb(pa, "sinkexp", [128, 8], F32)
            rC = sb(pa, "rC", [128, NE, 64], F32)
            rS = sb(pa, "rS", [128, NE, 64], F32)
            kbias = sb(pa, "kbias_s", [128, NE + 2], F32)
            trim = sb(pa, "trim", [128, 2, 512], BF16)
            gtgt = sb(pa, "gtgt", [128, 1024], F32)
            kTc = sb(pa, "kTc", [128, NE + 2, 128], BF16)
            vc = sb(pa, "vc", [128, NE + 2, 2, 65], BF16)
            NXR = 5
            xr = [sb(pa, "xr%d" % i, [128, 1024], F32) for i in range(NXR)]
            msr = sb(pa, "msr", [128, 8], F32)
            rsr = sb(pa, "rsr", [128, 8], F32)
            scr = sb(pa, "scrA", [128, 1024], F32)
            xn = [sb(pa, "xn%d" % i, [128, 1024], BF16) for i in range(2)]
            hT = [sb(pa, "hT%d" % i, [128, 8, 128], BF16) for i in range(2)]
            ccx = sb(pa, "ccx", [128, 8, 64], F32)
            ssx = sb(pa, "ssx", [128, 8, 64], F32)
            qks = sb(pa, "qks", [128, 640], F32)
            bsx = sb(pa, "bsx", [128, 8, 64], F32)
            rt1 = sb(pa, "rt1", [128, 640], F32)
            rt2 = sb(pa, "rt2", [128, 640], F32)
            qr = sb(pa, "qr", [128, 640], BF16)
            qT = [sb(pa, "qT%d" % i, [128, 4, 128], BF16) for i in range(2)]
            u_sb = [sb(pa, "u_sb%d" % i, [128, 512], F32) for i in range(2)]
            cen = sb(pa, "cen", [128, 512], F32)
            sq = sb(pa, "sq", [128, 512], F32)
            st8 = sb(pa, "st8", [128, 4, 8], F32)
            vn = [sb(pa, "vn%d" % i, [128, 512], BF16) for i in range(2)]
            pT = [sb(pa, "pT%d" % i, [128, 512], BF16) for i in range(3)]
            den = sb(pa, "den", [128, 2, 8], F32)
            mix = sb(pa, "mix", [128, 1024], BF16)
            gtmp = sb(pa, "gtmp", [128, 512], F32)
            mixT = sb(pa, "mixT", [128, 8, 128], BF16)
            tmpo = [sb(pa, "tmpo%d" % i, [128, 1024], F32) for i in range(2)]
            ss2 = sb(pa, "ss2", [128, 2], F32)
            rs2 = sb(pa, "rs2", [128, 1], F32)
            tpq = ps(pa, "tpq", [128, 8, 128], BF16)
            tm = tpq
            bk = [ps(pa, "bkA%d" % i, [128, 512]) for i in range(7)]
            pj0, pj1, pj2, pj3, spsb, ops0b, spsb2 = bk
            spsl = [spsb, spsb2]
            spskey = ["bk4", "bk6"]
            yps = [pj0, pj1]
            gps = pj2
            ops = [ops0b[:, 0:260].rearrange("p (h d) -> p h d", d=65), pj3[:, 0:260].rearrange("p (h d) -> p h d", d=65)]
            opskey = ["bk5", "bk3"]

            dma("pool", w_in_b[:], w_in.rearrange("(k p) n -> p k n", p=128), "w_in_b", writes=["w_in_b"])
            dma("pool", wsT_b[:].rearrange("p g i -> p (g i)"), wsT, "wsT_b", writes=["wsT_b"])
            dma("pool", trim[:].rearrange("p a n -> p (a n)"), trimask_in, "trim", writes=["trim"])
            dma("pool", w_out_b[:], w_out.rearrange("(k p) n -> p k n", p=128), "w_out_b", writes=["w_out_b"])
            dma("sp", bsT_s[:], bsT, "bsT_s", writes=["bsT_s"])
            dma("sp", vnorm_b[:], vnorm.partition_broadcast(128), "vnorm_b", writes=["vnorm_b"])
            dma("sp", sinkexp[:], sink.partition_broadcast(128), "sinkexp", writes=["sinkexp"])
            dma("sp", rC[:].rearrange("p e f -> p (e f)"), ropeC, "rC", writes=["rC"])
            dma("sp", rS[:].rearrange("p e f -> p (e f)"), ropeS, "rS", writes=["rS"])
            dma("sp", kbias[:], kbias_in, "kbias", writes=["kbias"])
            dma("sp", gtgt[:], gtgscr[0:128, :], "gtgt", reads=[("gtgscr", 0)], writes=["gtgt"])
            S.op("act", lambda e: e.activation(out=sinkexp[:], in_=sinkexp[:], func=AF.Exp), reads=["sinkexp"], writes=["sinkexp"])
            S.op("dve", lambda e: e.memset(vc[:].rearrange("p e k d -> p (e k d)"), 1.0), writes=["vc_init"])
            S.op("pool", lambda e: e.tensor_copy(out=bsx[:], in_=bsT_s[:].unsqueeze(2).to_broadcast([128, 8, 64])),
                 reads=["bsT_s"], writes=["bsx"])

            def proj_stage(e_idx, n):
                is_ctx = e_idx >= NE
                full = (not is_ctx) and (1 <= e_idx <= NE - 2)
                xb = xr[n % NXR]
                xk = "xr%d" % (n % NXR)
                c8 = n % 8
                h = hT[n % 2]
                hk = "hT%d" % (n % 2)
                prenorm(xb, xk, msr[:, c8:c8 + 1], ("ms", c8), rsr[:, c8:c8 + 1], ("rs", c8), xn[n % 2], "xn%d" % (n % 2),
                        tpq, "tpq", lambda k: h[:, k, :], hk, 0, 0, 1 if is_ctx else 0, scr, "scrA", part="b")
                if e_idx == DBG_E:
                    dbgdump("d_hT", h[:].rearrange("p k t -> p (k t)"), hk)
                if A_STEP < 7:
                    return
                groups = [(3, 1024 + 512, 256)]
                if full:
                    groups = [(0, 0, 512), (1, 512, 512), (2, 1024, 512), (3, 1536, 256)]

                def f_proj(e):
                    ins = None
                    for (b, c0, w) in groups:
                        for k in range(8):
                            ins = e.matmul(bk[b][:, 0:w], lhsT=h[:, k, :], rhs=w_in_b[:, k, c0:c0 + w],
                                           start=(k == 0), stop=(k == 7))
                    return ins
                S.op("pe", f_proj, reads=[hk, "w_in_b"], writes=["bk%d" % g[0] for g in groups])
                if A_STEP < 8:
                    return
                S.op("act", lambda e: e.activation(out=vc[:, e_idx, :, 0:64],
                                                   in_=pj3[:, 128:256].rearrange("p (k d) -> p k d", d=64), func=AF.Copy),
                     reads=["bk3", "vc_init"], writes=[("vc", e_idx)])
                if A_STEP < 9:
                    return
                if is_ctx:
                    S.op("dve", lambda e: e.tensor_copy(out=qr[:, 512:640], in_=pj3[:, 0:128]), reads=["bk3"], writes=["qr_k"])
                else:
                    def f_exp(e):
                        e.tensor_copy(out=ccx[:], in_=rC[:, e_idx, :].unsqueeze(1).to_broadcast([128, 8, 64]))
                        return e.tensor_copy(out=ssx[:], in_=rS[:, e_idx, :].unsqueeze(1).to_broadcast([128, 8, 64]))
                    S.op("pool", f_exp, reads=["rC", "rS"], writes=["ccx"])

                    def f_cp(e):
                        ins = e.activation(out=qks[:, 512:640], in_=pj3[:, 0:128], func=AF.Copy)
                        if full:
                            ins = e.activation(out=qks[:, 0:512], in_=pj0[:], func=AF.Copy)
                        return ins
                    S.op("act", f_cp, reads=["bk3"] + (["bk0"] if full else []), writes=["qks"])

                    segs = [(512, 640, 2)] + ([(0, 512, 8)] if full else [])

                    def f_rope1(e):
                        ins = None
                        for (c0, c1, nh) in segs:
                            e.tensor_tensor(out=rt1[:, c0:c1], in0=qks[:, c0:c1],
                                            in1=ccx[:, 0:nh, :].rearrange("p h f -> p (h f)"), op=ALU.mult)
                            s5 = qks[:, c0:c1].rearrange("p (h a b f) -> p h a b f", a=2, b=2, f=16)
                            t5 = rt2[:, c0:c1].rearrange("p (h a b f) -> p h a b f", a=2, b=2, f=16)
                            x5 = ssx[:, 0:nh, :].rearrange("p h (a b f) -> p h a b f", a=2, b=2, f=16)
                            for ab in range(2):
                                ins = e.tensor_tensor(out=t5[:, :, :, ab, :], in0=s5[:, :, :, 1 - ab, :], in1=x5[:, :, :, ab, :],
                                                      op=ALU.mult)
                        return ins

                    def f_rope2(e):
                        ins = None
                        for (c0, c1, nh) in segs:
                            ins = e.tensor_tensor(out=qr[:, c0:c1], in0=rt1[:, c0:c1], in1=rt2[:, c0:c1], op=ALU.add)
                        return ins
                    f_rope = [f_rope1, f_rope2]
                    S.op("dve", f_rope, reads=["qks", "ccx"], writes=["qr_k", "qr_q", "rt_k"])
                    if e_idx == DBG_E:
                        dbgdump("d_qks", qks[:], "qks")
                        dbgdump("d_qr", qr[:], "qr_k")
                if A_STEP < 10:
                    return
                nq = 4 if full else 0

                def f_trq(e):
                    ins = None
                    for s_ in range(nq):
                        ins = e.transpose(tpq[:, s_, :], qr[:, s_ * 128:(s_ + 1) * 128], ident[:])
                    return e.transpose(tpq[:, 4, :], qr[:, 512:640], ident[:])
                S.op("pe", f_trq, reads=["qr_k", "ident"] + (["qr_q"] if full else []), writes=["tpq"])
                if A_STEP < 11:
                    return
                S.op("act", lambda e: e.activation(out=kTc[:, e_idx, :], in_=tpq[:, 4, :], func=AF.Copy),
                     reads=["tpq"], writes=[("kT", e_idx)])
                if full:
                    qTt = qT[e_idx % 2]
                    S.op("act", lambda e: e.activation(out=qTt[:], in_=tpq[:, 0:4, :], func=AF.Copy),
                         reads=["tpq"], writes=["qT%d" % (e_idx % 2)])
                    us = u_sb[e_idx % 2]
                    S.op("act", lambda e: e.activation(out=us[:], in_=pj1[:], func=AF.Copy), reads=["bk1"],
                         writes=["u%d" % (e_idx % 2)])
                    vnt = vn[e_idx % 2]

                    S.op("act", lambda e: e.activation(out=sq[:], in_=pj2[:], func=AF.Square), reads=["bk2"], writes=["sq"])
                    S.op("act", lambda e: e.activation(out=cen[:], in_=pj2[:], func=AF.Copy), reads=["bk2"], writes=["cen"])

                    def f_ln1a(e):
                        e.tensor_reduce(out=st8[:, 0, :], in_=cen[:].rearrange("p (g d) -> p g d", d=64), axis=AX.X, op=ALU.add)
                        return e.tensor_reduce(out=st8[:, 2, :], in_=sq[:].rearrange("p (g d) -> p g d", d=64), axis=AX.X, op=ALU.add)

                    def f_ln1b(e):
                        e.tensor_scalar(out=st8[:, 0, :], in0=st8[:, 0, :], scalar1=1.0 / 64.0, scalar2=None, op0=ALU.mult)
                        return e.tensor_scalar(out=st8[:, 2, :], in0=st8[:, 2, :], scalar1=1.0 / 64.0, scalar2=None, op0=ALU.mult)
                    f_ln1 = [f_ln1a, f_ln1b,
                             lambda e: e.tensor_tensor(out=st8[:, 1, :], in0=st8[:, 0, :], in1=st8[:, 0, :], op=ALU.mult),
                             lambda e: e.tensor_tensor(out=st8[:, 2, :], in0=st8[:, 2, :], in1=st8[:, 1, :], op=ALU.subtract)]
                    S.op("dve", f_ln1, reads=["cen", "sq"], writes=["st8v", "st8n"])

                    f_ln3 = [lambda e: e.tensor_tensor(out=st8[:, 3, :], in0=st8[:, 2, :], in1=epsl8[:], op=ALU.add),
                             lambda e: e.tensor_tensor(out=st8[:, 3, :], in0=st8[:, 3, :], in1=mh8[:], op=ALU.pow)]
                    S.op("pool", f_ln3, reads=["st8v", "epsl8", "mh8"], writes=["st8r"])

                    S.op("dve", lambda e: e.scalar_tensor_tensor(out=st8[:, 1, :], in0=st8[:, 0, :], scalar=-1.0, in1=st8[:, 3, :],
                                                                   op0=ALU.mult, op1=ALU.mult),
                         reads=["st8r", "st8v"], writes=["st8n"])

                    def f_ln4a(e):
                        ins = None
                        for g in range(8):
                            ins = e.tensor_scalar(out=cen[:, g * 64:(g + 1) * 64], in0=cen[:, g * 64:(g + 1) * 64],
                                                  scalar1=st8[:, 3, g:g + 1], scalar2=st8[:, 1, g:g + 1], op0=ALU.mult, op1=ALU.add)
                        return ins
                    f_ln4 = [f_ln4a, lambda e: e.tensor_tensor(out=vnt[:], in0=cen[:], in1=vnorm_b[:], op=ALU.mult)]
                    S.op("dve", f_ln4, reads=["cen", "st8r", "st8n", "vnorm_b"], writes=["vn%d" % (e_idx % 2), "cen"], same_sync=True)
                    if e_idx == DBG_E:
                        dbgdump("d_u", us[:], "u%d" % (e_idx % 2))
                        dbgdump("d_vn", vnt[:], "vn%d" % (e_idx % 2))
                        dbgdump("d_qT", qTt[:].rearrange("p s t -> p (s t)"), "qT%d" % (e_idx % 2))

            def attn_stage(e_idx):
                qTt = qT[e_idx % 2]
                qk = "qT%d" % (e_idx % 2)
                us = u_sb[e_idx % 2]
                vnt = vn[e_idx % 2]
                items = [(kv, ci, kb) for kv in range(2) for ci, kb in enumerate([e_idx - 1, e_idx, e_idx + 1, NE, NE + 1])]

                def emit_qk(i):
                    kv, ci, kb = items[i]
                    sp_ = spsl[i % 2]

                    def f_qk(e):
                        ins = e.matmul(sp_[:], lhsT=kTc[kv * 64:(kv + 1) * 64, kb, :],
                                       rhs=qTt[kv * 64:(kv + 1) * 64, :, :].rearrange("p s t -> p (s t)"),
                                       start=True, stop=(ci not in (0, 2)))
                        if ci in (0, 2):
                            ins = e.matmul(sp_[:], lhsT=ident[:], rhs=trim[:, ci // 2, :], start=False, stop=True)
                        return ins
                    S.op("pe", f_qk, reads=[("kT", kb), qk, "ident", "trim"], writes=[spskey[i % 2]])
                    pt = pT[i % 3]
                    S.op("act", lambda e: e.activation(out=pt[:], in_=sp_[:], func=AF.Exp, scale=0.125, bias=kbias[:, kb:kb + 1]),
                         reads=[spskey[i % 2], "kbias"], writes=["pT%d" % (i % 3)])
                    if e_idx == DBG_E and i == 0:
                        dbgdump("d_pT", pt[:], "pT%d" % (i % 3))

                def emit_pv(i):
                    kv, ci, kb = items[i]
                    pt = pT[i % 3]

                    def f_pv(e):
                        ins = None
                        for hh in range(4):
                            ins = e.matmul(ops[kv][:, hh, :], lhsT=pt[:, hh * 128:(hh + 1) * 128], rhs=vc[:, kb, kv, :],
                                           start=(ci == 0 and hh == 0), stop=(ci == 4), skip_group_check=True)
                        return ins
                    S.op("pe", f_pv, reads=["pT%d" % (i % 3), ("vc", kb), "vc_init"], writes=[opskey[kv]])

                emit_qk(0)
                for i in range(len(items)):
                    if i + 1 < len(items):
                        emit_qk(i + 1)
                    emit_pv(i)
                for kv in range(2):
                    f_den = [lambda e, kv=kv: e.tensor_tensor(out=den[:, 0, kv * 4:(kv + 1) * 4], in0=ops[kv][:, :, 64],
                                                               in1=sinkexp[:, kv * 4:(kv + 1) * 4], op=ALU.add),
                             lambda e, kv=kv: e.reciprocal(out=den[:, 1, kv * 4:(kv + 1) * 4], in_=den[:, 0, kv * 4:(kv + 1) * 4])]
                    S.op("dve", f_den, reads=[opskey[kv], "sinkexp"], writes=[("den", kv)])

                    def f_norm(e, kv=kv):
                        ins = None
                        for hh in range(4):
                            c0 = kv * 256 + hh * 64
                            ins = e.tensor_scalar(out=mix[:, c0:c0 + 64], in0=ops[kv][:, hh, 0:64],
                                                  scalar1=den[:, 1, kv * 4 + hh:kv * 4 + hh + 1], scalar2=None, op0=ALU.mult)
                        return ins
                    S.op("dve", f_norm, reads=[opskey[kv], ("den", kv)], writes=["mix_a%d" % kv], same_sync=True)

                def f_gate(e):
                    ins = None
                    for g in range(8):
                        ins = e.matmul(gps[:, g * 64:(g + 1) * 64], lhsT=wsT_b[:, g, :], rhs=vnt[:, g * 64:(g + 1) * 64],
                                       start=True, stop=True)
                    return ins
                S.op("pe", f_gate, reads=["wsT_b", "vn%d" % (e_idx % 2)], writes=["bk2"])

                f_gev = [lambda e: e.tensor_tensor(out=gtmp[:], in0=gps[:], in1=bsx[:].rearrange("p g d -> p (g d)"), op=ALU.add),
                         lambda e: e.tensor_tensor(out=mix[:, 512:1024], in0=gtmp[:], in1=us[:], op=ALU.mult)]
                S.op("dve", f_gev, reads=["bk2", "bsx", "u%d" % (e_idx % 2)], writes=["mix_g", "gtmp"])

                def f_trm(e):
                    ins = None
                    for c in range(8):
                        ins = e.transpose(tm[:, c, :], mix[:, c * 128:(c + 1) * 128], ident[:])
                    return ins
                if e_idx == DBG_E:
                    dbgdump("d_mix", mix[:], "mix_g")
                S.op("pe", f_trm, reads=["mix_a0", "mix_a1", "mix_g", "ident"], writes=["tpq"])
                S.op("act", lambda e: e.activation(out=mixT[:], in_=tm[:], func=AF.Copy), reads=["tpq"], writes=["mixT"])
                for hf in range(2):
                    def f_y(e, hf=hf):
                        ins = None
                        for c in range(8):
                            ins = e.matmul(yps[hf][:], lhsT=mixT[:, c, :], rhs=w_out_b[:, c, hf * 512:(hf + 1) * 512],
                                           start=(c == 0), stop=(c == 7))
                        return ins
                    S.op("pe", f_y, reads=["mixT", "w_out_b"], writes=["bk%d" % hf])
                n = stage_of[e_idx]
                to = tmpo[e_idx % 2]
                postnorm(yps, ["bk0", "bk1"], ss2, "ss2", rs2[:, 0:1], "rs2", gtgt, xr[n % NXR], "xr%d" % (n % NXR), to,
                         "tmpo%d" % (e_idx % 2), scr, "scrA", x1[(e_idx - 1) * 128:e_idx * 128, :], ("x1", e_idx))

            stage_of = {}
            n = 0
            order = [NE, NE + 1] + list(range(NE))
            done_attn = 0
            if A_LIMIT is not None:
                order = order[:A_LIMIT[0]]
            def load_x(i):
                if i < len(order):
                    ei = order[i]
                    src = ctxin[(ei - NE) * 128:(ei - NE + 1) * 128, :] if ei >= NE else xin[ei * 128:(ei + 1) * 128, :]
                    dma("sp", xr[i % NXR][:], src, "xr%d" % (i % NXR), writes=["xr%d" % (i % NXR)])
            def pre_a(i):
                if i < len(order):
                    c8_ = i % 8
                    prenorm(xr[i % NXR], "xr%d" % (i % NXR), msr[:, c8_:c8_ + 1], ("ms", c8_), rsr[:, c8_:c8_ + 1], ("rs", c8_),
                            xn[i % 2], "xn%d" % (i % 2), tpq, "tpq", None, None, 0, 0, 0, scr, "scrA", part="a")
            load_x(0)
            load_x(1)
            pre_a(0)
            for e_idx in order:
                load_x(n + 2)
                stage_of[e_idx] = n
                proj_stage(e_idx, n)
                pre_a(n + 1)
                n += 1
                if e_idx < NE and e_idx >= 2 and (A_LIMIT is None or A_LIMIT[1]):
                    attn_stage(e_idx - 1)
            S.barrier()

        def ffn_phase(li, src, dst, nblk, gi, tagp):
            TT = 256
            ntile = nblk // 2
            with ExitStack() as pf:
                w1b = sb(pf, tagp + "w1b", [128, 8, DFF], BF16)
                w3b = sb(pf, tagp + "w3b", [128, 8, DFF], BF16)
                w2b = sb(pf, tagp + "w2b", [128, NJ, 1024], BF16)
                gtgt = sb(pf, tagp + "gtgt", [128, 1024], F32)
                NXR = 6
                xr = [sb(pf, tagp + "xr%d" % i, [128, 1024], F32) for i in range(NXR)]
                msr = sb(pf, tagp + "msr", [128, 8], F32)
                rsr = sb(pf, tagp + "rsr", [128, 8], F32)
                scr = sb(pf, tagp + "scr", [128, 1024], F32)
                xn = [sb(pf, tagp + "xn%d" % i, [128, 1024], BF16) for i in range(4)]
                hT = [sb(pf, tagp + "hT%d" % i, [128, 8, TT], BF16) for i in range(2)]
                sg = [sb(pf, tagp + "sg%d" % i, [128, TT], F32) for i in range(2)]
                act = sb(pf, tagp + "act", [128, NJ, TT], BF16)
                tmpo = [sb(pf, tagp + "tmpo%d" % i, [128, 1024], F32) for i in range(2)]
                ss2 = sb(pf, tagp + "ss2", [128, 2], F32)
                rs2 = sb(pf, tagp + "rs2", [128, 1], F32)
                tp = ps(pf, tagp + "tp", [128, 8, 128], BF16)
                gpsb = [ps(pf, tagp + "gps%d" % i, [128, 512]) for i in range(2)]
                upsb = [ps(pf, tagp + "ups%d" % i, [128, 512]) for i in range(2)]
                ypsb = [ps(pf, tagp + "yps%d" % i, [128, 512]) for i in range(3)]

                for k in range(8):
                    dma("pool", w1b[:, k, :], ffn_w1[li, k * 128:(k + 1) * 128, :], "w1b%d" % k, writes=[("w1b", k)])
                    dma("pool", w3b[:, k, :], ffn_w3[li, k * 128:(k + 1) * 128, :], "w3b%d" % k, writes=[("w3b", k)])
                for j0 in range(0, NJ, 2):
                    dma("pool", w2b[:, j0:j0 + 2, :], ffn_w2[li, j0 * 128:(j0 + 2) * 128, :].rearrange("(c p) n -> p c n", p=128),
                        "w2b%d" % j0, writes=[("w2b", j0)])
                dma("sp", gtgt[:], gtgscr[gi * 128:(gi + 1) * 128, :], "gtgt", reads=[("gtgscr", gi)], writes=["gtgt"])

                def pre(t, part="lab", bls=(0, 1)):
                    for bl in bls:
                        b = 2 * t + bl
                        xb = xr[b % NXR]
                        xk = "xr%d" % (b % NXR)
                        if "l" in part:
                            dma("sp", xb[:], src[b * 128:(b + 1) * 128, :], xk, reads=[("src", b)], writes=[xk])
                        h = hT[t % 2]
                        c8 = b % 8
                        prenorm(xb, xk, msr[:, c8:c8 + 1], ("ms", c8), rsr[:, c8:c8 + 1], ("rs", c8), xn[b % 4], "xn%d" % (b % 4),
                                tp, "tp", lambda k, h=h, bl=bl: h[:, k, bl * 128:(bl + 1) * 128], ("hT", t % 2, bl), li, 1, 0,
                                scr, "scr", part=part)

                yrot = 0
                pre(0)
                if ntile > 1:
                    pre(1, "l")
                for t in range(ntile):
                    if t + 2 < ntile:
                        pre(t + 2, "l")
                    if t + 1 < ntile:
                        pre(t + 1, "a")
                    h = hT[t % 2]
                    hkeys = [("hT", t % 2, 0), ("hT", t % 2, 1)]
                    for j in range(NJ):
                        if j == 7 and t + 1 < ntile:
                            pre(t + 1, "b", bls=(0,))
                        if j == 15 and t + 1 < ntile:
                            pre(t + 1, "b", bls=(1,))
                        g_ = gpsb[j % 2]
                        u_ = upsb[j % 2]

                        def f_gu(e, j=j, g_=g_, u_=u_, h=h):
                            ins = None
                            for k in range(8):
                                ins = e.matmul(g_[:, 0:TT], lhsT=w1b[:, k, j * 128:(j + 1) * 128], rhs=h[:, k, :],
                                               start=(k == 0), stop=(k == 7))
                            for k in range(8):
                                ins = e.matmul(u_[:, 0:TT], lhsT=w3b[:, k, j * 128:(j + 1) * 128], rhs=h[:, k, :],
                                               start=(k == 0), stop=(k == 7))
                            return ins
                        S.op("pe", f_gu, reads=hkeys + [("w1b", k) for k in range(8)] + [("w3b", k) for k in range(8)],
                             writes=["gps%d" % (j % 2), "ups%d" % (j % 2)])
                        sgt = sg[j % 2]
                        S.op("act", lambda e, g_=g_, sgt=sgt: e.activation(out=sgt[:], in_=g_[:, 0:TT], func=AF.Silu),
                             reads=["gps%d" % (j % 2)], writes=["sg%d" % (j % 2)])
                        S.op("dve", lambda e, j=j, u_=u_, sgt=sgt: e.tensor_tensor(out=act[:, j, :], in0=u_[:, 0:TT], in1=sgt[:],
                                                                                 op=ALU.mult),
                             reads=["ups%d" % (j % 2), "sg%d" % (j % 2)], writes=[("act", j)])
                    for bl in range(2):
                        b = 2 * t + bl
                        ybanks = []
                        ykeys = []
                        for hf in range(2):
                            yb = ypsb[yrot % 3]
                            yk = "yps%d" % (yrot % 3)
                            yrot += 1
                            ybanks.append(yb)
                            ykeys.append(yk)

                            def f_y(e, yb=yb, hf=hf, bl=bl):
                                ins = None
                                for j in range(NJ):
                                    ins = e.matmul(yb[:], lhsT=act[:, j, bl * 128:(bl + 1) * 128],
                                                   rhs=w2b[:, j, hf * 512:(hf + 1) * 512], start=(j == 0), stop=(j == NJ - 1))
                                return ins
                            S.op("pe", f_y, reads=[("act", j) for j in range(NJ)] + [("w2b", j0) for j0 in range(0, NJ, 2)], writes=[yk])
                        to = tmpo[b % 2]
                        postnorm(ybanks, ykeys, ss2, "ss2", rs2[:, 0:1], "rs2", gtgt, xr[b % NXR], "xr%d" % (b % NXR), to,
                                 "tmpo%d" % (b % 2), scr, "scr", dst[b * 128:(b + 1) * 128, :], ("dst", b))
                S.barrier()

        if "B" in PHASES:
            ffn_phase(0, x1, x2, NB + 2, 1, "B")

        with ExitStack() as pc_:
          if "C" in PHASES:
            scin = sb(pc_, "scin", [128, 8, 3072], BF16)
            scout = sb(pc_, "scout", [128, 8, 1024], BF16)
            cw = sb(pc_, "cw", [128, 8, 3], F32)
            cval = sb(pc_, "cval", [128, 2], F32)
            gtgt = sb(pc_, "gtgtC", [128, 1024], F32)
            NBX = NB + 2
            hTa = sb(pc_, "hTa", [128, 8, NBX * 128], BF16)
            NXR = 6
            xr = [sb(pc_, "xrC%d" % i, [128, 1024], F32) for i in range(NXR)]
            xres = [sb(pc_, "xresC%d" % i, [128, 1024], F32) for i in range(2)]
            msr = sb(pc_, "msrC", [128, 8], F32)
            rsr = sb(pc_, "rsrC", [128, 8], F32)
            scr = sb(pc_, "scrC", [128, 1024], F32)
            xn = [sb(pc_, "xnC%d" % i, [128, 1024], BF16) for i in range(4)]
            cgs = [sb(pc_, "cgs%d" % i, [128, 258], F32) for i in range(2)]
            yb_ = [sb(pc_, "ybC%d" % i, [128, 258], F32) for i in range(2)]
            t1_ = [sb(pc_, "t1C%d" % i, [128, 256], F32) for i in range(2)]
            bgs = [sb(pc_, "bgs%d" % i, [128, 256], F32) for i in range(2)]
            z = [sb(pc_, "zC%d" % i, [128, 8, 256], BF16) for i in range(2)]
            tmpo = [sb(pc_, "tmpoC%d" % i, [128, 1024], F32) for i in range(2)]
            ss2 = sb(pc_, "ss2C", [128, 2], F32)
            rs2 = sb(pc_, "rs2C", [128, 1], F32)
            tp = ps(pc_, "tpC", [128, 8, 128], BF16)
            cgp = [ps(pc_, "cgp%d" % i, [128, 512]) for i in range(2)]
            hxp = [ps(pc_, "hxp%d" % i, [128, 512]) for i in range(2)]
            bgp = ps(pc_, "bgp", [128, 512])
            ypsb = [ps(pc_, "ypsC%d" % i, [128, 512]) for i in range(2)]

            for k in range(8):
                dma("pool", scin[:, k, :], sc_w_in[k * 128:(k + 1) * 128, :], "scin%d" % k, writes=[("scin", k)])
            dma("pool", scout[:], sc_w_out.rearrange("(k p) n -> p k n", p=128), "scout", writes=["scout"])
            dma("sp", cw[:].rearrange("p c k -> p (c k)"), cwT, "cw", writes=["cw"])
            dma("sp", cval[:], cvalid_in, "cval", writes=["cval"])
            dma("sp", gtgt[:], gtgscr[2 * 128:3 * 128, :], "gtgt", reads=[("gtgscr", 2)], writes=["gtgt"])

            def preC(b, part="ab"):
                xb = xr[b % NXR]
                xk = "xrC%d" % (b % NXR)
                if "l" in part:
                    dma("sp", xb[:], x2[b * 128:(b + 1) * 128, :], xk, writes=[xk])
                c8 = b % 8
                prenorm(xb, xk, msr[:, c8:c8 + 1], ("ms", c8), rsr[:, c8:c8 + 1], ("rs", c8), xn[b % 4], "xnC%d" % (b % 4),
                        tp, "tpC", lambda k, b=b: hTa[:, k, b * 128:(b + 1) * 128], ("hTa", b), 1, 0, 0, scr, "scrC", part=part)

            nexta = 0
            nextb = 0
            ci_ = 0
            for b0 in range(6):
                preC(b0, "l")
            nextl = 6
            for b0 in range(4):
                preC(b0, "a")
            nexta = 4
            for t in range(NB // 2):
                needl = min(NBX, 2 * t + 8)
                while nextl < needl:
                    preC(nextl, "l")
                    nextl += 1
                for bl in range(2):
                    b = 2 * t + bl
                    dma("sp", xres[b % 2][:], x2[(b + 1) * 128:(b + 2) * 128, :], "xresC%d" % (b % 2), writes=["xresC%d" % (b % 2)])
                need = min(NBX, 2 * t + 4)
                while nextb < need:
                    preC(nextb, "b")
                    nextb += 1
                needa = min(NBX, 2 * t + 6)
                while nexta < needa:
                    preC(nexta, "a")
                    nexta += 1
                base = 128 + t * 256
                hk = [("hTa", 2 * t), ("hTa", 2 * t + 1), ("hTa", 2 * t + 2), ("hTa", 2 * t + 3)]
                zt = z[t % 2]
                for c in range(8):
                    if c in (2, 5) and nextb < min(NBX, 2 * t + 6):
                        preC(nextb, "b")
                        nextb += 1
                    pp = ci_ % 2
                    ci_ += 1

                    def f_c(e, c=c, pp=pp, base=base):
                        ins = None
                        for k in range(8):
                            ins = e.matmul(cgp[pp][:, 0:258], lhsT=scin[:, k, 1024 + c * 128:1024 + (c + 1) * 128],
                                           rhs=hTa[:, k, base - 1:base + 257], start=(k == 0), stop=(k == 7))
                        for k in range(8):
                            ins = e.matmul(hxp[pp][:, 0:258], lhsT=scin[:, k, 2048 + c * 128:2048 + (c + 1) * 128],
                                           rhs=hTa[:, k, base - 1:base + 257], start=(k == 0), stop=(k == 7))
                        return ins
                    S.op("pe", f_c, reads=hk + [("scin", k) for k in range(8)], writes=["cgp%d" % pp, "hxp%d" % pp])

                    def f_b(e, c=c, base=base):
                        ins = None
                        for k in range(8):
                            ins = e.matmul(bgp[:, 0:256], lhsT=scin[:, k, c * 128:(c + 1) * 128],
                                           rhs=hTa[:, k, base:base + 256], start=(k == 0), stop=(k == 7))
                        return ins
                    S.op("pe", f_b, reads=hk + [("scin", k) for k in range(8)], writes=["bgp"])
                    S.op("act", lambda e, pp=pp: e.activation(out=bgs[pp][:], in_=bgp[:, 0:256], func=AF.Copy),
                         reads=["bgp"], writes=["bgs%d" % pp])
                    S.op("act", lambda e, pp=pp: e.activation(out=cgs[pp][:], in_=cgp[pp][:, 0:258], func=AF.Copy),
                         reads=["cgp%d" % pp], writes=["cgs%d" % pp])

                    f_y1 = [lambda e, pp=pp: e.tensor_tensor(out=yb_[pp][:], in0=hxp[pp][:, 0:258], in1=cgs[pp][:], op=ALU.mult)]
                    if t == 0:
                        f_y1.append(lambda e, pp=pp: e.tensor_scalar(out=yb_[pp][:, 0:1], in0=yb_[pp][:, 0:1], scalar1=cval[:, 0:1],
                                                                    scalar2=None, op0=ALU.mult))
                    if t == NB // 2 - 1:
                        f_y1.append(lambda e, pp=pp: e.tensor_scalar(out=yb_[pp][:, 257:258], in0=yb_[pp][:, 257:258],
                                                                    scalar1=cval[:, 1:2], scalar2=None, op0=ALU.mult))
                    S.op("dve", f_y1, reads=["hxp%d" % pp, "cgs%d" % pp, "cval"], writes=["ybC%d" % pp])

                    f_cv = [lambda e, pp=pp, c=c: e.tensor_scalar(out=t1_[pp][:], in0=yb_[pp][:, 1:257], scalar1=cw[:, c, 1:2],
                                                                 scalar2=None, op0=ALU.mult),
                            lambda e, pp=pp, c=c: e.scalar_tensor_tensor(out=t1_[pp][:], in0=yb_[pp][:, 0:256], scalar=cw[:, c, 0:1],
                                                                        in1=t1_[pp][:], op0=ALU.mult, op1=ALU.add),
                            lambda e, pp=pp, c=c: e.scalar_tensor_tensor(out=t1_[pp][:], in0=yb_[pp][:, 2:258], scalar=cw[:, c, 2:3],
                                                                        in1=t1_[pp][:], op0=ALU.mult, op1=ALU.add)]
                    S.op("dve", f_cv, reads=["ybC%d" % pp, "cw"], writes=["t1C%d" % pp])
                    S.op("dve", lambda e, pp=pp, c=c, zt=zt: e.tensor_tensor(out=zt[:, c, :], in0=bgs[pp][:], in1=t1_[pp][:],
                                                                            op=ALU.mult),
                         reads=["bgs%d" % pp, "t1C%d" % pp], writes=[("z", t % 2, c)])
                for bl in range(2):
                    b = 2 * t + bl
                    for hf in range(2):
                        def f_yo(e, hf=hf, bl=bl, zt=zt):
                            ins = None
                            for c in range(8):
                                ins = e.matmul(ypsb[hf][:], lhsT=zt[:, c, bl * 128:(bl + 1) * 128],
                                               rhs=scout[:, c, hf * 512:(hf + 1) * 512], start=(c == 0), stop=(c == 7))
                            return ins
                        S.op("pe", f_yo, reads=[("z", t % 2, c) for c in range(8)] + ["scout"], writes=["ypsC%d" % hf])
                    xb = xres[b % 2]
                    xk = "xresC%d" % (b % 2)
                    to = tmpo[b % 2]
                    postnorm(ypsb, ["ypsC0", "ypsC1"], ss2, "ss2C", rs2[:, 0:1], "rs2C", gtgt, xb, xk, to, "tmpoC%d" % (b % 2),
                             scr, "scrC", x3[b * 128:(b + 1) * 128, :], ("x3", b))
            S.barrier()

        if "D" in PHASES:
            ffn_phase(1, x3, out, NB, 3, "D")

        S.finalize()
        last_dmas = list(S.dsem.values())
        block = es.enter_context(nc.Block())

        @block.tensor
        def _(e):
            S.emit("pe", e)

        @block.scalar
        def _(e):
            S.emit("act", e)

        @block.vector
        def _(e):
            S.emit("dve", e)

        @block.gpsimd
        def _(e):
            S.emit("pool", e)

        @block.sync
        def _(e):
            S.emit("sp", e)
            for sem, cnt in last_dmas:
                e.wait_ge(sem, cnt)
    return nc


def _host_prep(inputs):
    f32 = np.float32
    x = np.asarray(inputs["x"], f32)
    c = np.asarray(inputs["c"], f32)
    ctx = np.asarray(inputs["ctx"], f32)
    c_ctx = np.asarray(inputs["c_ctx"], f32)
    w_mod = np.ascontiguousarray(np.asarray(inputs["w_mod"], f32))
    b_mod = np.ascontiguousarray(np.asarray(inputs["b_mod"], f32))
    g_mix_pre = np.asarray(inputs["g_mix_pre"], f32)
    g_mix_post = np.asarray(inputs["g_mix_post"], f32)
    g_ffn_pre = np.asarray(inputs["g_ffn_pre"], f32)
    g_ffn_post = np.asarray(inputs["g_ffn_post"], f32)
    a_w_in = np.asarray(inputs["a_w_in"], f32)[0]
    qcols = np.concatenate([np.arange(h * 64, (h + 1) * 64) for h in (0, 4, 1, 5, 2, 6, 3, 7)])
    cols = np.concatenate([qcols, np.arange(768, 1280), np.arange(1280, 1792), np.arange(512, 640), np.arange(640, 768)])
    w_in = np.ascontiguousarray(a_w_in[:, cols])
    shared = {
        "w_mod": w_mod,
        "b_mod": b_mod,
        "b_modT": np.ascontiguousarray(b_mod.reshape(2, 48, 128).transpose(2, 0, 1).reshape(128, 96)),
        "gT": np.ascontiguousarray(np.stack([g_mix_pre, g_ffn_pre], axis=1).reshape(2, 2, 8, 128).transpose(3, 0, 1, 2).reshape(128, 32)),
        "g_post": np.ascontiguousarray(np.stack([g_mix_post, g_ffn_post], axis=1).reshape(4, D)),
        "w_in": w_in,
        "w_out": np.ascontiguousarray(np.asarray(inputs["a_w_out"], f32)[0]),
        "wsT": np.ascontiguousarray(np.asarray(inputs["gm_ws"], f32)[0].transpose(2, 0, 1).reshape(128, 1024)),
        "bsT": np.ascontiguousarray(np.asarray(inputs["gm_bs"], f32)[0].T),
        "vnorm": np.ascontiguousarray(np.asarray(inputs["gm_v_norm"], f32).reshape(1, 512)),
        "sink": np.ascontiguousarray(np.asarray(inputs["a_sink"], f32).reshape(1, 8)),
        "ident": np.eye(128, dtype=f32),
        "ffn_w1": np.ascontiguousarray(np.asarray(inputs["ffn_w1"], f32)),
        "ffn_w3": np.ascontiguousarray(np.asarray(inputs["ffn_w3"], f32)),
        "ffn_w2": np.ascontiguousarray(np.asarray(inputs["ffn_w2"], f32)),
        "sc_w_in": np.ascontiguousarray(np.asarray(inputs["sc_w_in"], f32)[0]),
        "sc_w_out": np.ascontiguousarray(np.asarray(inputs["sc_w_out"], f32)[0]),
        "cwT": np.ascontiguousarray(np.asarray(inputs["sc_conv"], f32)[0].reshape(3, 8, 128).transpose(2, 1, 0).reshape(128, 24)),
    }
    kj = np.arange(128)[:, None]
    qi = np.arange(128)[None, :]
    m_prev = np.where(kj >= qi, 0.0, -30000.0).astype(f32)
    m_next = np.where(kj <= qi, 0.0, -30000.0).astype(f32)
    shared["trimask"] = np.ascontiguousarray(np.concatenate([np.tile(m_prev, (1, 4)), np.tile(m_next, (1, 4))], axis=1))
    inv = (np.float32(10000.0) ** (-np.arange(16, dtype=f32) / np.float32(16))).astype(f32)
    in_maps = []
    for core in range(NCORES):
        b, qd = divmod(core, 4)
        t0 = qd * 4096 - 256
        xin = np.zeros((NE * 128, D), f32)
        lo, hi = max(t0, 0), min(t0 + NE * 128, SEQ)
        xin[lo - t0:hi - t0] = x[b, lo:hi]
        tpos = np.arange(t0, t0 + NE * 128)
        tpos = np.clip(tpos, 0, SEQ - 1)
        row = (tpos // 64).astype(f32)[:, None]
        col = (tpos % 64).astype(f32)[:, None]
        ar = (row * inv).astype(f32)
        ac = (col * inv).astype(f32)
        cr, sr, cc_, sc_ = np.cos(ar).astype(f32), np.sin(ar).astype(f32), np.cos(ac).astype(f32), np.sin(ac).astype(f32)
        ropeC = np.concatenate([cr, cr, cc_, cc_], axis=1)
        ropeS = np.concatenate([-sr, sr, -sc_, sc_], axis=1)
        kb = np.zeros((128, NE + 2), f32)
        for e in range(NE):
            gb = qd * 32 + e - 2
            if gb < 0 or gb >= SEQ // 128:
                kb[:, e] = -30000.0
        cv = np.zeros((128, 2), f32)
        cv[:, 0] = 1.0 if qd > 0 else 0.0
        cv[:, 1] = 1.0 if qd < 3 else 0.0
        cT = np.concatenate([c[b].reshape(8, 128).T, c_ctx.reshape(8, 128).T], axis=1)
        m = dict(shared)
        m.update({
            "xin": xin,
            "ctxin": np.ascontiguousarray(ctx[b]),
            "cT": np.ascontiguousarray(cT),
            "ropeC": np.ascontiguousarray(ropeC.reshape(NE, 128, 64).transpose(1, 0, 2).reshape(128, NE * 64)),
            "ropeS": np.ascontiguousarray(ropeS.reshape(NE, 128, 64).transpose(1, 0, 2).reshape(128, NE * 64)),
            "kbias": kb,
            "cvalid": cv,
        })
        in_maps.append(m)
    return in_maps


_NC_CACHE = {}


def kernel(**inputs):
    in_maps = _host_prep(inputs)
    if "nc" not in _NC_CACHE:
        _NC_CACHE["nc"] = build_nc()
    nc = _NC_CACHE["nc"]
    if PHASES != "MABCD":
        drop = set()
        if "M" not in PHASES:
            drop |= {"cT", "w_mod", "b_modT", "b_mod", "gT", "g_post"}
        if "B" not in PHASES and "D" not in PHASES:
            drop |= {"ffn_w1", "ffn_w3", "ffn_w2"}
        if "C" not in PHASES:
            drop |= {"sc_w_in", "sc_w_out", "cwT", "cvalid"}
        in_maps = [{k: v for k, v in m.items() if k not in drop} for m in in_maps]
    res = run_bass_kernel_spmd(nc, in_maps, core_ids=list(range(NCORES)))
    outs = [np.asarray(r["out"]) for r in res.results]
    full = np.stack([np.concatenate(outs[b * 4:(b + 1) * 4], axis=0) for b in range(2)], axis=0)
    if DEBUG:
        kernel.debug = res.results
    return full.astype(np.float32)
```

```python
import numpy as np
from contextlib import ExitStack
import concourse.bass as bass
import concourse.mybir as mybir
from concourse.bass_utils import run_bass_kernel_spmd

F32 = mybir.dt.float32
BF16 = mybir.dt.bfloat16
AF = mybir.ActivationFunctionType
ALU = mybir.AluOpType
AX = mybir.AxisListType

NCORES = 8
D = 1024
SEQ = 16384
NB = 32
NE = NB + 4
DFF = 2816
NJ = DFF // 128
DEBUG = False
PHASES = "MABCD"
A_LIMIT = None
A_STEP = 99
DBG_E = 6


import re
_PSUM_RE = re.compile(r"^(colps|rowps\d|tpq|tm|bk\d|tp|tpC|gps\d|ups\d|yps\d|ypsC\d|cgp\d|hxp\d|bgp)$")


class _Op:
    __slots__ = ("eng", "fn", "deps", "token", "is_dma", "signal", "seq", "same_sync")


class Sched:
    ENGS = ("pe", "act", "dve", "pool", "sp")

    def __init__(self, nc, es):
        self.nc = nc
        self.es = es
        self.ops = {e: [] for e in self.ENGS}
        self.esem = {e: es.enter_context(nc.semaphore("s_" + e)) for e in ("pe", "act", "dve", "pool")}
        self.csem = {e: es.enter_context(nc.semaphore("c_" + e)) for e in ("act", "dve", "pool")}
        self.ccnt = {e: 0 for e in ("act", "dve", "pool")}
        self.buf = {}
        self.dsem = {}
        self.pending = {e: [] for e in self.ENGS}
        self.dma_since_barrier = []
        self.nops = 0

    def op(self, eng, fn, reads=(), writes=(), dma_key=None, same_sync=False):
        o = _Op()
        o.same_sync = same_sync
        o.eng = eng
        o.fn = fn
        o.is_dma = dma_key is not None
        o.signal = False
        o.token = None
        o.seq = self.nops
        self.nops += 1
        deps = []
        reads = list(reads)
        writes = list(writes)
        for k in list(reads):
            if isinstance(k, str) and _PSUM_RE.match(k):
                reads.remove(k)
                if k not in writes:
                    writes.append(k)
        for k in reads:
            b = self.buf.setdefault(k, [None, []])
            if b[0] is not None:
                deps.append(b[0])
        for k in writes:
            b = self.buf.setdefault(k, [None, []])
            if b[0] is not None:
                deps.append(b[0])
            deps.extend(b[1])
        for k in reads:
            self.buf[k][1].append(o)
        for k in writes:
            b = self.buf[k]
            b[0] = o
            b[1] = []
        deps.extend(self.pending[eng])
        self.pending[eng] = []
        o.deps = [d for d in deps if d is not o]
        self.ops[eng].append(o)
        if o.is_dma:
            if dma_key not in self.dsem:
                self.dsem[dma_key] = [self.es.enter_context(self.nc.semaphore("d_" + str(len(self.dsem)))), 0]
            s = self.dsem[dma_key]
            s[1] += 16
            o.token = (s[0], s[1])
            self.dma_since_barrier.append(o)
        return o

    def barrier(self):
        toks = []
        for e in self.ENGS:
            if self.ops[e]:
                toks.append(self.ops[e][-1])
        toks.extend(self.dma_since_barrier)
        self.dma_since_barrier = []
        for e in self.ENGS:
            self.pending[e].extend(toks)
        self.buf = {}

    @staticmethod
    def _needs_sync(d, o):
        return d.eng != o.eng or d.is_dma or o.is_dma or o.same_sync or o.eng != "pe"

    def finalize(self):
        for e in self.ENGS:
            for o in self.ops[e]:
                for d in o.deps:
                    if self._needs_sync(d, o) and not d.is_dma:
                        d.signal = True
        for e in ("pe", "act", "dve", "pool", "sp"):
            c = 0
            for o in self.ops[e]:
                if not o.is_dma and o.signal:
                    assert e != "sp"
                    c += 1
                    o.token = (self.esem[e], c)

    def emit(self, eng, engine):
        waited = {}
        for o in self.ops[eng]:
            need = {}
            for d in o.deps:
                if self._needs_sync(d, o):
                    sem, v = d.token
                    k = id(sem)
                    if v > need.get(k, (None, 0))[1]:
                        need[k] = (sem, v)
            for k, (sem, v) in need.items():
                if waited.get(k, 0) < v:
                    engine.wait_ge(sem, v)
                    waited[k] = v
            if isinstance(o.fn, (list, tuple)):
                ins = None
                for i, f in enumerate(o.fn):
                    ins = f(engine)
                    if i < len(o.fn) - 1:
                        self.ccnt[eng] += 1
                        ins.then_inc(self.csem[eng], 1)
                        engine.wait_ge(self.csem[eng], self.ccnt[eng])
            else:
                ins = o.fn(engine)
            if o.is_dma:
                ins.then_inc(o.token[0], 16)
            elif o.signal:
                ins.then_inc(o.token[0], 1)

    def final_wait(self, eng, engine, ops):
        for o in ops:
            engine.wait_ge(o.token[0], o.token[1])


def build_nc():
    nc = bass.Bass("TRN2", target_bir_lowering=False)
    dk = "ExternalOutput" if DEBUG else "Internal"

    def din(name, shape):
        return nc.dram_tensor(name, list(shape), F32, kind="ExternalInput").ap()

    xin = din("xin", [NE * 128, D])
    ctxin = din("ctxin", [256, D])
    if "M" in PHASES:
        cT = din("cT", [128, 16])
        w_mod = din("w_mod", [2, D, 6 * D])
        b_modT = din("b_modT", [128, 96])
        b_mod = din("b_mod", [2, 6 * D])
        gT = din("gT", [128, 32])
        g_post = din("g_post", [4, D])
    w_in = din("w_in", [D, 1792])
    w_out = din("w_out", [D, D])
    wsT = din("wsT", [128, 8 * 128])
    bsT = din("bsT", [128, 8])
    vnorm = din("vnorm", [1, 512])
    sink = din("sink", [1, 8])
    ropeC = din("ropeC", [128, NE * 64])
    ropeS = din("ropeS", [128, NE * 64])
    kbias_in = din("kbias", [128, NE + 2])
    trimask_in = din("trimask", [128, 1024])
    ident_in = din("ident", [128, 128])
    if "C" in PHASES:
        cvalid_in = din("cvalid", [128, 2])
    if "B" in PHASES or "D" in PHASES:
        ffn_w1 = din("ffn_w1", [2, D, DFF])
        ffn_w3 = din("ffn_w3", [2, D, DFF])
        ffn_w2 = din("ffn_w2", [2, DFF, D])
    if "C" in PHASES:
        sc_w_in = din("sc_w_in", [D, 3 * D])
        sc_w_out = din("sc_w_out", [D, D])
        cwT = din("cwT", [128, 24])
    out = nc.dram_tensor("out", [NB * 128, D], F32, kind="ExternalOutput").ap()
    x1 = nc.dram_tensor("x1", [(NB + 2) * 128, D], F32, kind=dk).ap()
    x2 = nc.dram_tensor("x2", [(NB + 2) * 128, D], F32, kind=dk).ap()
    x3 = nc.dram_tensor("x3", [NB * 128, D], F32, kind=dk).ap()
    gtgscr = nc.dram_tensor("gtgscr", [4 * 128, D], F32, kind=dk).ap()
    dbg_modc = nc.dram_tensor("dbg_modc", [128, 128], F32, kind=dk).ap()
    dbgt = {}
    if DEBUG:
        for nm, shp, dt_ in [("d_hT", [128, 1024], BF16), ("d_qks", [128, 640], F32), ("d_qr", [128, 640], BF16),
                             ("d_u", [128, 512], F32), ("d_vn", [128, 512], BF16), ("d_mix", [128, 1024], BF16),
                             ("d_qT", [128, 512], BF16), ("d_pT", [128, 512], BF16), ]:
            dbgt[nm] = nc.dram_tensor(nm, shp, dt_, kind="ExternalOutput").ap()

    with ExitStack() as es:
        S = Sched(nc, es)

        def sb(stack, name, shape, dt):
            return stack.enter_context(nc.sbuf_tensor("sb_" + name, list(shape), dt))

        def ps(stack, name, shape, dt=F32):
            return stack.enter_context(nc.psum_tensor("ps_" + name, list(shape), dt))

        def dma(eng, out_ap, in_ap, key, reads=(), writes=()):
            return S.op(eng, lambda e: e.dma_start(out=out_ap, in_=in_ap), reads=reads, writes=writes, dma_key=key)

        def dbgdump(name, ap2d, key):
            if DEBUG:
                dma("sp", dbgt[name], ap2d, "dbg_" + name, reads=[key])

        ident = sb(es, "ident", [128, 128], BF16)
        modc = sb(es, "modc", [128, 2, 2, 2, 8, 2], F32)
        consts = sb(es, "consts", [128, 8], F32)
        mh8 = sb(es, "mh8", [128, 8], F32)
        epsl8 = sb(es, "epsl8", [128, 8], F32)

        dma("pool", ident[:], ident_in, "ident", writes=["ident"])
        S.op("pool", lambda e: e.memset(consts[:, 0:1], 1e-6), writes=["consts"])
        S.op("pool", lambda e: e.memset(consts[:, 1:2], -0.5), writes=["consts"])
        S.op("pool", lambda e: e.memset(mh8[:], -0.5), writes=["mh8"])
        S.op("pool", lambda e: e.memset(epsl8[:], 1e-5), writes=["epsl8"])

        def rstd_ops(ss, ncols, rs, key_ss, key_rs):
            if ncols == 2:
                f = [lambda e: e.tensor_tensor(out=rs, in0=ss[:, 0:1], in1=ss[:, 1:2], op=ALU.add),
                     lambda e: e.tensor_tensor(out=rs, in0=rs, in1=consts[:, 0:1], op=ALU.add),
                     lambda e: e.tensor_tensor(out=rs, in0=rs, in1=consts[:, 1:2], op=ALU.pow)]
            else:
                f = [lambda e: e.tensor_tensor(out=rs, in0=ss[:, 0:1], in1=consts[:, 0:1], op=ALU.add),
                     lambda e: e.tensor_tensor(out=rs, in0=rs, in1=consts[:, 1:2], op=ALU.pow)]
            S.op("pool", f, reads=[key_ss, "consts"], writes=[key_rs])

        with ExitStack() as pm:
          if "M" in PHASES:
            cTs = sb(pm, "cTs", [128, 16], F32)
            sil = sb(pm, "sil", [128, 16], F32)
            srep = sb(pm, "srep", [128, 8, 128], F32)
            svec = sb(pm, "svec", [128, 8, 2], F32)
            bmT = sb(pm, "bmT", [128, 2, 48], F32)
            gTs = sb(pm, "gTs", [128, 2, 2, 8], F32)
            modT = sb(pm, "modT", [128, 2, 4, 8, 2], F32)
            wm = [sb(pm, "wm%d" % i, [128, 8, 1024], F32) for i in range(3)]
            brow = [sb(pm, "brow%d" % i, [128, 1024], F32) for i in range(2)]
            grow = [sb(pm, "grow%d" % i, [128, 1024], F32) for i in range(2)]
            gtg = [sb(pm, "gtgm%d" % i, [128, 1024], F32) for i in range(2)]
            colps = ps(pm, "colps", [128, 8, 2])
            rowps = [ps(pm, "rowps%d" % i, [128, 512]) for i in range(2)]

            dma("sp", cTs[:], cT, "cTs", writes=["cTs"])
            dma("sp", bmT[:].rearrange("p a b -> p (a b)"), b_modT, "bmT", writes=["bmT"])
            dma("sp", gTs[:].rearrange("p a b c -> p (a b c)"), gT, "gTs", writes=["gTs"])
            S.op("act", lambda e: e.activation(out=sil[:], in_=cTs[:], func=AF.Silu), reads=["cTs"], writes=["sil"])
            S.op("dve", lambda e: e.tensor_copy(out=srep[:], in_=sil[:, 0:8].unsqueeze(2).to_broadcast([128, 8, 128])),
                 reads=["sil"], writes=["srep"])

            def f_svec(e):
                e.tensor_copy(out=svec[:, :, 0], in_=sil[:, 0:8])
                return e.tensor_copy(out=svec[:, :, 1], in_=sil[:, 8:16])
            S.op("dve", f_svec, reads=["sil"], writes=["svec"])

            pi = 0
            ri = 0
            pieces = [(i, pc) for i in range(2) for pc in range(6)]

            def load_piece(q):
                if q < len(pieces):
                    i_, pc_ = pieces[q]
                    dma("sp", wm[q % 3][:], w_mod[i_, :, pc_ * 1024:(pc_ + 1) * 1024].rearrange("(k p) n -> p k n", p=128),
                        "wm%d" % (q % 3), writes=["wm%d" % (q % 3)])
            load_piece(0)
            load_piece(1)
            for i in range(2):
                for pc in range(6):
                    wmb = wm[pi % 3]
                    wk = "wm%d" % (pi % 3)
                    load_piece(pi + 2)
                    pi += 1
                    if pc in (0, 1, 3, 4):
                        kind = {0: 0, 1: 1, 3: 2, 4: 3}[pc]

                        def f_col(e, wmb=wmb):
                            ins = None
                            for oc in range(8):
                                for k in range(8):
                                    ins = e.matmul(colps[:, oc, :], lhsT=wmb[:, k, oc * 128:(oc + 1) * 128],
                                                   rhs=svec[:, k, :], start=(k == 0), stop=(k == 7))
                            return ins
                        S.op("pe", f_col, reads=[wk, "svec"], writes=["colps"])
                        S.op("dve", lambda e, i=i, kind=kind, pc=pc: e.tensor_tensor(
                            out=modT[:, i, kind], in0=colps[:],
                            in1=bmT[:, i, pc * 8:(pc + 1) * 8].unsqueeze(2).to_broadcast([128, 8, 2]), op=ALU.add),
                            reads=["colps", "bmT"], writes=["modT"])
                    else:
                        which = 0 if pc == 2 else 1
                        r = ri % 2
                        ri += 1
                        dma("sp", brow[r][:], b_mod[i:i + 1, pc * 1024:(pc + 1) * 1024].partition_broadcast(128),
                            "brow%d" % r, writes=["brow%d" % r])
                        dma("sp", grow[r][:], g_post[2 * i + which:2 * i + which + 1, :].partition_broadcast(128),
                            "grow%d" % r, writes=["grow%d" % r])
                        for hf in range(2):
                            def f_row(e, wmb=wmb, hf=hf):
                                ins = None
                                for k in range(8):
                                    ins = e.matmul(rowps[hf][:], lhsT=srep[:, k, :],
                                                   rhs=wmb[:, k, hf * 512:(hf + 1) * 512], start=(k == 0), stop=(k == 7))
                                return ins
                            S.op("pe", f_row, reads=[wk, "srep"], writes=["rowps%d" % hf])

                            sl_ = slice(hf * 512, (hf + 1) * 512)
                            f_rowev = [lambda e, r=r, hf=hf, sl=sl_: e.tensor_tensor(out=gtg[r][:, sl], in0=rowps[hf][:], in1=brow[r][:, sl], op=ALU.add),
                                       lambda e, r=r, hf=hf, sl=sl_: e.tensor_tensor(out=gtg[r][:, sl], in0=gtg[r][:, sl], in1=grow[r][:, sl], op=ALU.mult)]
                            S.op("dve", f_rowev, reads=["rowps%d" % hf, "brow%d" % r, "grow%d" % r], writes=["gtg%d" % r])
                        gi = 2 * i + which
                        dma("sp", gtgscr[gi * 128:(gi + 1) * 128, :], gtg[r][:], "gtg%d" % r,
                            reads=["gtg%d" % r], writes=[("gtgscr", gi)])

            def f_modc(e):
                ins = None
                for i in range(2):
                    for kd in range(2):
                        e.scalar_tensor_tensor(out=modc[:, i, kd, 0], in0=modT[:, i, 2 * kd + 1], scalar=1.0,
                                               in1=gTs[:, i, kd].unsqueeze(2).to_broadcast([128, 8, 2]),
                                               op0=ALU.add, op1=ALU.mult)
                        ins = e.tensor_copy(out=modc[:, i, kd, 1], in_=modT[:, i, 2 * kd])
                return ins
            S.op("dve", f_modc, reads=["modT", "gTs"], writes=["modc"])
            if DEBUG:
                dma("sp", dbg_modc, modc[:].rearrange("p a b c d e -> p (a b c d e)"), "dbgmodc", reads=["modc"])
            S.barrier()

        def prenorm(xb, xkey, ms, mskey, rs, rskey, xn, xnkey, tp, tpkey, hT_dst, hkey, li, kd, vec, scr, scrkey, part="ab"):
            if "a" in part:
                S.op("act", lambda e: e.activation(out=scr[:], in_=xb[:], func=AF.Square, scale=1.0 / 32.0, accum_out=ms),
                     reads=[xkey], writes=[scrkey, mskey])
                rstd_ops(ms, 1, rs, mskey, rskey)
                S.op("dve", lambda e: e.tensor_scalar(out=xn[:], in0=xb[:], scalar1=rs, scalar2=None, op0=ALU.mult),
                     reads=[xkey, rskey], writes=[xnkey])
            if "b" not in part:
                return

            def f_tr(e):
                ins = None
                for k in range(8):
                    ins = e.transpose(tp[:, k, :], xn[:, k * 128:(k + 1) * 128], ident[:])
                return ins
            S.op("pe", f_tr, reads=[xnkey, "ident"], writes=[tpkey])
            if A_STEP < 6:
                return

            def f_mod(e):
                ins = None
                for k in range(8):
                    ins = e.activation(out=hT_dst(k), in_=tp[:, k, :], func=AF.Identity,
                                       scale=modc[:, li, kd, 0, k, vec:vec + 1], bias=modc[:, li, kd, 1, k, vec:vec + 1])
                return ins
            S.op("act", f_mod, reads=[tpkey, "modc"], writes=[hkey])

        def postnorm(yps, ykeys, ss2, sskey, rs, rskey, gtgt, xb, xkey, tmp, tmpkey, scr, scrkey, dst_ap, dstkey):
            def f_sq(e):
                e.activation(out=scr[:, 0:512], in_=yps[0][:], func=AF.Square, scale=1.0 / 32.0, accum_out=ss2[:, 0:1])
                return e.activation(out=scr[:, 512:1024], in_=yps[1][:], func=AF.Square, scale=1.0 / 32.0,
                                    accum_out=ss2[:, 1:2])
            S.op("act", f_sq, reads=list(ykeys), writes=[scrkey, sskey])
            rstd_ops(ss2, 2, rs, sskey, rskey)

            def f_pn1(e):
                ins = None
                for hf in range(2):
                    sl = slice(hf * 512, (hf + 1) * 512)
                    ins = e.scalar_tensor_tensor(out=tmp[:, sl], in0=yps[hf][:], scalar=rs, in1=gtgt[:, sl],
                                                 op0=ALU.mult, op1=ALU.mult)
                return ins
            f_pn = [f_pn1, lambda e: e.tensor_tensor(out=tmp[:], in0=tmp[:], in1=xb[:], op=ALU.add)]
            S.op("dve", f_pn, reads=list(ykeys) + [rskey, "gtgt", xkey], writes=[tmpkey])
            dma("sp", dst_ap, tmp[:], tmpkey, reads=[tmpkey], writes=[dstkey])

        with ExitStack() as pa:
          if "A" in PHASES:
            w_in_b = sb(pa, "w_in_b", [128, 8, 1792], BF16)
            w_out_b = sb(pa, "w_out_b", [128, 8, 1024], BF16)
            wsT_b = sb(pa, "wsT_b", [128, 8, 128], BF16)
            bsT_s = sb(pa, "bsT_s", [128, 8], F32)
            vnorm_b = sb(pa, "vnorm_b", [128, 512], F32)
            sinkexp = sb(pa, "sinkexp", [128, 8], F32)
            rC = sb(pa, "rC", [128, NE, 64], F32)
            rS = sb(pa, "rS", [128, NE, 64], F32)
            kbias = sb(pa, "kbias_s", [128, NE + 2], F32)
            trim = sb(pa, "trim", [128, 2, 512], BF16)
            gtgt = sb(pa, "gtgt", [128, 1024], F32)
            kTc = sb(pa, "kTc", [128, NE + 2, 128], BF16)
            vc = sb(pa, "vc", [128, NE + 2, 2, 65], BF16)
            NXR = 5
            xr = [sb(pa, "xr%d" % i, [128, 1024], F32) for i in range(NXR)]
            msr = sb(pa, "msr", [128, 8], F32)
            rsr = sb(pa, "rsr", [128, 8], F32)
            scr = sb(pa, "scrA", [128, 1024], F32)
            xn = [sb(pa, "xn%d" % i, [128, 1024], BF16) for i in range(2)]
            hT = [sb(pa, "hT%d" % i, [128, 8, 128], BF16) for i in range(2)]
            ccx = sb(pa, "ccx", [128, 8, 64], F32)
            ssx = sb(pa, "ssx", [128, 8, 64], F32)
            qks = sb(pa, "qks", [128, 640], F32)
            bsx = sb(pa, "bsx", [128, 8, 64], F32)
            rt1 = sb(pa, "rt1", [128, 640], F32)
            rt2 = sb(pa, "rt2", [128, 640], F32)
            qr = sb(pa, "qr", [128, 640], BF16)
            qT = [sb(pa, "qT%d" % i, [128, 4, 128], BF16) for i in range(2)]
            u_sb = [sb(pa, "u_sb%d" % i, [128, 512], F32) for i in range(2)]
            cen = sb(pa, "cen", [128, 512], F32)
            sq = sb(pa, "sq", [128, 512], F32)
            st8 = sb(pa, "st8", [128, 4, 8], F32)
            vn = [sb(pa, "vn%d" % i, [128, 512], BF16) for i in range(2)]
            pT = [sb(pa, "pT%d" % i, [128, 512], BF16) for i in range(3)]
            den = sb(pa, "den", [128, 2, 8], F32)
            mix = sb(pa, "mix", [128, 1024], BF16)
            gtmp = sb(pa, "gtmp", [128, 512], F32)
            mixT = sb(pa, "mixT", [128, 8, 128], BF16)
            tmpo = [sb(pa, "tmpo%d" % i, [128, 1024], F32) for i in range(2)]
            ss2 = sb(pa, "ss2", [128, 2], F32)
            rs2 = sb(pa, "rs2", [128, 1], F32)
            tpq = ps(pa, "tpq", [128, 8, 128], BF16)
            tm = tpq
            bk = [ps(pa, "bkA%d" % i, [128, 512]) for i in range(7)]
            pj0, pj1, pj2, pj3, spsb, ops0b, spsb2 = bk
            spsl = [spsb, spsb2]
            spskey = ["bk4", "bk6"]
            yps = [pj0, pj1]
            gps = pj2
            ops = [ops0b[:, 0:260].rearrange("p (h d) -> p h d", d=65), pj3[:, 0:260].rearrange("p (h d) -> p h d", d=65)]
            opskey = ["bk5", "bk3"]

            dma("pool", w_in_b[:], w_in.rearrange("(k p) n -> p k n", p=128), "w_in_b", writes=["w_in_b"])
            dma("pool", wsT_b[:].rearrange("p g i -> p (g i)"), wsT, "wsT_b", writes=["wsT_b"])
            dma("pool", trim[:].rearrange("p a n -> p (a n)"), trimask_in, "trim", writes=["trim"])
            dma("pool", w_out_b[:], w_out.rearrange("(k p) n -> p k n", p=128), "w_out_b", writes=["w_out_b"])
            dma("sp", bsT_s[:], bsT, "bsT_s", writes=["bsT_s"])
            dma("sp", vnorm_b[:], vnorm.partition_broadcast(128), "vnorm_b", writes=["vnorm_b"])
            dma("sp", sinkexp[:], sink.partition_broadcast(128), "sinkexp", writes=["sinkexp"])
            dma("sp", rC[:].rearrange("p e f -> p (e f)"), ropeC, "rC", writes=["rC"])
            dma("sp", rS[:].rearrange("p e f -> p (e f)"), ropeS, "rS", writes=["rS"])
            dma("sp", kbias[:], kbias_in, "kbias", writes=["kbias"])
            dma("sp", gtgt[:], gtgscr[0:128, :], "gtgt", reads=[("gtgscr", 0)], writes=["gtgt"])
            S.op("act", lambda e: e.activation(out=sinkexp[:], in_=sinkexp[:], func=AF.Exp), reads=["sinkexp"], writes=["sinkexp"])
            S.op("dve", lambda e: e.memset(vc[:].rearrange("p e k d -> p (e k d)"), 1.0), writes=["vc_init"])
            S.op("pool", lambda e: e.tensor_copy(out=bsx[:], in_=bsT_s[:].unsqueeze(2).to_broadcast([128, 8, 64])),
                 reads=["bsT_s"], writes=["bsx"])

            def proj_stage(e_idx, n):
                is_ctx = e_idx >= NE
                full = (not is_ctx) and (1 <= e_idx <= NE - 2)
                xb = xr[n % NXR]
                xk = "xr%d" % (n % NXR)
                c8 = n % 8
                h = hT[n % 2]
                hk = "hT%d" % (n % 2)
                prenorm(xb, xk, msr[:, c8:c8 + 1], ("ms", c8), rsr[:, c8:c8 + 1], ("rs", c8), xn[n % 2], "xn%d" % (n % 2),
                        tpq, "tpq", lambda k: h[:, k, :], hk, 0, 0, 1 if is_ctx else 0, scr, "scrA", part="b")
                if e_idx == DBG_E:
                    dbgdump("d_hT", h[:].rearrange("p k t -> p (k t)"), hk)
                if A_STEP < 7:
                    return
                groups = [(3, 1024 + 512, 256)]
                if full:
                    groups = [(0, 0, 512), (1, 512, 512), (2, 1024, 512), (3, 1536, 256)]

                def f_proj(e):
                    ins = None
                    for (b, c0, w) in groups:
                        for k in range(8):
                            ins = e.matmul(bk[b][:, 0:w], lhsT=h[:, k, :], rhs=w_in_b[:, k, c0:c0 + w],
                                           start=(k == 0), stop=(k == 7))
                    return ins
                S.op("pe", f_proj, reads=[hk, "w_in_b"], writes=["bk%d" % g[0] for g in groups])
                if A_STEP < 8:
                    return
                S.op("act", lambda e: e.activation(out=vc[:, e_idx, :, 0:64],
                                                   in_=pj3[:, 128:256].rearrange("p (k d) -> p k d", d=64), func=AF.Copy),
                     reads=["bk3", "vc_init"], writes=[("vc", e_idx)])
                if A_STEP < 9:
                    return
                if is_ctx:
                    S.op("dve", lambda e: e.tensor_copy(out=qr[:, 512:640], in_=pj3[:, 0:128]), reads=["bk3"], writes=["qr_k"])
                else:
                    def f_exp(e):
                        e.tensor_copy(out=ccx[:], in_=rC[:, e_idx, :].unsqueeze(1).to_broadcast([128, 8, 64]))
                        return e.tensor_copy(out=ssx[:], in_=rS[:, e_idx, :].unsqueeze(1).to_broadcast([128, 8, 64]))
                    S.op("pool", f_exp, reads=["rC", "rS"], writes=["ccx"])

                    def f_cp(e):
                        ins = e.activation(out=qks[:, 512:640], in_=pj3[:, 0:128], func=AF.Copy)
                        if full:
                            ins = e.activation(out=qks[:, 0:512], in_=pj0[:], func=AF.Copy)
                        return ins
                    S.op("act", f_cp, reads=["bk3"] + (["bk0"] if full else []), writes=["qks"])

                    segs = [(512, 640, 2)] + ([(0, 512, 8)] if full else [])

                    def f_rope1(e):
                        ins = None
                        for (c0, c1, nh) in segs:
                            e.tensor_tensor(out=rt1[:, c0:c1], in0=qks[:, c0:c1],
                                            in1=ccx[:, 0:nh, :].rearrange("p h f -> p (h f)"), op=ALU.mult)
                            s5 = qks[:, c0:c1].rearrange("p (h a b f) -> p h a b f", a=2, b=2, f=16)
                            t5 = rt2[:, c0:c1].rearrange("p (h a b f) -> p h a b f", a=2, b=2, f=16)
                            x5 = ssx[:, 0:nh, :].rearrange("p h (a b f) -> p h a b f", a=2, b=2, f=16)
                            for ab in range(2):
                                ins = e.tensor_tensor(out=t5[:, :, :, ab, :], in0=s5[:, :, :, 1 - ab, :], in1=x5[:, :, :, ab, :],
                                                      op=ALU.mult)
                        return ins

                    def f_rope2(e):
                        ins = None
                        for (c0, c1, nh) in segs:
                            ins = e.tensor_tensor(out=qr[:, c0:c1], in0=rt1[:, c0:c1], in1=rt2[:, c0:c1], op=ALU.add)
                        return ins
                    f_rope = [f_rope1, f_rope2]
                    S.op("dve", f_rope, reads=["qks", "ccx"], writes=["qr_k", "qr_q", "rt_k"])
                    if e_idx == DBG_E:
                        dbgdump("d_qks", qks[:], "qks")
                        dbgdump("d_qr", qr[:], "qr_k")
                if A_STEP < 10:
                    return
                nq = 4 if full else 0

                def f_trq(e):
                    ins = None
                    for s_ in range(nq):
                        ins = e.transpose(tpq[:, s_, :], qr[:, s_ * 128:(s_ + 1) * 128], ident[:])
                    return e.transpose(tpq[:, 4, :], qr[:, 512:640], ident[:])
                S.op("pe", f_trq, reads=["qr_k", "ident"] + (["qr_q"] if full else []), writes=["tpq"])
                if A_STEP < 11:
                    return
                S.op("act", lambda e: e.activation(out=kTc[:, e_idx, :], in_=tpq[:, 4, :], func=AF.Copy),
                     reads=["tpq"], writes=[("kT", e_idx)])
                if full:
                    qTt = qT[e_idx % 2]
                    S.op("act", lambda e: e.activation(out=qTt[:], in_=tpq[:, 0:4, :], func=AF.Copy),
                         reads=["tpq"], writes=["qT%d" % (e_idx % 2)])
                    us = u_sb[e_idx % 2]
                    S.op("act", lambda e: e.activation(out=us[:], in_=pj1[:], func=AF.Copy), reads=["bk1"],
                         writes=["u%d" % (e_idx % 2)])
                    vnt = vn[e_idx % 2]

                    S.op("act", lambda e: e.activation(out=sq[:], in_=pj2[:], func=AF.Square), reads=["bk2"], writes=["sq"])
                    S.op("act", lambda e: e.activation(out=cen[:], in_=pj2[:], func=AF.Copy), reads=["bk2"], writes=["cen"])

                    def f_ln1a(e):
                        e.tensor_reduce(out=st8[:, 0, :], in_=cen[:].rearrange("p (g d) -> p g d", d=64), axis=AX.X, op=ALU.add)
                        return e.tensor_reduce(out=st8[:, 2, :], in_=sq[:].rearrange("p (g d) -> p g d", d=64), axis=AX.X, op=ALU.add)

                    def f_ln1b(e):
                        e.tensor_scalar(out=st8[:, 0, :], in0=st8[:, 0, :], scalar1=1.0 / 64.0, scalar2=None, op0=ALU.mult)
                        return e.tensor_scalar(out=st8[:, 2, :], in0=st8[:, 2, :], scalar1=1.0 / 64.0, scalar2=None, op0=ALU.mult)
                    f_ln1 = [f_ln1a, f_ln1b,
                             lambda e: e.tensor_tensor(out=st8[:, 1, :], in0=st8[:, 0, :], in1=st8[:, 0, :], op=ALU.mult),
                             lambda e: e.tensor_tensor(out=st8[:, 2, :], in0=st8[:, 2, :], in1=st8[:, 1, :], op=ALU.subtract)]
                    S.op("dve", f_ln1, reads=["cen", "sq"], writes=["st8v", "st8n"])

                    f_ln3 = [lambda e: e.tensor_tensor(out=st8[:, 3, :], in0=st8[:, 2, :], in1=epsl8[:], op=ALU.add),
                             lambda e: e.tensor_tensor(out=st8[:, 3, :], in0=st8[:, 3, :], in1=mh8[:], op=ALU.pow)]
                    S.op("pool", f_ln3, reads=["st8v", "epsl8", "mh8"], writes=["st8r"])

                    S.op("dve", lambda e: e.scalar_tensor_tensor(out=st8[:, 1, :], in0=st8[:, 0, :], scalar=-1.0, in1=st8[:, 3, :],
                                                                   op0=ALU.mult, op1=ALU.mult),
                         reads=["st8r", "st8v"], writes=["st8n"])

                    def f_ln4a(e):
                        ins = None
                        for g in range(8):
                            ins = e.tensor_scalar(out=cen[:, g * 64:(g + 1) * 64], in0=cen[:, g * 64:(g + 1) * 64],
                                                  scalar1=st8[:, 3, g:g + 1], scalar2=st8[:, 1, g:g + 1], op0=ALU.mult, op1=ALU.add)
                        return ins
                    f_ln4 = [f_ln4a, lambda e: e.tensor_tensor(out=vnt[:], in0=cen[:], in1=vnorm_b[:], op=ALU.mult)]
                    S.op("dve", f_ln4, reads=["cen", "st8r", "st8n", "vnorm_b"], writes=["vn%d" % (e_idx % 2), "cen"], same_sync=True)
                    if e_idx == DBG_E:
                        dbgdump("d_u", us[:], "u%d" % (e_idx % 2))
                        dbgdump("d_vn", vnt[:], "vn%d" % (e_idx % 2))
                        dbgdump("d_qT", qTt[:].rearrange("p s t -> p (s t)"), "qT%d" % (e_idx % 2))

            def attn_stage(e_idx):
                qTt = qT[e_idx % 2]
                qk = "qT%d" % (e_idx % 2)
                us = u_sb[e_idx % 2]
                vnt = vn[e_idx % 2]
                items = [(kv, ci, kb) for kv in range(2) for ci, kb in enumerate([e_idx - 1, e_idx, e_idx + 1, NE, NE + 1])]

                def emit_qk(i):
                    kv, ci, kb = items[i]
                    sp_ = spsl[i % 2]

                    def f_qk(e):
                        ins = e.matmul(sp_[:], lhsT=kTc[kv * 64:(kv + 1) * 64, kb, :],
                                       rhs=qTt[kv * 64:(kv + 1) * 64, :, :].rearrange("p s t -> p (s t)"),
                                       start=True, stop=(ci not in (0, 2)))
                        if ci in (0, 2):
                            ins = e.matmul(sp_[:], lhsT=ident[:], rhs=trim[:, ci // 2, :], start=False, stop=True)
                        return ins
                    S.op("pe", f_qk, reads=[("kT", kb), qk, "ident", "trim"], writes=[spskey[i % 2]])
                    pt = pT[i % 3]
                    S.op("act", lambda e: e.activation(out=pt[:], in_=sp_[:], func=AF.Exp, scale=0.125, bias=kbias[:, kb:kb + 1]),
                         reads=[spskey[i % 2], "kbias"], writes=["pT%d" % (i % 3)])
                    if e_idx == DBG_E and i == 0:
                        dbgdump("d_pT", pt[:], "pT%d" % (i % 3))

                def emit_pv(i):
                    kv, ci, kb = items[i]
                    pt = pT[i % 3]

                    def f_pv(e):
                        ins = None
                        for hh in range(4):
                            ins = e.matmul(ops[kv][:, hh, :], lhsT=pt[:, hh * 128:(hh + 1) * 128], rhs=vc[:, kb, kv, :],
                                           start=(ci == 0 and hh == 0), stop=(ci == 4), skip_group_check=True)
                        return ins
                    S.op("pe", f_pv, reads=["pT%d" % (i % 3), ("vc", kb), "vc_init"], writes=[opskey[kv]])

                emit_qk(0)
                for i in range(len(items)):
                    if i + 1 < len(items):
                        emit_qk(i + 1)
                    emit_pv(i)
                for kv in range(2):
                    f_den = [lambda e, kv=kv: e.tensor_tensor(out=den[:, 0, kv * 4:(kv + 1) * 4], in0=ops[kv][:, :, 64],
                                                               in1=sinkexp[:, kv * 4:(kv + 1) * 4], op=ALU.add),
                             lambda e, kv=kv: e.reciprocal(out=den[:, 1, kv * 4:(kv + 1) * 4], in_=den[:, 0, kv * 4:(kv + 1) * 4])]
                    S.op("dve", f_den, reads=[opskey[kv], "sinkexp"], writes=[("den", kv)])

                    def f_norm(e, kv=kv):
                        ins = None
                        for hh in range(4):
                            c0 = kv * 256 + hh * 64
                            ins = e.tensor_scalar(out=mix[:, c0:c0 + 64], in0=ops[kv][:, hh, 0:64],
                                                  scalar1=den[:, 1, kv * 4 + hh:kv * 4 + hh + 1], scalar2=None, op0=ALU.mult)
                        return ins
                    S.op("dve", f_norm, reads=[opskey[kv], ("den", kv)], writes=["mix_a%d" % kv], same_sync=True)

                def f_gate(e):
                    ins = None
                    for g in range(8):
                        ins = e.matmul(gps[:, g * 64:(g + 1) * 64], lhsT=wsT_b[:, g, :], rhs=vnt[:, g * 64:(g + 1) * 64],
                                       start=True, stop=True)
                    return ins
                S.op("pe", f_gate, reads=["wsT_b", "vn%d" % (e_idx % 2)], writes=["bk2"])

                f_gev = [lambda e: e.tensor_tensor(out=gtmp[:], in0=gps[:], in1=bsx[:].rearrange("p g d -> p (g d)"), op=ALU.add),
                         lambda e: e.tensor_tensor(out=mix[:, 512:1024], in0=gtmp[:], in1=us[:], op=ALU.mult)]
                S.op("dve", f_gev, reads=["bk2", "bsx", "u%d" % (e_idx % 2)], writes=["mix_g", "gtmp"])

                def f_trm(e):
                    ins = None
                    for c in range(8):
                        ins = e.transpose(tm[:, c, :], mix[:, c * 128:(c + 1) * 128], ident[:])
                    return ins
                if e_idx == DBG_E:
                    dbgdump("d_mix", mix[:], "mix_g")
                S.op("pe", f_trm, reads=["mix_a0", "mix_a1", "mix_g", "ident"], writes=["tpq"])
                S.op("act", lambda e: e.activation(out=mixT[:], in_=tm[:], func=AF.Copy), reads=["tpq"], writes=["mixT"])
                for hf in range(2):
                    def f_y(e, hf=hf):
                        ins = None
                        for c in range(8):
                            ins = e.matmul(yps[hf][:], lhsT=mixT[:, c, :], rhs=w_out_b[:, c, hf * 512:(hf + 1) * 512],
                                           start=(c == 0), stop=(c == 7))
                        return ins
                    S.op("pe", f_y, reads=["mixT", "w_out_b"], writes=["bk%d" % hf])
                n = stage_of[e_idx]
                to = tmpo[e_idx % 2]
                postnorm(yps, ["bk0", "bk1"], ss2, "ss2", rs2[:, 0:1], "rs2", gtgt, xr[n % NXR], "xr%d" % (n % NXR), to,
                         "tmpo%d" % (e_idx % 2), scr, "scrA", x1[(e_idx - 1) * 128:e_idx * 128, :], ("x1", e_idx))

            stage_of = {}
            n = 0
            order = [NE, NE + 1] + list(range(NE))
            done_attn = 0
            if A_LIMIT is not None:
                order = order[:A_LIMIT[0]]
            def load_x(i):
                if i < len(order):
                    ei = order[i]
                    src = ctxin[(ei - NE) * 128:(ei - NE + 1) * 128, :] if ei >= NE else xin[ei * 128:(ei + 1) * 128, :]
                    dma("sp", xr[i % NXR][:], src, "xr%d" % (i % NXR), writes=["xr%d" % (i % NXR)])
            def pre_a(i):
                if i < len(order):
                    c8_ = i % 8
                    prenorm(xr[i % NXR], "xr%d" % (i % NXR), msr[:, c8_:c8_ + 1], ("ms", c8_), rsr[:, c8_:c8_ + 1], ("rs", c8_),
                            xn[i % 2], "xn%d" % (i % 2), tpq, "tpq", None, None, 0, 0, 0, scr, "scrA", part="a")
            load_x(0)
            load_x(1)
            pre_a(0)
            for e_idx in order:
                load_x(n + 2)
                stage_of[e_idx] = n
                proj_stage(e_idx, n)
                pre_a(n + 1)
                n += 1
                if e_idx < NE and e_idx >= 2 and (A_LIMIT is None or A_LIMIT[1]):
                    attn_stage(e_idx - 1)
            S.barrier()

        def ffn_phase(li, src, dst, nblk, gi, tagp):
            TT = 256
            ntile = nblk // 2
            with ExitStack() as pf:
                w1b = sb(pf, tagp + "w1b", [128, 8, DFF], BF16)
                w3b = sb(pf, tagp + "w3b", [128, 8, DFF], BF16)
                w2b = sb(pf, tagp + "w2b", [128, NJ, 1024], BF16)
                gtgt = sb(pf, tagp + "gtgt", [128, 1024], F32)
                NXR = 6
                xr = [sb(pf, tagp + "xr%d" % i, [128, 1024], F32) for i in range(NXR)]
                msr = sb(pf, tagp + "msr", [128, 8], F32)
                rsr = sb(pf, tagp + "rsr", [128, 8], F32)
                scr = sb(pf, tagp + "scr", [128, 1024], F32)
                xn = [sb(pf, tagp + "xn%d" % i, [128, 1024], BF16) for i in range(4)]
                hT = [sb(pf, tagp + "hT%d" % i, [128, 8, TT], BF16) for i in range(2)]
                sg = [sb(pf, tagp + "sg%d" % i, [128, TT], F32) for i in range(2)]
                act = sb(pf, tagp + "act", [128, NJ, TT], BF16)
                tmpo = [sb(pf, tagp + "tmpo%d" % i, [128, 1024], F32) for i in range(2)]
                ss2 = sb(pf, tagp + "ss2", [128, 2], F32)
                rs2 = sb(pf, tagp + "rs2", [128, 1], F32)
                tp = ps(pf, tagp + "tp", [128, 8, 128], BF16)
                gpsb = [ps(pf, tagp + "gps%d" % i, [128, 512]) for i in range(2)]
                upsb = [ps(pf, tagp + "ups%d" % i, [128, 512]) for i in range(2)]
                ypsb = [ps(pf, tagp + "yps%d" % i, [128, 512]) for i in range(3)]

                for k in range(8):
                    dma("pool", w1b[:, k, :], ffn_w1[li, k * 128:(k + 1) * 128, :], "w1b%d" % k, writes=[("w1b", k)])
                    dma("pool", w3b[:, k, :], ffn_w3[li, k * 128:(k + 1) * 128, :], "w3b%d" % k, writes=[("w3b", k)])
                for j0 in range(0, NJ, 2):
                    dma("pool", w2b[:, j0:j0 + 2, :], ffn_w2[li, j0 * 128:(j0 + 2) * 128, :].rearrange("(c p) n -> p c n", p=128),
                        "w2b%d" % j0, writes=[("w2b", j0)])
                dma("sp", gtgt[:], gtgscr[gi * 128:(gi + 1) * 128, :], "gtgt", reads=[("gtgscr", gi)], writes=["gtgt"])

                def pre(t, part="lab", bls=(0, 1)):
                    for bl in bls:
                        b = 2 * t + bl
                        xb = xr[b % NXR]
                        xk = "xr%d" % (b % NXR)
                        if "l" in part:
                            dma("sp", xb[:], src[b * 128:(b + 1) * 128, :], xk, reads=[("src", b)], writes=[xk])
                        h = hT[t % 2]
                        c8 = b % 8
                        prenorm(xb, xk, msr[:, c8:c8 + 1], ("ms", c8), rsr[:, c8:c8 + 1], ("rs", c8), xn[b % 4], "xn%d" % (b % 4),
                                tp, "tp", lambda k, h=h, bl=bl: h[:, k, bl * 128:(bl + 1) * 128], ("hT", t % 2, bl), li, 1, 0,
                                scr, "scr", part=part)

                yrot = 0
                pre(0)
                if ntile > 1:
                    pre(1, "l")
                for t in range(ntile):
                    if t + 2 < ntile:
                        pre(t + 2, "l")
                    h = hT[t % 2]
                    hkeys = [("hT", t % 2, 0), ("hT", t % 2, 1)]
                    for j in range(NJ):
                        if j == 2 and t + 1 < ntile:
                            pre(t + 1, "a")
                        if j == 7 and t + 1 < ntile:
                            pre(t + 1, "b", bls=(0,))
                        if j == 15 and t + 1 < ntile:
                            pre(t + 1, "b", bls=(1,))
                        g_ = gpsb[j % 2]
                        u_ = upsb[j % 2]

                        def f_gu(e, j=j, g_=g_, u_=u_, h=h):
                            ins = None
                            for k in range(8):
                                ins = e.matmul(g_[:, 0:TT], lhsT=w1b[:, k, j * 128:(j + 1) * 128], rhs=h[:, k, :],
                                               start=(k == 0), stop=(k == 7))
                            for k in range(8):
                                ins = e.matmul(u_[:, 0:TT], lhsT=w3b[:, k, j * 128:(j + 1) * 128], rhs=h[:, k, :],
                                               start=(k == 0), stop=(k == 7))
                            return ins
                        S.op("pe", f_gu, reads=hkeys + [("w1b", k) for k in range(8)] + [("w3b", k) for k in range(8)],
                             writes=["gps%d" % (j % 2), "ups%d" % (j % 2)])
                        sgt = sg[j % 2]
                        S.op("act", lambda e, g_=g_, sgt=sgt: e.activation(out=sgt[:], in_=g_[:, 0:TT], func=AF.Silu),
                             reads=["gps%d" % (j % 2)], writes=["sg%d" % (j % 2)])
                        S.op("dve", lambda e, j=j, u_=u_, sgt=sgt: e.tensor_tensor(out=act[:, j, :], in0=u_[:, 0:TT], in1=sgt[:],
                                                                                 op=ALU.mult),
                             reads=["ups%d" % (j % 2), "sg%d" % (j % 2)], writes=[("act", j)])
                    for bl in range(2):
                        b = 2 * t + bl
                        ybanks = []
                        ykeys = []
                        for hf in range(2):
                            yb = ypsb[yrot % 3]
                            yk = "yps%d" % (yrot % 3)
                            yrot += 1
                            ybanks.append(yb)
                            ykeys.append(yk)

                            def f_y(e, yb=yb, hf=hf, bl=bl):
                                ins = None
                                for j in range(NJ):
                                    ins = e.matmul(yb[:], lhsT=act[:, j, bl * 128:(bl + 1) * 128],
                                                   rhs=w2b[:, j, hf * 512:(hf + 1) * 512], start=(j == 0), stop=(j == NJ - 1))
                                return ins
                            S.op("pe", f_y, reads=[("act", j) for j in range(NJ)] + [("w2b", j0) for j0 in range(0, NJ, 2)], writes=[yk])
                        to = tmpo[b % 2]
                        postnorm(ybanks, ykeys, ss2, "ss2", rs2[:, 0:1], "rs2", gtgt, xr[b % NXR], "xr%d" % (b % NXR), to,
                                 "tmpo%d" % (b % 2), scr, "scr", dst[b * 128:(b + 1) * 128, :], ("dst", b))
                S.barrier()

        if "B" in PHASES:
            ffn_phase(0, x1, x2, NB + 2, 1, "B")

        with ExitStack() as pc_:
          if "C" in PHASES:
            scin = sb(pc_, "scin", [128, 8, 3072], BF16)
            scout = sb(pc_, "scout", [128, 8, 1024], BF16)
            cw = sb(pc_, "cw", [128, 8, 3], F32)
            cval = sb(pc_, "cval", [128, 2], F32)
            gtgt = sb(pc_, "gtgtC", [128, 1024], F32)
            NBX = NB + 2
            hTa = sb(pc_, "hTa", [128, 8, NBX * 128], BF16)
            NXR = 6
            xr = [sb(pc_, "xrC%d" % i, [128, 1024], F32) for i in range(NXR)]
            xres = [sb(pc_, "xresC%d" % i, [128, 1024], F32) for i in range(2)]
            msr = sb(pc_, "msrC", [128, 8], F32)
            rsr = sb(pc_, "rsrC", [128, 8], F32)
            scr = sb(pc_, "scrC", [128, 1024], F32)
            xn = [sb(pc_, "xnC%d" % i, [128, 1024], BF16) for i in range(4)]
            cgs = [sb(pc_, "cgs%d" % i, [128, 258], F32) for i in range(2)]
            yb_ = [sb(pc_, "ybC%d" % i, [128, 258], F32) for i in range(2)]
            t1_ = [sb(pc_, "t1C%d" % i, [128, 256], F32) for i in range(2)]
            bgs = [sb(pc_, "bgs%d" % i, [128, 256], F32) for i in range(2)]
            z = [sb(pc_, "zC%d" % i, [128, 8, 256], BF16) for i in range(2)]
            tmpo = [sb(pc_, "tmpoC%d" % i, [128, 1024], F32) for i in range(2)]
            ss2 = sb(pc_, "ss2C", [128, 2], F32)
            rs2 = sb(pc_, "rs2C", [128, 1], F32)
            tp = ps(pc_, "tpC", [128, 8, 128], BF16)
            cgp = [ps(pc_, "cgp%d" % i, [128, 512]) for i in range(2)]
            hxp = [ps(pc_, "hxp%d" % i, [128, 512]) for i in range(2)]
            bgp = ps(pc_, "bgp", [128, 512])
            ypsb = [ps(pc_, "ypsC%d" % i, [128, 512]) for i in range(2)]

            for k in range(8):
                dma("pool", scin[:, k, :], sc_w_in[k * 128:(k + 1) * 128, :], "scin%d" % k, writes=[("scin", k)])
            dma("pool", scout[:], sc_w_out.rearrange("(k p) n -> p k n", p=128), "scout", writes=["scout"])
            dma("sp", cw[:].rearrange("p c k -> p (c k)"), cwT, "cw", writes=["cw"])
            dma("sp", cval[:], cvalid_in, "cval", writes=["cval"])
            dma("sp", gtgt[:], gtgscr[2 * 128:3 * 128, :], "gtgt", reads=[("gtgscr", 2)], writes=["gtgt"])

            def preC(b, part="ab"):
                xb = xr[b % NXR]
                xk = "xrC%d" % (b % NXR)
                if "l" in part:
                    dma("sp", xb[:], x2[b * 128:(b + 1) * 128, :], xk, writes=[xk])
                c8 = b % 8
                prenorm(xb, xk, msr[:, c8:c8 + 1], ("ms", c8), rsr[:, c8:c8 + 1], ("rs", c8), xn[b % 4], "xnC%d" % (b % 4),
                        tp, "tpC", lambda k, b=b: hTa[:, k, b * 128:(b + 1) * 128], ("hTa", b), 1, 0, 0, scr, "scrC", part=part)

            nexta = 0
            nextb = 0
            ci_ = 0
            for b0 in range(6):
                preC(b0, "l")
            nextl = 6
            for b0 in range(4):
                preC(b0, "a")
            nexta = 4
            for t in range(NB // 2):
                needl = min(NBX, 2 * t + 8)
                while nextl < needl:
                    preC(nextl, "l")
                    nextl += 1
                for bl in range(2):
                    b = 2 * t + bl
                    dma("sp", xres[b % 2][:], x2[(b + 1) * 128:(b + 2) * 128, :], "xresC%d" % (b % 2), writes=["xresC%d" % (b % 2)])
                need = min(NBX, 2 * t + 4)
                while nextb < need:
                    preC(nextb, "b")
                    nextb += 1
                needa = min(NBX, 2 * t + 6)
                while nexta < needa:
                    preC(nexta, "a")
                    nexta += 1
                base = 128 + t * 256
                hk = [("hTa", 2 * t), ("hTa", 2 * t + 1), ("hTa", 2 * t + 2), ("hTa", 2 * t + 3)]
                zt = z[t % 2]
                for c in range(8):
                    if c in (2, 5) and nextb < min(NBX, 2 * t + 6):
                        preC(nextb, "b")
                        nextb += 1
                    pp = ci_ % 2
                    ci_ += 1

                    def f_c(e, c=c, pp=pp, base=base):
                        ins = None
                        for k in range(8):
                            ins = e.matmul(cgp[pp][:, 0:258], lhsT=scin[:, k, 1024 + c * 128:1024 + (c + 1) * 128],
                                           rhs=hTa[:, k, base - 1:base + 257], start=(k == 0), stop=(k == 7))
                        for k in range(8):
                            ins = e.matmul(hxp[pp][:, 0:258], lhsT=scin[:, k, 2048 + c * 128:2048 + (c + 1) * 128],
                                           rhs=hTa[:, k, base - 1:base + 257], start=(k == 0), stop=(k == 7))
                        return ins
                    S.op("pe", f_c, reads=hk + [("scin", k) for k in range(8)], writes=["cgp%d" % pp, "hxp%d" % pp])

                    def f_b(e, c=c, base=base):
                        ins = None
                        for k in range(8):
                            ins = e.matmul(bgp[:, 0:256], lhsT=scin[:, k, c * 128:(c + 1) * 128],
                                           rhs=hTa[:, k, base:base + 256], start=(k == 0), stop=(k == 7))
                        return ins
                    S.op("pe", f_b, reads=hk + [("scin", k) for k in range(8)], writes=["bgp"])
                    S.op("act", lambda e, pp=pp: e.activation(out=bgs[pp][:], in_=bgp[:, 0:256], func=AF.Copy),
                         reads=["bgp"], writes=["bgs%d" % pp])
                    S.op("act", lambda e, pp=pp: e.activation(out=cgs[pp][:], in_=cgp[pp][:, 0:258], func=AF.Copy),
                         reads=["cgp%d" % pp], writes=["cgs%d" % pp])

                    f_y1 = [lambda e, pp=pp: e.tensor_tensor(out=yb_[pp][:], in0=hxp[pp][:, 0:258], in1=cgs[pp][:], op=ALU.mult)]
                    if t == 0:
                        f_y1.append(lambda e, pp=pp: e.tensor_scalar(out=yb_[pp][:, 0:1], in0=yb_[pp][:, 0:1], scalar1=cval[:, 0:1],
                                                                    scalar2=None, op0=ALU.mult))
                    if t == NB // 2 - 1:
                        f_y1.append(lambda e, pp=pp: e.tensor_scalar(out=yb_[pp][:, 257:258], in0=yb_[pp][:, 257:258],
                                                                    scalar1=cval[:, 1:2], scalar2=None, op0=ALU.mult))
                    S.op("dve", f_y1, reads=["hxp%d" % pp, "cgs%d" % pp, "cval"], writes=["ybC%d" % pp])

                    f_cv = [lambda e, pp=pp, c=c: e.tensor_scalar(out=t1_[pp][:], in0=yb_[pp][:, 1:257], scalar1=cw[:, c, 1:2],
                                                                 scalar2=None, op0=ALU.mult),
                            lambda e, pp=pp, c=c: e.scalar_tensor_tensor(out=t1_[pp][:], in0=yb_[pp][:, 0:256], scalar=cw[:, c, 0:1],
                                                                        in1=t1_[pp][:], op0=ALU.mult, op1=ALU.add),
                            lambda e, pp=pp, c=c: e.scalar_tensor_tensor(out=t1_[pp][:], in0=yb_[pp][:, 2:258], scalar=cw[:, c, 2:3],
                                                                        in1=t1_[pp][:], op0=ALU.mult, op1=ALU.add)]
                    S.op("dve", f_cv, reads=["ybC%d" % pp, "cw"], writes=["t1C%d" % pp])
                    S.op("dve", lambda e, pp=pp, c=c, zt=zt: e.tensor_tensor(out=zt[:, c, :], in0=bgs[pp][:], in1=t1_[pp][:],
                                                                            op=ALU.mult),
                         reads=["bgs%d" % pp, "t1C%d" % pp], writes=[("z", t % 2, c)])
                for bl in range(2):
                    b = 2 * t + bl
                    for hf in range(2):
                        def f_yo(e, hf=hf, bl=bl, zt=zt):
                            ins = None
                            for c in range(8):
                                ins = e.matmul(ypsb[hf][:], lhsT=zt[:, c, bl * 128:(bl + 1) * 128],
                                               rhs=scout[:, c, hf * 512:(hf + 1) * 512], start=(c == 0), stop=(c == 7))
                            return ins
                        S.op("pe", f_yo, reads=[("z", t % 2, c) for c in range(8)] + ["scout"], writes=["ypsC%d" % hf])
                    xb = xres[b % 2]
                    xk = "xresC%d" % (b % 2)
                    to = tmpo[b % 2]
                    postnorm(ypsb, ["ypsC0", "ypsC1"], ss2, "ss2C", rs2[:, 0:1], "rs2C", gtgt, xb, xk, to, "tmpoC%d" % (b % 2),
                             scr, "scrC", x3[b * 128:(b + 1) * 128, :], ("x3", b))
            S.barrier()

        if "D" in PHASES:
            ffn_phase(1, x3, out, NB, 3, "D")

        S.finalize()
        last_dmas = list(S.dsem.values())
        block = es.enter_context(nc.Block())

        @block.tensor
        def _(e):
            S.emit("pe", e)

        @block.scalar
        def _(e):
            S.emit("act", e)

        @block.vector
        def _(e):
            S.emit("dve", e)

        @block.gpsimd
        def _(e):
            S.emit("pool", e)

        @block.sync
        def _(e):
            S.emit("sp", e)
            for sem, cnt in last_dmas:
                e.wait_ge(sem, cnt)
    return nc


def _host_prep(inputs):
    f32 = np.float32
    x = np.asarray(inputs["x"], f32)
    c = np.asarray(inputs["c"], f32)
    ctx = np.asarray(inputs["ctx"], f32)
    c_ctx = np.asarray(inputs["c_ctx"], f32)
    w_mod = np.ascontiguousarray(np.asarray(inputs["w_mod"], f32))
    b_mod = np.ascontiguousarray(np.asarray(inputs["b_mod"], f32))
    g_mix_pre = np.asarray(inputs["g_mix_pre"], f32)
    g_mix_post = np.asarray(inputs["g_mix_post"], f32)
    g_ffn_pre = np.asarray(inputs["g_ffn_pre"], f32)
    g_ffn_post = np.asarray(inputs["g_ffn_post"], f32)
    a_w_in = np.asarray(inputs["a_w_in"], f32)[0]
    qcols = np.concatenate([np.arange(h * 64, (h + 1) * 64) for h in (0, 4, 1, 5, 2, 6, 3, 7)])
    cols = np.concatenate([qcols, np.arange(768, 1280), np.arange(1280, 1792), np.arange(512, 640), np.arange(640, 768)])
    w_in = np.ascontiguousarray(a_w_in[:, cols])
    shared = {
        "w_mod": w_mod,
        "b_mod": b_mod,
        "b_modT": np.ascontiguousarray(b_mod.reshape(2, 48, 128).transpose(2, 0, 1).reshape(128, 96)),
        "gT": np.ascontiguousarray(np.stack([g_mix_pre, g_ffn_pre], axis=1).reshape(2, 2, 8, 128).transpose(3, 0, 1, 2).reshape(128, 32)),
        "g_post": np.ascontiguousarray(np.stack([g_mix_post, g_ffn_post], axis=1).reshape(4, D)),
        "w_in": w_in,
        "w_out": np.ascontiguousarray(np.asarray(inputs["a_w_out"], f32)[0]),
        "wsT": np.ascontiguousarray(np.asarray(inputs["gm_ws"], f32)[0].transpose(2, 0, 1).reshape(128, 1024)),
        "bsT": np.ascontiguousarray(np.asarray(inputs["gm_bs"], f32)[0].T),
        "vnorm": np.ascontiguousarray(np.asarray(inputs["gm_v_norm"], f32).reshape(1, 512)),
        "sink": np.ascontiguousarray(np.asarray(inputs["a_sink"], f32).reshape(1, 8)),
        "ident": np.eye(128, dtype=f32),
        "ffn_w1": np.ascontiguousarray(np.asarray(inputs["ffn_w1"], f32)),
        "ffn_w3": np.ascontiguousarray(np.asarray(inputs["ffn_w3"], f32)),
        "ffn_w2": np.ascontiguousarray(np.asarray(inputs["ffn_w2"], f32)),
        "sc_w_in": np.ascontiguousarray(np.asarray(inputs["sc_w_in"], f32)[0]),
        "sc_w_out": np.ascontiguousarray(np.asarray(inputs["sc_w_out"], f32)[0]),
        "cwT": np.ascontiguousarray(np.asarray(inputs["sc_conv"], f32)[0].reshape(3, 8, 128).transpose(2, 1, 0).reshape(128, 24)),
    }
    kj = np.arange(128)[:, None]
    qi = np.arange(128)[None, :]
    m_prev = np.where(kj >= qi, 0.0, -30000.0).astype(f32)
    m_next = np.where(kj <= qi, 0.0, -30000.0).astype(f32)
    shared["trimask"] = np.ascontiguousarray(np.concatenate([np.tile(m_prev, (1, 4)), np.tile(m_next, (1, 4))], axis=1))
    inv = (np.float32(10000.0) ** (-np.arange(16, dtype=f32) / np.float32(16))).astype(f32)
    in_maps = []
    for core in range(NCORES):
        b, qd = divmod(core, 4)
        t0 = qd * 4096 - 256
        xin = np.zeros((NE * 128, D), f32)
        lo, hi = max(t0, 0), min(t0 + NE * 128, SEQ)
        xin[lo - t0:hi - t0] = x[b, lo:hi]
        tpos = np.arange(t0, t0 + NE * 128)
        tpos = np.clip(tpos, 0, SEQ - 1)
        row = (tpos // 64).astype(f32)[:, None]
        col = (tpos % 64).astype(f32)[:, None]
        ar = (row * inv).astype(f32)
        ac = (col * inv).astype(f32)
        cr, sr, cc_, sc_ = np.cos(ar).astype(f32), np.sin(ar).astype(f32), np.cos(ac).astype(f32), np.sin(ac).astype(f32)
        ropeC = np.concatenate([cr, cr, cc_, cc_], axis=1)
        ropeS = np.concatenate([-sr, sr, -sc_, sc_], axis=1)
        kb = np.zeros((128, NE + 2), f32)
        for e in range(NE):
            gb = qd * 32 + e - 2
            if gb < 0 or gb >= SEQ // 128:
                kb[:, e] = -30000.0
        cv = np.zeros((128, 2), f32)
        cv[:, 0] = 1.0 if qd > 0 else 0.0
        cv[:, 1] = 1.0 if qd < 3 else 0.0
        cT = np.concatenate([c[b].reshape(8, 128).T, c_ctx.reshape(8, 128).T], axis=1)
        m = dict(shared)
        m.update({
            "xin": xin,
            "ctxin": np.ascontiguousarray(ctx[b]),
            "cT": np.ascontiguousarray(cT),
            "ropeC": np.ascontiguousarray(ropeC.reshape(NE, 128, 64).transpose(1, 0, 2).reshape(128, NE * 64)),
            "ropeS": np.ascontiguousarray(ropeS.reshape(NE, 128, 64).transpose(1, 0, 2).reshape(128, NE * 64)),
            "kbias": kb,
            "cvalid": cv,
        })
        in_maps.append(m)
    return in_maps


_NC_CACHE = {}


def kernel(**inputs):
    in_maps = _host_prep(inputs)
    if "nc" not in _NC_CACHE:
        _NC_CACHE["nc"] = build_nc()
    nc = _NC_CACHE["nc"]
    if PHASES != "MABCD":
        drop = set()
        if "M" not in PHASES:
            drop |= {"cT", "w_mod", "b_modT", "b_mod", "gT", "g_post"}
        if "B" not in PHASES and "D" not in PHASES:
            drop |= {"ffn_w1", "ffn_w3", "ffn_w2"}
        if "C" not in PHASES:
            drop |= {"sc_w_in", "sc_w_out", "cwT", "cvalid"}
        in_maps = [{k: v for k, v in m.items() if k not in drop} for m in in_maps]
    res = run_bass_kernel_spmd(nc, in_maps, core_ids=list(range(NCORES)))
    outs = [np.asarray(r["out"]) for r in res.results]
    full = np.stack([np.concatenate(outs[b * 4:(b + 1) * 4], axis=0) for b in range(2)], axis=0)
    if DEBUG:
        kernel.debug = res.results
    return full.astype(np.float32)
```
